# Optimizing a Trainium2 kernel written in Bass

```python
import math
import jax
import jax.numpy as jnp
from jax import lax
import numpy as np

D_MODEL = 1024
BATCH = 16
SEQ = 2048
DEPTH = 4

GRID_W = 64
CTX_LEN = 256
N_MIXERS = 3
N_A = (DEPTH + 2) // 3
N_B = (DEPTH + 1) // 3
N_C = DEPTH // 3
SC_WIDTH = 3
DA_HEADS = 8
DA_HEAD_DIM = D_MODEL // (2 * DA_HEADS)
DA_V_DIM = 2 * DA_HEAD_DIM
ROPE_FREQS = DA_HEAD_DIM // 4
ROPE_BASE = 10000.0
Q_BLOCK = 128
CF_WIDTH = 31
PEER_HEADS = 8
PEER_KEYS = 128
PEER_EXPERTS = PEER_KEYS * PEER_KEYS
PEER_QDIM = 256
PEER_HALF = PEER_QDIM // 2
PEER_TOPK = 16
PEER_CHUNK = 128
RMS_EPS = 1e-6
LN_EPS = 1e-5

kernel_name = 'hybrid_dit_shortconv_diffattn_conformer_peer'


def rms_norm(x, g, eps=RMS_EPS):
    xf = x.astype(jnp.float32)
    y = xf * lax.rsqrt(jnp.mean(xf * xf, axis=-1, keepdims=True) + eps)
    return (y * g.astype(jnp.float32)).astype(x.dtype)


def layer_norm(x, g, b, eps=LN_EPS):
    xf = x.astype(jnp.float32)
    xc = xf - jnp.mean(xf, axis=-1, keepdims=True)
    y = xc * lax.rsqrt(jnp.mean(xc * xc, axis=-1, keepdims=True) + eps)
    return (y * g.astype(jnp.float32) + b.astype(jnp.float32)).astype(x.dtype)


def modulate(h, shift, scale):
    return h * (1.0 + scale) + shift


def depthwise_conv(x, w):
    k = w.shape[0]
    return lax.conv_general_dilated(
        x, w.astype(x.dtype)[:, None, :], window_strides=(1,),
        padding=[(k // 2, k // 2)], dimension_numbers=('NWC', 'WIO', 'NWC'),
        feature_group_count=x.shape[-1])


def short_conv_mixer(h, w_in, conv_w, w_out):
    b_gate, c_gate, u = jnp.split(h @ w_in, 3, axis=-1)
    return (b_gate * depthwise_conv(c_gate * u, conv_w)) @ w_out


def conformer_conv(h, w_pw1, b_pw1, dw_w, dw_b, ln_g, ln_b, w_pw2, b_pw2):
    a, g = jnp.split(h @ w_pw1 + b_pw1, 2, axis=-1)
    u = a * jax.nn.sigmoid(g)
    u = depthwise_conv(u, dw_w) + dw_b
    u = jax.nn.silu(layer_norm(u, ln_g, ln_b))
    return u @ w_pw2 + b_pw2


def axial_rope_tables(n_tokens):
    rows = n_tokens // GRID_W
    row = jnp.repeat(jnp.arange(rows, dtype=jnp.float32), GRID_W)
    col = jnp.tile(jnp.arange(GRID_W, dtype=jnp.float32), rows)
    inv_freq = ROPE_BASE ** (-jnp.arange(ROPE_FREQS, dtype=jnp.float32) / ROPE_FREQS)
    ang = jnp.stack([row[:, None] * inv_freq, col[:, None] * inv_freq], axis=1)
    return jnp.cos(ang), jnp.sin(ang)


def apply_axial_rope(t, cos, sin):
    tr = t.reshape(*t.shape[:-1], 2, 2, ROPE_FREQS).astype(jnp.float32)
    a, b = tr[..., 0, :], tr[..., 1, :]
    cs, sn = cos[None, :, None, None], sin[None, :, None, None]
    out = jnp.stack([a * cs - b * sn, a * sn + b * cs], axis=-2)
    return out.reshape(t.shape).astype(t.dtype)


def diff_lambda_init(layer):
    return 0.8 - 0.6 * math.exp(-0.3 * layer)


def diff_softmax_attend(q1, q2, k1, k2, v, lam):
    scale = DA_HEAD_DIM ** -0.5
    s1 = jnp.einsum('bhqd,bhkd->bhqk', q1, k1, preferred_element_type=jnp.float32) * scale
    s2 = jnp.einsum('bhqd,bhkd->bhqk', q2, k2, preferred_element_type=jnp.float32) * scale
    p = jax.nn.softmax(s1, axis=-1) - lam * jax.nn.softmax(s2, axis=-1)
    return jnp.einsum('bhqk,bhkd->bhqd', p.astype(v.dtype), v)


def diff_attention(a_lat, a_ctx, w_qkv, q_norm_g, k_norm_g, lam_q1, lam_k1, lam_q2, lam_k2,
                   subln_g, w_o, lambda_init, with_ctx_out):
    bsz, n_lat, d = a_lat.shape
    f32 = jnp.float32
    lam = (jnp.exp(jnp.sum(lam_q1.astype(f32) * lam_k1.astype(f32)))
           - jnp.exp(jnp.sum(lam_q2.astype(f32) * lam_k2.astype(f32))) + lambda_init)

    def project(h):
        b, n, _ = h.shape
        q, k, v = jnp.split(h @ w_qkv, 3, axis=-1)
        q = rms_norm(q.reshape(b, n, DA_HEADS, 2, DA_HEAD_DIM), q_norm_g)
        k = rms_norm(k.reshape(b, n, DA_HEADS, 2, DA_HEAD_DIM), k_norm_g)
        return q, k, v.reshape(b, n, DA_HEADS, DA_V_DIM).transpose(0, 2, 1, 3)

    def heads(t, comp):
        return t[:, :, :, comp].transpose(0, 2, 1, 3)

    def finish(o):
        o = rms_norm(o, subln_g) * (1.0 - lambda_init)
        return o.reshape(o.shape[0], o.shape[1], d) @ w_o

    q_l, k_l, v_l = project(a_lat)
    cos, sin = axial_rope_tables(n_lat)
    q_l = apply_axial_rope(q_l, cos, sin)
    k_l = apply_axial_rope(k_l, cos, sin)
    q_c, k_c, v_c = project(a_ctx)
    k1 = jnp.concatenate([heads(k_l, 0), heads(k_c, 0)], axis=2)
    k2 = jnp.concatenate([heads(k_l, 1), heads(k_c, 1)], axis=2)
    v = jnp.concatenate([v_l, v_c], axis=2)

    n_blk = n_lat // Q_BLOCK

    def blocks(t):
        return t.reshape(bsz, DA_HEADS, n_blk, Q_BLOCK, DA_HEAD_DIM).transpose(2, 0, 1, 3, 4)

    o = lax.map(lambda qb: diff_softmax_attend(qb[0], qb[1], k1, k2, v, lam),
                (blocks(heads(q_l, 0)), blocks(heads(q_l, 1))))
    o_lat = o.transpose(1, 0, 3, 2, 4).reshape(bsz, n_lat, DA_HEADS, DA_V_DIM)
    y_lat = finish(o_lat)
    if not with_ctx_out:
        return y_lat, None
    o_ctx = diff_softmax_attend(heads(q_c, 0), heads(q_c, 1), heads(k_c, 0), heads(k_c, 1), v_c, lam)
    return y_lat, finish(o_ctx.transpose(0, 2, 1, 3))


def peer(h, w_query, sub_keys, expert_u, expert_v):
    n_tok, d = h.shape
    q = (h @ w_query).reshape(n_tok, PEER_HEADS, 2, PEER_HALF)
    s = jnp.einsum('thpd,hpnd->thpn', q, sub_keys, preferred_element_type=jnp.float32)
    v_half, i_half = lax.top_k(s, PEER_TOPK)
    cand_s = (v_half[:, :, 0, :, None] + v_half[:, :, 1, None, :]).reshape(n_tok, PEER_HEADS, -1)
    cand_i = (i_half[:, :, 0, :, None] * PEER_KEYS + i_half[:, :, 1, None, :]).reshape(n_tok, PEER_HEADS, -1)
    top_s, top_pos = lax.top_k(cand_s, PEER_TOPK)
    idx = jnp.take_along_axis(cand_i, top_pos, axis=-1)
    gates = jax.nn.softmax(top_s, axis=-1)
    n_chunk = n_tok // PEER_CHUNK
    hk = PEER_HEADS * PEER_TOPK

    def experts(args):
        hb, ib, gb = args
        u = jnp.take(expert_u, ib, axis=0)
        z = jnp.einsum('td,tkd->tk', hb, u, preferred_element_type=jnp.float32)
        act = jax.nn.gelu(z, approximate=False) * gb
        return jnp.einsum('tk,tkd->td', act.astype(hb.dtype), jnp.take(expert_v, ib, axis=0))

    out = lax.map(experts, (h.reshape(n_chunk, PEER_CHUNK, d),
                            idx.reshape(n_chunk, PEER_CHUNK, hk),
                            gates.reshape(n_chunk, PEER_CHUNK, hk)))
    return out.reshape(n_tok, d)


def setup_inputs(seed: int = 0) -> dict:
    key = jax.random.key(seed)
    ks = iter(jax.random.split(key, 40))
    D = D_MODEL

    def nrm(shape, std):
        return jax.random.normal(next(ks), shape, jnp.float32) * std

    return {
        'x': nrm((BATCH, SEQ, D), 1.0),
        'c': nrm((BATCH, D), 1.0),
        'ctx': nrm((BATCH, CTX_LEN, D), 1.0),
        'c_ctx': nrm((D,), 1.0),
        'w_mod': nrm((DEPTH, D, 6 * D), 0.5 * D ** -0.5),
        'b_mod': nrm((DEPTH, 6 * D), 0.02),
        'norm1_g': 1.0 + nrm((DEPTH, D), 0.02),
        'norm2_g': 1.0 + nrm((DEPTH, D), 0.02),
        'sc_w_in': nrm((N_A, D, 3 * D), D ** -0.5),
        'sc_conv_w': nrm((N_A, SC_WIDTH, D), SC_WIDTH ** -0.5),
        'sc_w_out': nrm((N_A, D, D), D ** -0.5),
        'da_w_qkv': nrm((N_B, D, 3 * D), D ** -0.5),
        'da_q_norm_g': 1.0 + nrm((N_B, DA_HEAD_DIM), 0.02),
        'da_k_norm_g': 1.0 + nrm((N_B, DA_HEAD_DIM), 0.02),
        'da_lam_q1': nrm((N_B, DA_HEAD_DIM), 0.1),
        'da_lam_k1': nrm((N_B, DA_HEAD_DIM), 0.1),
        'da_lam_q2': nrm((N_B, DA_HEAD_DIM), 0.1),
        'da_lam_k2': nrm((N_B, DA_HEAD_DIM), 0.1),
        'da_subln_g': 1.0 + nrm((N_B, DA_V_DIM), 0.02),
        'da_w_o': nrm((N_B, D, D), D ** -0.5),
        'cf_w_pw1': nrm((N_C, D, 2 * D), D ** -0.5),
        'cf_b_pw1': nrm((N_C, 2 * D), 0.02),
        'cf_dw_w': nrm((N_C, CF_WIDTH, D), CF_WIDTH ** -0.5),
        'cf_dw_b': nrm((N_C, D), 0.02),
        'cf_ln_g': 1.0 + nrm((N_C, D), 0.02),
        'cf_ln_b': nrm((N_C, D), 0.02),
        'cf_w_pw2': nrm((N_C, D, D), D ** -0.5),
        'cf_b_pw2': nrm((N_C, D), 0.02),
        'peer_w_query': nrm((DEPTH, D, PEER_HEADS * PEER_QDIM), D ** -0.5),
        'peer_sub_keys': nrm((DEPTH, PEER_HEADS, 2, PEER_KEYS, PEER_HALF), PEER_HALF ** -0.5),
        'peer_u': nrm((DEPTH, PEER_EXPERTS, D), D ** -0.5),
        'peer_v': nrm((DEPTH, PEER_EXPERTS, D), PEER_HEADS ** -0.5),
    }


def reference(x, c, ctx, c_ctx, w_mod, b_mod, norm1_g, norm2_g,
              sc_w_in, sc_conv_w, sc_w_out,
              da_w_qkv, da_q_norm_g, da_k_norm_g, da_lam_q1, da_lam_k1,
              da_lam_q2, da_lam_k2, da_subln_g, da_w_o,
              cf_w_pw1, cf_b_pw1, cf_dw_w, cf_dw_b, cf_ln_g, cf_ln_b, cf_w_pw2, cf_b_pw2,
              peer_w_query, peer_sub_keys, peer_u, peer_v):
    bsz, n_lat, d = x.shape
    n_ctx = ctx.shape[1]
    ctx_read = [i for i in range(DEPTH) if i % N_MIXERS == 1]
    last_ctx_read = ctx_read[-1] if ctx_read else -1
    s_lat = jax.nn.silu(c)
    s_ctx = jax.nn.silu(c_ctx)[None]
    h_lat, h_ctx = x, ctx
    for i in range(DEPTH):
        kind, j = i % N_MIXERS, i // N_MIXERS
        use_ctx = i <= last_ctx_read
        adv_ctx = i < last_ctx_read
        m_lat = jnp.split((s_lat @ w_mod[i] + b_mod[i])[:, None, :], 6, axis=-1)
        a_lat = modulate(rms_norm(h_lat, norm1_g[i]), m_lat[0], m_lat[1])
        if use_ctx:
            m_ctx = jnp.split((s_ctx @ w_mod[i] + b_mod[i])[:, None, :], 6, axis=-1)
            a_ctx = modulate(rms_norm(h_ctx, norm1_g[i]), m_ctx[0], m_ctx[1])
        if kind == 0:
            y_lat = short_conv_mixer(a_lat, sc_w_in[j], sc_conv_w[j], sc_w_out[j])
            if adv_ctx:
                y_ctx = short_conv_mixer(a_ctx, sc_w_in[j], sc_conv_w[j], sc_w_out[j])
        elif kind == 1:
            y_lat, y_ctx = diff_attention(a_lat, a_ctx, da_w_qkv[j], da_q_norm_g[j], da_k_norm_g[j],
                                          da_lam_q1[j], da_lam_k1[j], da_lam_q2[j], da_lam_k2[j],
                                          da_subln_g[j], da_w_o[j], diff_lambda_init(i), adv_ctx)
        else:
            y_lat = conformer_conv(a_lat, cf_w_pw1[j], cf_b_pw1[j], cf_dw_w[j], cf_dw_b[j],
                                   cf_ln_g[j], cf_ln_b[j], cf_w_pw2[j], cf_b_pw2[j])
            if adv_ctx:
                y_ctx = conformer_conv(a_ctx, cf_w_pw1[j], cf_b_pw1[j], cf_dw_w[j], cf_dw_b[j],
                                       cf_ln_g[j], cf_ln_b[j], cf_w_pw2[j], cf_b_pw2[j])
        h_lat = h_lat + m_lat[2] * y_lat
        f_lat = modulate(rms_norm(h_lat, norm2_g[i]), m_lat[3], m_lat[4]).reshape(-1, d)
        if adv_ctx:
            h_ctx = h_ctx + m_ctx[2] * y_ctx
            f_ctx = modulate(rms_norm(h_ctx, norm2_g[i]), m_ctx[3], m_ctx[4]).reshape(-1, d)
            f_all = jnp.concatenate([f_lat, f_ctx], axis=0)
        else:
            f_all = f_lat
        y_all = peer(f_all, peer_w_query[i], peer_sub_keys[i], peer_u[i], peer_v[i])
        h_lat = h_lat + m_lat[5] * y_all[:bsz * n_lat].reshape(bsz, n_lat, d)
        if adv_ctx:
            h_ctx = h_ctx + m_ctx[5] * y_all[bsz * n_lat:].reshape(bsz, n_ctx, d)
    return h_lat
```

```python
import contextlib
import math
import numpy as np
import concourse.bass as bass
import concourse.mybir as mybir
from concourse.bass_utils import run_bass_kernel_spmd

F32 = mybir.dt.float32
BF16 = mybir.dt.bfloat16
ALU = mybir.AluOpType
AF = mybir.ActivationFunctionType
AX = mybir.AxisListType

D = 1024
KC = 8
NLAT = 2048
NCTX = 256
NB = 2
DEPTH = 4
RMS_EPS = 1e-6
LN_EPS = 1e-5
PEER_E = 16384


class Buf:
    def __init__(self, name, a=None):
        self.name = name
        self.a = a
        self.w = {}
        self.r = {}
        self.sem = None
        self.cnt = 0


class KB:
    def __init__(self):
        self.nc = bass.Bass("TRN2", target_bir_lowering=False)
        self.es = contextlib.ExitStack()
        nc = self.nc
        self.sems = []
        self.eng = {}
        for nm, e in (("pe", nc.tensor), ("act", nc.scalar), ("dve", nc.vector),
                      ("pool", nc.gpsimd), ("sp", nc.sync)):
            self.eng[nm] = dict(e=e, sem=self.newsem("s_" + nm), cnt=0, waited={})
        self.nuniq = 0
        self.dcount = {}
        self.freed = []
        self.stack = [self.es]
        self.scoped = []

    def newsem(self, name):
        h = self.es.enter_context(self.nc.semaphore(name))
        self.sems.append(h)
        return len(self.sems) - 1

    def barrier(self):
        deps = {}
        for E in self.eng.values():
            deps[E["sem"]] = E["cnt"]
        deps.update(self.dcount)
        for en in self.eng:
            self._wait(en, {k: v for k, v in deps.items() if v > 0})

    @contextlib.contextmanager
    def scope(self):
        st = contextlib.ExitStack()
        self.stack.append(st)
        self.scoped.append([])
        try:
            yield
        finally:
            self.barrier()
            for b in self.scoped.pop():
                if b.sem is not None:
                    self.freed.append(b.sem)
                    b.sem = None
            self.stack.pop()
            st.close()

    def sb(self, name, shape, dt):
        self.nuniq += 1
        t = self.stack[-1].enter_context(self.nc.sbuf_tensor("%s_%d" % (name, self.nuniq), list(shape), dt))
        b = Buf(name, t[:])
        if self.scoped:
            self.scoped[-1].append(b)
        return b

    def ps(self, name, shape, dt=F32):
        t = self.es.enter_context(self.nc.psum_tensor(name, list(shape), dt))
        return Buf(name, t[:])

    def dram(self, name, shape, dt, kind="Internal"):
        t = self.nc.dram_tensor(name, list(shape), dt, kind=kind)
        return Buf(name, t.ap())

    def _deps(self, r, w):
        deps = {}
        for b in r:
            for k, v in b.w.items():
                if deps.get(k, 0) < v:
                    deps[k] = v
        for b in w:
            for dd in (b.w, b.r):
                for k, v in dd.items():
                    if deps.get(k, 0) < v:
                        deps[k] = v
        return deps

    def _wait(self, en, deps):
        E = self.eng[en]
        for k, v in deps.items():
            if E["waited"].get(k, 0) >= v:
                continue
            E["e"].wait_ge(self.sems[k], v)
            E["waited"][k] = v

    def _done(self, tok, r, w):
        k, v = tok
        for b in r:
            if b.r.get(k, 0) < v:
                b.r[k] = v
        for b in w:
            if b.w.get(k, 0) < v:
                b.w[k] = v
            b.r = {}

    def op(self, en, fn, r=(), w=()):
        deps = self._deps(r, w)
        if en == "pe":
            deps.pop(self.eng["pe"]["sem"], None)
        self._wait(en, deps)
        E = self.eng[en]
        ins = fn(E["e"])
        E["cnt"] += 1
        ins.then_inc(self.sems[E["sem"]], 1)
        self._done((E["sem"], E["cnt"]), r, w)

    def dma(self, q, out, in_, r=(), w=()):
        self._wait(q, self._deps(r, w))
        E = self.eng[q]
        ins = E["e"].dma_start(out=out, in_=in_)
        d = w[0]
        if d.sem is None:
            if self.freed:
                d.sem = self.freed.pop()
            else:
                d.sem = self.newsem("d%d" % len(self.sems))
        c = self.dcount.get(d.sem, 0) + 16
        self.dcount[d.sem] = c
        ins.then_inc(self.sems[d.sem], 16)
        self._done((d.sem, c), r, w)

    def mm(self, out, pairs, r=(), w=(), start=True, stop=True):
        def fn(pe):
            n = len(pairs)
            ins = None
            for i, (l, rh) in enumerate(pairs):
                ins = pe.matmul(out, lhsT=l, rhs=rh, start=(start and i == 0), stop=(stop and i == n - 1))
            return ins
        self.op("pe", fn, r, w)

    def finish(self, outs):
        deps = {}
        for b in outs:
            for k, v in b.w.items():
                deps[k] = max(deps.get(k, 0), v)
        self._wait("sp", deps)


def packv(v):
    v = np.asarray(v, np.float32).reshape(-1, 128)
    return np.ascontiguousarray(v.T)


class VecPack:
    def __init__(self):
        self.off = {}
        self.n = 0
        self.cols = []

    def add(self, key, arr128xn):
        a = np.asarray(arr128xn, np.float32)
        assert a.shape[0] == 128
        self.off[key] = (self.n, a.shape[1])
        self.n += a.shape[1]
        self.cols.append(a)

    def array(self):
        return np.ascontiguousarray(np.concatenate(self.cols, axis=1))


class Prog:
    def __init__(self, segs_lat, segs_ctx, voff, nv, layers=range(DEPTH)):
        self.kb = KB()
        kb = self.kb
        self.segs_lat = segs_lat
        self.segs_ctx = segs_ctx
        self.ntok = (segs_ctx[-1][1] if segs_ctx else segs_lat[-1][1])
        self.voff = voff
        self.nv = nv
        self.din = {}
        self.vec = kb.sb("vec", [128, nv], F32)
        self.ones_bf = kb.sb("ones_bf", [128, 128], BF16)
        self.ones_f = kb.sb("ones_f", [128, 128], F32)
        self.P = [kb.ps("P%d" % i, [128, 512], F32) for i in range(8)]

    def inp(self, name, shape, dt=F32):
        b = self.kb.dram(name, shape, dt, kind="ExternalInput")
        self.din[name] = b
        return b

    def v(self, key, lo=0, n=None):
        o, w = self.voff[key]
        if n is None:
            n = w - lo
        return self.vec.a[:, o + lo:o + lo + n]

    def prologue(self):
        kb = self.kb
        vecd = self.inp("vecs", [128, self.nv])
        kb.dma("sp", self.vec.a, vecd.a, r=[vecd], w=[self.vec])
        kb.op("dve", lambda e: e.memset(self.ones_bf.a, 1.0), w=[self.ones_bf])
        kb.op("dve", lambda e: e.memset(self.ones_f.a, 1.0), w=[self.ones_f])
        self.mod = kb.sb("mod", [128, DEPTH, 3, 48], F32)
        self.gs = kb.sb("gs", [128, DEPTH, 3, 2, 8], F32)

    def stage_mod(self, layers):
        kb = self.kb
        with kb.scope():
            sc = kb.sb("sc", [128, 8, 4], F32)
            kb.op("act", lambda e: e.activation(sc.a[:, :, 0:3], self.v("cT").rearrange("p (k r) -> p k r", r=3),
                                                AF.Silu), r=[self.vec], w=[sc])
            wts = [kb.sb("wm%d" % i, [128, 8, 512], F32) for i in range(2)]
            it = 0
            for l in layers:
                wd = self.inp("w_mod%d" % l, [128, 8, 6144])
                pm = self.P[l % 2]
                for cb in range(12):
                    wt = wts[it % 2]
                    it += 1
                    kb.dma("sp", wt.a, wd.a[:, :, cb * 512:(cb + 1) * 512], r=[wd], w=[wt])
                    for jj in range(4):
                        j = cb * 4 + jj
                        kb.mm(pm.a[:, j * 4:j * 4 + 3],
                              [(wt.a[:, k, jj * 128:(jj + 1) * 128], sc.a[:, k, 0:3]) for k in range(8)],
                              r=[wt, sc], w=[pm])
                for r in range(3):
                    kb.op("dve", lambda e, r=r, l=l, pm=pm: e.tensor_tensor(
                        self.mod.a[:, l, r, :], pm.a[:, 0:192].rearrange("p (j r) -> p j r", r=4)[:, :, r],
                        self.v("bmod%d" % l), op=ALU.add), r=[pm, self.vec], w=[self.mod])
                    for h, (c0, gk) in enumerate(((8, "n1g%d" % l), (32, "n2g%d" % l))):
                        kb.op("dve", lambda e, r=r, l=l, h=h, c0=c0, gk=gk: e.scalar_tensor_tensor(
                            self.gs.a[:, l, r, h, :], self.mod.a[:, l, r, c0:c0 + 8], 1.0, self.v(gk),
                            op0=ALU.add, op1=ALU.mult), r=[self.mod, self.vec], w=[self.gs])

    def rstd_of(self, x3, n, sq, pss, rstd, rbufs, eps=RMS_EPS, nfeat=D, ones=None, nk=8):
        kb = self.kb
        ones = ones if ones is not None else self.ones_bf
        kb.op("act", lambda e: e.activation(sq.a[:, 0:nk, 0:n], x3, AF.Square), r=rbufs, w=[sq])
        kb.mm(pss.a[:, 0:n], [(ones.a, sq.a[:, k, 0:n]) for k in range(nk)], r=[ones, sq], w=[pss])
        kb.op("act", lambda e: e.activation(rstd.a[:, 0:n], pss.a[:, 0:n], AF.Sqrt, bias=eps, scale=1.0 / nfeat),
              r=[pss], w=[rstd])
        kb.op("dve", lambda e: e.reciprocal(rstd.a[:, 0:n], rstd.a[:, 0:n]), r=[rstd], w=[rstd])

    def modulate(self, x3, n, rstd, tmp, out, gsc, shift, rbufs):
        kb = self.kb
        kb.op("dve", lambda e: e.tensor_tensor(tmp.a[:, :, 0:n], x3,
                                               rstd.a[:, 0:n].unsqueeze(1).to_broadcast([128, 8, n]), op=ALU.mult),
              r=rbufs + [rstd], w=[tmp])
        for k in range(8):
            if k % 2 == 0:
                kb.op("act", lambda e, k=k: e.activation(out.a[:, k, 0:n], tmp.a[:, k, 0:n], AF.Identity,
                                                         bias=shift[:, k:k + 1], scale=gsc[:, k:k + 1]),
                      r=[tmp, self.mod, self.gs], w=[out])
            else:
                kb.op("pool", lambda e, k=k: e.tensor_scalar(out.a[:, k, 0:n], tmp.a[:, k, 0:n], gsc[:, k:k + 1],
                                                             shift[:, k:k + 1], op0=ALU.mult, op1=ALU.add),
                      r=[tmp, self.mod, self.gs], w=[out])

    def load_w_bf(self, dst, src, nk, ncols, step=1024):
        kb = self.kb
        for k in range(nk):
            for c0 in range(0, ncols, step):
                c1 = min(ncols, c0 + step)
                kb.dma("pool", dst.a[:, k, c0:c1], src.a[:, k, c0:c1], r=[src], w=[dst])

    def tiles(self, segs, T):
        for (s0, s1, row) in segs:
            for t0 in range(s0, s1, T):
                yield s0, s1, row, t0

    def post_mixer(self, l, row, hn, T, dst, fdst, t0, bufs):
        kb = self.kb
        sq, tmp, rstd, fbf = bufs
        self.rstd_of(hn.a[:, :, 0:T], T, sq, self.P[1], rstd, [hn])
        self.modulate(hn.a[:, :, 0:T], T, rstd, tmp, fbf, self.gs.a[:, l, row, 1, :], self.mod.a[:, l, row, 24:32], [hn])
        kb.dma("sp", dst.a[:, :, t0:t0 + T], hn.a[:, :, 0:T], r=[hn], w=[dst])
        kb.dma("sp", fdst.a[:, :, t0:t0 + T], fbf.a[:, :, 0:T], r=[fbf], w=[fdst])

    def stage_sconv(self, l, src, dst, fdst, segs):
        kb = self.kb
        T, H = 256, 1
        W = T + 2 * H
        j = l // 3
        with kb.scope():
            win = kb.sb("win", [128, 8, 3072], BF16)
            wout = kb.sb("wout", [128, 8, 1024], BF16)
            self.load_w_bf(win, self.inp("sc_w_in%d" % j, [128, 8, 3072]), 8, 3072)
            self.load_w_bf(wout, self.inp("sc_w_out%d" % j, [128, 8, 1024]), 8, 1024)
            xts = [kb.sb("xt%d" % i, [128, 8, W], F32) for i in range(2)]
            abfs = [kb.sb("abf%d" % i, [128, 8, W], BF16) for i in range(2)]
            gTs = [kb.sb("gT%d" % i, [128, 8, T], BF16) for i in range(2)]
            hns = [kb.sb("hn%d" % i, [128, 8, T], F32) for i in range(2)]
            fbfs = [kb.sb("fbf%d" % i, [128, 8, T], BF16) for i in range(2)]
            sq = kb.sb("sq", [128, 8, W], BF16)
            tmp = kb.sb("tmp", [128, 8, W], F32)
            rstd = kb.sb("rstd", [128, W], F32)
            rstd2 = kb.sb("rstd2", [128, W], F32)
            csbs = [kb.sb("csb%d" % i, [128, W], F32) for i in range(2)]
            cus = [kb.sb("cu%d" % i, [128, W], F32) for i in range(2)]
            accs = [kb.sb("acc%d" % i, [128, T], F32) for i in range(2)]
            scw = self.v("scw%d" % j)
            P = self.P
            for it, (s0, s1, row, t0) in enumerate(self.tiles(segs, T)):
                lo, hi = max(t0 - H, s0), min(t0 + T + H, s1)
                n = hi - lo
                off = lo - (t0 - H)
                xt, abf, gT, hn, fbf = xts[it % 2], abfs[it % 2], gTs[it % 2], hns[it % 2], fbfs[it % 2]
                kb.dma("sp", xt.a[:, :, off:off + n], src.a[:, :, lo:hi], r=[src], w=[xt])
                x3 = xt.a[:, :, off:off + n]
                self.rstd_of(x3, n, sq, P[0], rstd, [xt])
                self.modulate(x3, n, rstd, tmp, abf, self.gs.a[:, l, row, 0, :], self.mod.a[:, l, row, 0:8], [xt])
                for ch in range(8):
                    pc, pu, pb = P[2 + ch % 2], P[4 + ch % 2], P[6 + ch % 2]
                    csb, cu, acc = csbs[ch % 2], cus[ch % 2], accs[ch % 2]
                    for (pp, cc) in ((pc, 8 + ch), (pu, 16 + ch), (pb, ch)):
                        kb.mm(pp.a[:, 0:n], [(win.a[:, k, cc * 128:(cc + 1) * 128], abf.a[:, k, 0:n]) for k in range(8)],
                              r=[win, abf], w=[pp])
                    kb.op("act", lambda e, csb=csb, pc=pc: e.copy(csb.a[:, 0:n], pc.a[:, 0:n]), r=[pc], w=[csb])
                    if off > 0:
                        kb.op("pool", lambda e, cu=cu: e.memset(cu.a[:, 0:off], 0.0), w=[cu])
                    if off + n < W:
                        kb.op("pool", lambda e, cu=cu: e.memset(cu.a[:, off + n:W], 0.0), w=[cu])
                    kb.op("dve", lambda e, cu=cu, csb=csb, pu=pu: e.tensor_tensor(
                        cu.a[:, off:off + n], csb.a[:, 0:n], pu.a[:, 0:n], op=ALU.mult), r=[csb, pu], w=[cu])
                    kb.op("pool", lambda e, cu=cu, acc=acc: e.tensor_scalar(
                        acc.a, cu.a[:, 0:T], scw[:, ch:ch + 1], None, op0=ALU.mult), r=[cu, self.vec], w=[acc])
                    for tap in (1, 2):
                        kb.op("dve", lambda e, cu=cu, acc=acc, tap=tap: e.scalar_tensor_tensor(
                            acc.a, cu.a[:, tap:tap + T], scw[:, tap * 8 + ch:tap * 8 + ch + 1], acc.a,
                            op0=ALU.mult, op1=ALU.add), r=[cu, self.vec, acc], w=[acc])
                    c0 = H - off
                    kb.op("dve", lambda e, acc=acc, pb=pb, gT=gT, ch=ch, c0=c0: e.tensor_tensor(
                        gT.a[:, ch, :], acc.a, pb.a[:, c0:c0 + T], op=ALU.mult), r=[acc, pb], w=[gT])
                for dm in range(8):
                    py = P[2 + dm % 6]
                    kb.mm(py.a[:, 0:T], [(wout.a[:, k, dm * 128:(dm + 1) * 128], gT.a[:, k, :]) for k in range(8)],
                          r=[wout, gT], w=[py])
                    kb.op("dve", lambda e, py=py, dm=dm, hn=hn, xt=xt: e.scalar_tensor_tensor(
                        hn.a[:, dm, :], py.a[:, 0:T], self.mod.a[:, l, row, 16 + dm:17 + dm], xt.a[:, dm, H:H + T],
                        op0=ALU.mult, op1=ALU.add), r=[py, self.mod, xt], w=[hn])
                self.post_mixer(l, row, hn, T, dst, fdst, t0, (sq, tmp, rstd2, fbf))


    def stage_conformer(self, l, src, dst, fdst, segs):
        kb = self.kb
        T, H = 256, 15
        W = T + 2 * H
        P = self.P
        with kb.scope():
            w1 = kb.sb("w1", [128, 8, 2048], BF16)
            w2 = kb.sb("w2", [128, 8, 1024], BF16)
            self.load_w_bf(w1, self.inp("cf_w_pw1", [128, 8, 2048]), 8, 2048)
            self.load_w_bf(w2, self.inp("cf_w_pw2", [128, 8, 1024]), 8, 1024)
            xts = [kb.sb("xt%d" % i, [128, 8, W], F32) for i in range(2)]
            abf = kb.sb("abf", [128, 8, W], BF16)
            sq = kb.sb("sq", [128, 8, W], BF16)
            tmp = kb.sb("tmp", [128, 8, W], F32)
            rstd = kb.sb("rstd", [128, W], F32)
            rstd2 = kb.sb("rstd2", [128, W], F32)
            sgs = [kb.sb("sg%d" % i, [128, W], F32) for i in range(2)]
            us = [kb.sb("u%d" % i, [128, W], F32) for i in range(2)]
            uc = kb.sb("uc", [128, 8, T], F32)
            ucq = kb.sb("ucq", [128, 8, T], F32)
            mean = kb.sb("mean", [128, T], F32)
            var = kb.sb("var", [128, T], F32)
            sT = kb.sb("sT", [128, 8, T], BF16)
            hns = [kb.sb("hn%d" % i, [128, 8, T], F32) for i in range(2)]
            fbfs = [kb.sb("fbf%d" % i, [128, 8, T], BF16) for i in range(2)]
            yt = kb.sb("yt", [128, T], F32)
            b1, dw, dwb = self.v("cfb1"), self.v("cfdw"), self.v("cfdwb")
            lng, lnb, b2 = self.v("cflng"), self.v("cflnb"), self.v("cfb2")
            for it, (s0, s1, row, t0) in enumerate(self.tiles(segs, T)):
                lo, hi = max(t0 - H, s0), min(t0 + T + H, s1)
                n = hi - lo
                off = lo - (t0 - H)
                xt, hn, fbf = xts[it % 2], hns[it % 2], fbfs[it % 2]
                kb.dma("sp", xt.a[:, :, off:off + n], src.a[:, :, lo:hi], r=[src], w=[xt])
                x3 = xt.a[:, :, off:off + n]
                self.rstd_of(x3, n, sq, P[0], rstd, [xt])
                self.modulate(x3, n, rstd, tmp, abf, self.gs.a[:, l, row, 0, :], self.mod.a[:, l, row, 0:8], [xt])
                for ch in range(8):
                    pa, pg = P[2 + ch % 2], P[4 + ch % 2]
                    sg, u = sgs[ch % 2], us[ch % 2]
                    for (pp, cc) in ((pa, ch), (pg, 8 + ch)):
                        kb.mm(pp.a[:, 0:n], [(w1.a[:, k, cc * 128:(cc + 1) * 128], abf.a[:, k, 0:n]) for k in range(8)],
                              r=[w1, abf], w=[pp])
                    kb.op("act", lambda e, sg=sg, pg=pg, ch=ch: e.activation(sg.a[:, 0:n], pg.a[:, 0:n], AF.Sigmoid,
                                                                             bias=b1[:, 8 + ch:9 + ch]), r=[pg, self.vec], w=[sg])
                    if off > 0:
                        kb.op("pool", lambda e, u=u: e.memset(u.a[:, 0:off], 0.0), w=[u])
                    if off + n < W:
                        kb.op("pool", lambda e, u=u: e.memset(u.a[:, off + n:W], 0.0), w=[u])
                    kb.op("dve", lambda e, u=u, pa=pa, sg=sg, ch=ch: e.scalar_tensor_tensor(
                        u.a[:, off:off + n], pa.a[:, 0:n], b1[:, ch:ch + 1], sg.a[:, 0:n], op0=ALU.add, op1=ALU.mult),
                        r=[pa, sg, self.vec], w=[u])
                    kb.op("dve", lambda e, u=u, ch=ch: e.tensor_scalar(
                        uc.a[:, ch, :], u.a[:, 0:T], dw[:, ch:ch + 1], dwb[:, ch:ch + 1], op0=ALU.mult, op1=ALU.add),
                        r=[u, self.vec], w=[uc])
                    for tap in range(1, 31):
                        kb.op("dve", lambda e, u=u, ch=ch, tap=tap: e.scalar_tensor_tensor(
                            uc.a[:, ch, :], u.a[:, tap:tap + T], dw[:, tap * 8 + ch:tap * 8 + ch + 1], uc.a[:, ch, :],
                            op0=ALU.mult, op1=ALU.add), r=[u, self.vec, uc], w=[uc])
                kb.op("act", lambda e: e.activation(ucq.a, uc.a, AF.Square), r=[uc], w=[ucq])
                kb.mm(P[6].a[:, 0:T], [(self.ones_f.a, uc.a[:, k, :]) for k in range(8)], r=[self.ones_f, uc], w=[P[6]])
                kb.mm(P[7].a[:, 0:T], [(self.ones_f.a, ucq.a[:, k, :]) for k in range(8)], r=[self.ones_f, ucq], w=[P[7]])
                kb.op("act", lambda e: e.activation(mean.a, P[6].a[:, 0:T], AF.Identity, scale=1.0 / D), r=[P[6]], w=[mean])
                kb.op("dve", lambda e: e.tensor_tensor(var.a, mean.a, mean.a, op=ALU.mult), r=[mean], w=[var])
                kb.op("dve", lambda e: e.scalar_tensor_tensor(var.a, P[7].a[:, 0:T], 1.0 / D, var.a,
                                                              op0=ALU.mult, op1=ALU.subtract), r=[P[7], var], w=[var])
                kb.op("act", lambda e: e.activation(var.a, var.a, AF.Sqrt, bias=LN_EPS, scale=1.0), r=[var], w=[var])
                kb.op("dve", lambda e: e.reciprocal(var.a, var.a), r=[var], w=[var])
                kb.op("dve", lambda e: e.tensor_tensor(uc.a, uc.a, mean.a.unsqueeze(1).to_broadcast([128, 8, T]),
                                                       op=ALU.subtract), r=[uc, mean], w=[uc])
                kb.op("dve", lambda e: e.tensor_tensor(uc.a, uc.a, var.a.unsqueeze(1).to_broadcast([128, 8, T]),
                                                       op=ALU.mult), r=[uc, var], w=[uc])
                for k in range(8):
                    kb.op("act", lambda e, k=k: e.activation(sT.a[:, k, :], uc.a[:, k, :], AF.Silu,
                                                             bias=lnb[:, k:k + 1], scale=lng[:, k:k + 1]),
                          r=[uc, self.vec], w=[sT])
                for dm in range(8):
                    py = P[2 + dm % 4]
                    kb.mm(py.a[:, 0:T], [(w2.a[:, k, dm * 128:(dm + 1) * 128], sT.a[:, k, :]) for k in range(8)],
                          r=[w2, sT], w=[py])
                    kb.op("dve", lambda e, py=py, dm=dm: e.tensor_scalar(
                        yt.a, py.a[:, 0:T], b2[:, dm:dm + 1], self.mod.a[:, l, row, 16 + dm:17 + dm],
                        op0=ALU.add, op1=ALU.mult), r=[py, self.vec, self.mod], w=[yt])
                    kb.op("dve", lambda e, dm=dm, hn=hn, xt=xt: e.tensor_tensor(
                        hn.a[:, dm, :], yt.a, xt.a[:, dm, H:H + T], op=ALU.add), r=[yt, xt], w=[hn])
                self.post_mixer(l, row, hn, T, dst, fdst, t0, (sq, tmp, rstd2, fbf))

    def stage_attn(self, l, src, dst, fdst, segs_lat, segs_ctx, lambda_init):
        kb = self.kb
        P = self.P
        T = 512
        nlat = segs_lat[-1][1]
        ntok = segs_ctx[-1][1]
        qTd = kb.dram("a_qT", [128, 8, nlat], BF16)
        kTd = kb.dram("a_kT", [128, 8, ntok], BF16)
        vTd = kb.dram("a_v", [ntok // 128, 128, 1024], BF16)
        oTd = kb.dram("a_oT", [128, 8, nlat], BF16)
        with kb.scope():
            wqkv = kb.sb("wqkv", [128, 8, 3072], BF16)
            self.load_w_bf(wqkv, self.inp("da_w_qkv", [128, 8, 3072]), 8, 3072)
            cosT = kb.sb("cosT", [128, NLAT], F32)
            sinT = kb.sb("sinT", [128, NLAT], F32)
            blk = kb.sb("blk", [128, 128], BF16)
            prot = kb.sb("prot", [128, 128], F32)
            kb.dma("sp", cosT.a, self.inp("ropecos", [128, NLAT]).a, r=[self.din["ropecos"]], w=[cosT])
            kb.dma("sp", sinT.a, self.inp("ropesin", [128, NLAT]).a, r=[self.din["ropesin"]], w=[sinT])
            kb.dma("pool", blk.a, self.inp("blkones", [128, 128]).a, r=[self.din["blkones"]], w=[blk])
            kb.dma("sp", prot.a, self.inp("protm", [128, 128]).a, r=[self.din["protm"]], w=[prot])
            xts = [kb.sb("xt%d" % i, [128, 8, T], F32) for i in range(2)]
            abf = kb.sb("abf", [128, 8, T], BF16)
            sq = kb.sb("sq", [128, 8, T], BF16)
            tmp = kb.sb("tmp", [128, 8, T], F32)
            rstd = kb.sb("rstd", [128, T], F32)
            sqh = kb.sb("sqh", [128, T], BF16)
            rs = kb.sb("rs", [128, T], F32)
            qn = kb.sb("qn", [128, T], F32)
            t1 = kb.sb("t1", [128, T], F32)
            t2 = kb.sb("t2", [128, T], F32)
            qks = [kb.sb("qk%d" % i, [128, 8, T], BF16) for i in range(2)]
            vsb = kb.sb("vsb", [128, 1024], BF16)
            for it, (s0, s1, row, t0) in enumerate(self.tiles(list(segs_lat) + list(segs_ctx), T)):
                n = min(T, s1 - t0)
                is_lat = t0 < nlat
                pos0 = t0 - s0
                xt = xts[it % 2]
                kb.dma("sp", xt.a[:, :, 0:n], src.a[:, :, t0:t0 + n], r=[src], w=[xt])
                x3 = xt.a[:, :, 0:n]
                self.rstd_of(x3, n, sq, P[0], rstd, [xt])
                self.modulate(x3, n, rstd, tmp, abf, self.gs.a[:, l, row, 0, :], self.mod.a[:, l, row, 0:8], [xt])
                for qi, (base, gk, dd, stage) in enumerate(((0, "qg", qTd, qks[0]), (8, "kg", kTd, qks[1]))):
                    if qi == 0 and not is_lat:
                        continue
                    for h in range(8):
                        pq, pss, pr = P[2 + h % 2], P[4 + h % 2], P[6 + h % 2]
                        cc = base + h
                        kb.mm(pq.a[:, 0:n], [(wqkv.a[:, k, cc * 128:(cc + 1) * 128], abf.a[:, k, 0:n]) for k in range(8)],
                              r=[wqkv, abf], w=[pq])
                        kb.op("act", lambda e, pq=pq: e.activation(sqh.a[:, 0:n], pq.a[:, 0:n], AF.Square), r=[pq], w=[sqh])
                        kb.mm(pss.a[:, 0:n], [(blk.a, sqh.a[:, 0:n])], r=[blk, sqh], w=[pss])
                        kb.op("act", lambda e, pss=pss: e.activation(rs.a[:, 0:n], pss.a[:, 0:n], AF.Sqrt, bias=RMS_EPS,
                                                                     scale=1.0 / 64), r=[pss], w=[rs])
                        kb.op("dve", lambda e: e.reciprocal(rs.a[:, 0:n], rs.a[:, 0:n]), r=[rs], w=[rs])
                        kb.op("dve", lambda e, pq=pq, gk=gk: e.scalar_tensor_tensor(
                            qn.a[:, 0:n], pq.a[:, 0:n], self.v(gk), rs.a[:, 0:n], op0=ALU.mult, op1=ALU.mult),
                            r=[pq, rs, self.vec], w=[qn])
                        if is_lat:
                            kb.mm(pr.a[:, 0:n], [(prot.a, qn.a[:, 0:n])], r=[prot, qn], w=[pr])
                            kb.op("pool", lambda e: e.tensor_tensor(t1.a[:, 0:n], qn.a[:, 0:n], cosT.a[:, pos0:pos0 + n],
                                                                    op=ALU.mult), r=[qn, cosT], w=[t1])
                            kb.op("dve", lambda e, pr=pr: e.tensor_tensor(t2.a[:, 0:n], pr.a[:, 0:n], sinT.a[:, pos0:pos0 + n],
                                                                          op=ALU.mult), r=[pr, sinT], w=[t2])
                            kb.op("dve", lambda e, stage=stage, h=h: e.tensor_tensor(stage.a[:, h, 0:n], t1.a[:, 0:n], t2.a[:, 0:n],
                                                                                     op=ALU.add), r=[t1, t2], w=[stage])
                        else:
                            kb.op("act", lambda e, stage=stage, h=h: e.copy(stage.a[:, h, 0:n], qn.a[:, 0:n]), r=[qn], w=[stage])
                    kb.dma("sp", dd.a[:, :, t0:t0 + n], stage.a[:, :, 0:n], r=[stage], w=[dd])
                for sub in range(n // 128):
                    for half in range(2):
                        pv = P[2 + half]
                        kb.mm(pv.a, [(abf.a[:, k, sub * 128:(sub + 1) * 128],
                                      wqkv.a[:, k, 2048 + half * 512:2048 + (half + 1) * 512]) for k in range(8)],
                              r=[abf, wqkv], w=[pv])
                        if half == 0:
                            kb.op("act", lambda e, pv=pv: e.copy(vsb.a[:, 0:512], pv.a), r=[pv], w=[vsb])
                        else:
                            kb.op("dve", lambda e, pv=pv: e.tensor_copy(vsb.a[:, 512:1024], pv.a), r=[pv], w=[vsb])
                    kb.dma("sp", vTd.a[(t0 + sub * 128) // 128], vsb.a, r=[vsb], w=[vTd])
        with kb.scope():
            lam = kb.sb("lam", [128, 4], F32)
            lt = kb.sb("lt", [128, 64], F32)
            for ii, (ka, kb_) in enumerate((("lq1", "lk1"), ("lq2", "lk2"))):
                kb.op("dve", lambda e, ka=ka, kb_=kb_: e.tensor_tensor(lt.a, self.v(ka), self.v(kb_), op=ALU.mult),
                      r=[self.vec], w=[lt])
                kb.op("dve", lambda e, ii=ii: e.tensor_reduce(out=lam.a[:, ii:ii + 1], in_=lt.a, axis=AX.X, op=ALU.add),
                      r=[lt], w=[lam])
            kb.op("act", lambda e: e.activation(lam.a[:, 0:2], lam.a[:, 0:2], AF.Exp), r=[lam], w=[lam])
            kb.op("dve", lambda e: e.tensor_tensor(lam.a[:, 2:3], lam.a[:, 1:2], lam.a[:, 0:1], op=ALU.subtract), r=[lam], w=[lam])
            kb.op("dve", lambda e: e.tensor_scalar(lam.a[:, 2:3], lam.a[:, 2:3], -float(lambda_init), None, op0=ALU.add),
                  r=[lam], w=[lam])
            kb.op("dve", lambda e: e.tensor_scalar(lam.a[:, 3:4], self.v("subg"), 1.0 - float(lambda_init), None, op0=ALU.mult),
                  r=[self.vec], w=[lam])
            kts = [kb.sb("kt%d" % i, [128, NLAT + NCTX], BF16) for i in range(2)]
            vts = [kb.sb("vt%d" % i, [128, 18, 128], BF16) for i in range(2)]
            qts = [kb.sb("qt%d" % i, [128, T], BF16) for i in range(2)]
            pTs = [kb.sb("pT%d" % i, [128, T], BF16) for i in range(4)]
            rz = kb.sb("rz", [128, 2, T], F32)
            o1 = kb.sb("o1", [128, T], F32)
            o2 = kb.sb("o2", [128, T], F32)
            osq = kb.sb("osq", [128, T], BF16)
            ors = kb.sb("ors", [128, T], F32)
            ofs = [kb.sb("of%d" % i, [128, T], BF16) for i in range(2)]
            scale = 64 ** -0.5
            nit = 0
            npt = 0
            for bi, ((l0, l1, _), (c0, c1, _)) in enumerate(zip(segs_lat, segs_ctx)):
                nkl = (l1 - l0) // 128
                nkc = (c1 - c0) // 128
                nk = nkl + nkc
                for h in range(8):
                    kt, vt = kts[(bi * 8 + h) % 2], vts[(bi * 8 + h) % 2]
                    kb.dma("sp", kt.a[:, 0:l1 - l0], kTd.a[:, h, l0:l1], r=[kTd], w=[kt])
                    kb.dma("sp", kt.a[:, l1 - l0:l1 - l0 + c1 - c0], kTd.a[:, h, c0:c1], r=[kTd], w=[kt])
                    kb.dma("sp", vt.a[:, 0:nkl, :], vTd.a[l0 // 128:l1 // 128, :, h * 128:(h + 1) * 128].rearrange("c p d -> p c d"),
                           r=[vTd], w=[vt])
                    kb.dma("sp", vt.a[:, nkl:nk, :], vTd.a[c0 // 128:c1 // 128, :, h * 128:(h + 1) * 128].rearrange("c p d -> p c d"),
                           r=[vTd], w=[vt])
                    for q0 in range(l0, l1, T):
                        qt = qts[nit % 2]
                        of = ofs[nit % 2]
                        nit += 1
                        kb.dma("sp", qt.a, qTd.a[:, h, q0:q0 + T], r=[qTd], w=[qt])
                        for kc in range(nk):
                            for comp in range(2):
                                pS = P[4 + npt % 4]
                                pT = pTs[npt % 4]
                                npt += 1
                                kb.mm(pS.a, [(kt.a[comp * 64:(comp + 1) * 64, kc * 128:(kc + 1) * 128],
                                              qt.a[comp * 64:(comp + 1) * 64, :])], r=[kt, qt], w=[pS])
                                kb.op("act", lambda e, pS=pS, pT=pT: e.activation(pT.a, pS.a, AF.Exp, scale=scale), r=[pS], w=[pT])
                                kb.mm(P[comp].a, [(vt.a[:, kc, :], pT.a)], r=[vt, pT], w=[P[comp]],
                                      start=(kc == 0), stop=(kc == nk - 1))
                                kb.mm(P[2 + comp].a, [(self.ones_bf.a, pT.a)], r=[self.ones_bf, pT], w=[P[2 + comp]],
                                      start=(kc == 0), stop=(kc == nk - 1))
                        for comp in range(2):
                            kb.op("dve", lambda e, comp=comp: e.reciprocal(rz.a[:, comp, :], P[2 + comp].a), r=[P[2 + comp]], w=[rz])
                        kb.op("dve", lambda e: e.tensor_tensor(o1.a, P[0].a, rz.a[:, 0, :], op=ALU.mult), r=[P[0], rz], w=[o1])
                        kb.op("dve", lambda e: e.tensor_tensor(o2.a, P[1].a, rz.a[:, 1, :], op=ALU.mult), r=[P[1], rz], w=[o2])
                        kb.op("dve", lambda e: e.scalar_tensor_tensor(o1.a, o2.a, lam.a[:, 2:3], o1.a, op0=ALU.mult, op1=ALU.add),
                              r=[o1, o2, lam], w=[o1])
                        kb.op("act", lambda e: e.activation(osq.a, o1.a, AF.Square), r=[o1], w=[osq])
                        pss = P[4 + npt % 4]
                        npt += 1
                        kb.mm(pss.a, [(self.ones_bf.a, osq.a)], r=[self.ones_bf, osq], w=[pss])
                        kb.op("act", lambda e, pss=pss: e.activation(ors.a, pss.a, AF.Sqrt, bias=RMS_EPS, scale=1.0 / 128),
                              r=[pss], w=[ors])
                        kb.op("dve", lambda e: e.reciprocal(ors.a, ors.a), r=[ors], w=[ors])
                        kb.op("dve", lambda e, of=of: e.scalar_tensor_tensor(of.a, o1.a, lam.a[:, 3:4], ors.a, op0=ALU.mult, op1=ALU.mult),
                              r=[o1, lam, ors], w=[of])
                        kb.dma("sp", oTd.a[:, h, q0:q0 + T], of.a, r=[of], w=[oTd])
        with kb.scope():
            wo = kb.sb("wo", [128, 8, 1024], BF16)
            self.load_w_bf(wo, self.inp("da_w_o", [128, 8, 1024]), 8, 1024)
            T3 = 256
            xts = [kb.sb("xt%d" % i, [128, 8, T3], F32) for i in range(2)]
            ots = [kb.sb("ot%d" % i, [128, 8, T3], BF16) for i in range(2)]
            hns = [kb.sb("hn%d" % i, [128, 8, T3], F32) for i in range(2)]
            fbfs = [kb.sb("fbf%d" % i, [128, 8, T3], BF16) for i in range(2)]
            sq = kb.sb("sq", [128, 8, T3], BF16)
            tmp = kb.sb("tmp", [128, 8, T3], F32)
            rstd2 = kb.sb("rstd2", [128, T3], F32)
            for it, (s0, s1, row, t0) in enumerate(self.tiles(segs_lat, T3)):
                xt, ot, hn, fbf = xts[it % 2], ots[it % 2], hns[it % 2], fbfs[it % 2]
                kb.dma("sp", xt.a, src.a[:, :, t0:t0 + T3], r=[src], w=[xt])
                kb.dma("sp", ot.a, oTd.a[:, :, t0:t0 + T3], r=[oTd], w=[ot])
                for dm in range(8):
                    py = P[2 + dm % 6]
                    kb.mm(py.a[:, 0:T3], [(wo.a[:, k, dm * 128:(dm + 1) * 128], ot.a[:, k, :]) for k in range(8)],
                          r=[wo, ot], w=[py])
                    kb.op("dve", lambda e, py=py, dm=dm, hn=hn, xt=xt: e.scalar_tensor_tensor(
                        hn.a[:, dm, :], py.a[:, 0:T3], self.mod.a[:, l, row, 16 + dm:17 + dm], xt.a[:, dm, :],
                        op0=ALU.mult, op1=ALU.add), r=[py, self.mod, xt], w=[hn])
                self.post_mixer(l, row, hn, T3, dst, fdst, t0, (sq, tmp, rstd2, fbf))

    def peer_convert(self, l):
        kb = self.kb
        uT = self.inp("peer_uT%d" % l, [128, 8, PEER_E])
        vv = self.inp("peer_v%d" % l, [128, 128, 1024])
        ubf = kb.dram("ubf%d" % l, [128, 8, PEER_E], BF16)
        vbf = kb.dram("vbf%d" % l, [128, 128, 1024], BF16)
        for k in range(8):
            for c in range(0, PEER_E, 4096):
                kb.dma("pool", ubf.a[:, k, c:c + 4096], uT.a[:, k, c:c + 4096], r=[uT], w=[ubf])
        for i0 in range(0, 128, 4):
            kb.dma("pool", vbf.a[:, i0:i0 + 4, :], vv.a[:, i0:i0 + 4, :], r=[vv], w=[vbf])
        return ubf, vbf

    def stage_peer_q(self, l, fsrc, qTd, t_lo, t_hi):
        kb = self.kb
        T = 512
        with kb.scope():
            wq = kb.sb("wq", [128, 8, 2048], BF16)
            self.load_w_bf(wq, self.inp("peer_wq%d" % l, [128, 8, 2048]), 8, 2048)
            fts = [kb.sb("ft%d" % i, [128, 8, T], BF16) for i in range(2)]
            qts = [kb.sb("qt%d" % i, [128, 16, T], BF16) for i in range(2)]
            for it, t0 in enumerate(range(t_lo, t_hi, T)):
                n = min(T, t_hi - t0)
                ft, qt = fts[it % 2], qts[it % 2]
                kb.dma("sp", ft.a[:, :, 0:n], fsrc.a[:, :, t0:t0 + n], r=[fsrc], w=[ft])
                for c in range(16):
                    pq = self.P[c % 8]
                    kb.mm(pq.a[:, 0:n], [(wq.a[:, k, c * 128:(c + 1) * 128], ft.a[:, k, 0:n]) for k in range(8)],
                          r=[wq, ft], w=[pq])
                    en = "act" if c % 2 == 0 else "dve"
                    if en == "act":
                        kb.op("act", lambda e, c=c, pq=pq, qt=qt: e.copy(qt.a[:, c, 0:n], pq.a[:, 0:n]), r=[pq], w=[qt])
                    else:
                        kb.op("dve", lambda e, c=c, pq=pq, qt=qt: e.tensor_copy(qt.a[:, c, 0:n], pq.a[:, 0:n]), r=[pq], w=[qt])
                kb.dma("sp", qTd.a[:, :, t0:t0 + n], qt.a[:, :, 0:n], r=[qt], w=[qTd])

    def stage_peer_route(self, l, qTd, Wd, t_lo, t_hi):
        kb = self.kb
        P = self.P
        NEG = -1.0e30
        with kb.scope():
            KT = kb.sb("KT", [128, 16, 128], BF16)
            kb.dma("pool", KT.a, self.inp("peer_kT%d" % l, [128, 16, 128]).a, r=[self.din["peer_kT%d" % l]], w=[KT])
            ident = kb.sb("ident", [128, 128], BF16)
            if "ident" not in self.din:
                self.inp("ident", [128, 128])
            kb.dma("pool", ident.a, self.din["ident"].a, r=[self.din["ident"]], w=[ident])
            qts = [kb.sb("rq%d" % i, [128, 16, 128], BF16) for i in range(2)]
            S = kb.sb("S", [128, 16, 128], F32)
            Sx = kb.sb("Sx", [128, 256], F32)
            M = kb.sb("M", [128, 16, 16], F32)
            cand = kb.sb("cand", [128, 8, 256], F32)
            C16 = kb.sb("C16", [128, 8, 16], F32)
            sm = kb.sb("sm", [128, 8, 16], F32)
            Zs = kb.sb("Zs", [128, 8], F32)
            thr = kb.sb("thr", [128, 8], F32)
            E1 = kb.sb("E1", [128, 8, 16], F32)
            cc = kb.sb("cc", [128, 8, 16], F32)
            E2 = kb.sb("E2", [128, 8, 128], F32)
            tms = [kb.sb("tm%d" % i, [128, 128], F32) for i in range(4)]
            R = kb.sb("R", [128, 128, 128], BF16)
            OH = kb.sb("OH", [128, 128, 128], BF16)
            RT = kb.sb("RT", [128, 128, 128], BF16)
            OHT = kb.sb("OHT", [128, 128, 128], BF16)
            S4 = S.a.rearrange("p (h two) n -> p h two n", two=2)
            M4 = M.a.rearrange("p (h two) n -> p h two n", two=2)
            for it, t0 in enumerate(range(t_lo, t_hi, 128)):
                g = t0 // 128
                qt = qts[it % 2]
                kb.dma("sp", qt.a, qTd.a[:, :, t0:t0 + 128], r=[qTd], w=[qt])
                for b in range(4):
                    for cI in range(4):
                        c = b * 4 + cI
                        kb.mm(P[b].a[:, cI * 128:(cI + 1) * 128], [(qt.a[:, c, :], KT.a[:, c, :])], r=[qt, KT], w=[P[b]])
                    kb.op("act", lambda e, b=b: e.copy(S.a[:, b * 4:(b + 1) * 4, :].rearrange("p c n -> p (c n)"), P[b].a),
                          r=[P[b]], w=[S])
                for c in range(16):
                    kb.op("dve", lambda e, c=c: e.max(out=M.a[:, c, 0:8], in_=S.a[:, c, :]), r=[S], w=[M])
                    kb.op("dve", lambda e, c=c: e.match_replace(out=Sx.a[:, 0:128], in_to_replace=M.a[:, c, 0:8],
                                                                in_values=S.a[:, c, :], imm_value=NEG), r=[S, M], w=[Sx])
                    kb.op("dve", lambda e, c=c: e.max(out=M.a[:, c, 8:16], in_=Sx.a[:, 0:128]), r=[Sx], w=[M])
                for h in range(8):
                    kb.op("pool", lambda e, h=h: e.tensor_tensor(
                        cand.a[:, h, :].rearrange("p (a b) -> p a b", a=16),
                        M.a[:, 2 * h, :].unsqueeze(2).to_broadcast([128, 16, 16]),
                        M.a[:, 2 * h + 1, :].unsqueeze(1).to_broadcast([128, 16, 16]), op=ALU.add), r=[M], w=[cand])
                for h in range(8):
                    kb.op("dve", lambda e, h=h: e.max(out=C16.a[:, h, 0:8], in_=cand.a[:, h, :]), r=[cand], w=[C16])
                    kb.op("dve", lambda e, h=h: e.match_replace(out=Sx.a, in_to_replace=C16.a[:, h, 0:8],
                                                                in_values=cand.a[:, h, :], imm_value=NEG), r=[cand, C16], w=[Sx])
                    kb.op("dve", lambda e, h=h: e.max(out=C16.a[:, h, 8:16], in_=Sx.a), r=[Sx], w=[C16])
                kb.op("dve", lambda e: e.tensor_tensor(sm.a, C16.a, C16.a[:, :, 0:1].to_broadcast([128, 8, 16]),
                                                       op=ALU.subtract), r=[C16], w=[sm])
                kb.op("act", lambda e: e.activation(sm.a, sm.a, AF.Exp), r=[sm], w=[sm])
                kb.op("dve", lambda e: e.tensor_reduce(out=Zs.a, in_=sm.a, axis=AX.X, op=ALU.add), r=[sm], w=[Zs])
                kb.op("dve", lambda e: e.reciprocal(Zs.a, Zs.a), r=[Zs], w=[Zs])
                kb.op("dve", lambda e: e.scalar_tensor_tensor(thr.a, C16.a[:, :, 15], -1.0, C16.a[:, :, 0],
                                                              op0=ALU.mult, op1=ALU.max), r=[C16], w=[thr])
                kb.op("dve", lambda e: e.scalar_tensor_tensor(thr.a, thr.a, -2.0e-5, C16.a[:, :, 15],
                                                              op0=ALU.mult, op1=ALU.add), r=[thr, C16], w=[thr])
                kb.op("dve", lambda e: e.tensor_tensor(E1.a, M4[:, :, 0, :], M4[:, :, 0, 0:1].to_broadcast([128, 8, 16]),
                                                       op=ALU.subtract), r=[M], w=[E1])
                kb.op("act", lambda e: e.activation(E1.a, E1.a, AF.Exp), r=[E1], w=[E1])
                kb.op("dve", lambda e: e.tensor_tensor(E1.a, E1.a, Zs.a.unsqueeze(2).to_broadcast([128, 8, 16]),
                                                       op=ALU.mult), r=[E1, Zs], w=[E1])
                kb.op("dve", lambda e: e.scalar_tensor_tensor(cc.a, M4[:, :, 0, :], -1.0,
                                                              thr.a.unsqueeze(2).to_broadcast([128, 8, 16]),
                                                              op0=ALU.mult, op1=ALU.add), r=[M, thr], w=[cc])
                kb.op("dve", lambda e: e.tensor_tensor(E2.a, S4[:, :, 1, :], M4[:, :, 1, 0:1].to_broadcast([128, 8, 128]),
                                                       op=ALU.subtract), r=[S, M], w=[E2])
                kb.op("act", lambda e: e.activation(E2.a, E2.a, AF.Exp), r=[E2], w=[E2])
                for h in range(8):
                    kb.op("dve", lambda e, h=h: e.tensor_tensor(
                        OH.a[:, h * 16:(h + 1) * 16, :],
                        S.a[:, 2 * h, :].unsqueeze(1).to_broadcast([128, 16, 128]),
                        M.a[:, 2 * h, :].unsqueeze(2).to_broadcast([128, 16, 128]), op=ALU.is_equal), r=[S, M], w=[OH])
                    for r in range(16):
                        c = h * 16 + r
                        tm = tms[c % 4]
                        kb.op("dve", lambda e, h=h, r=r, tm=tm: e.scalar_tensor_tensor(
                            tm.a, S.a[:, 2 * h + 1, :], cc.a[:, h, r:r + 1], E2.a[:, h, :], op0=ALU.is_ge, op1=ALU.mult),
                            r=[S, cc, E2], w=[tm])
                        kb.op("act", lambda e, h=h, r=r, c=c, tm=tm: e.activation(
                            R.a[:, c, :], tm.a, AF.Identity, scale=E1.a[:, h, r:r + 1]), r=[tm, E1], w=[R])
                nb = 0
                for (srcb, dstb) in ((R, RT), (OH, OHT)):
                    for j0 in range(0, 128, 4):
                        pb = P[nb % 8]
                        nb += 1
                        for jj in range(4):
                            kb.mm(pb.a[:, jj * 128:(jj + 1) * 128], [(srcb.a[:, :, j0 + jj], ident.a)],
                                  r=[srcb, ident], w=[pb])
                        dv = dstb.a[:, j0:j0 + 4, :].rearrange("p j t -> p (j t)")
                        if nb % 2 == 0:
                            kb.op("act", lambda e, dv=dv, pb=pb: e.copy(dv, pb.a), r=[pb], w=[dstb])
                        else:
                            kb.op("dve", lambda e, dv=dv, pb=pb: e.tensor_copy(dv, pb.a), r=[pb], w=[dstb])
                Wv = R
                for tq in range(0, 128, 4):
                    pb = P[nb % 8]
                    nb += 1
                    for tt in range(4):
                        kb.mm(pb.a[:, tt * 128:(tt + 1) * 128], [(RT.a[:, :, tq + tt], OHT.a[:, :, tq + tt])],
                              r=[RT, OHT], w=[pb])
                    dv = Wv.a[:, :, tq:tq + 4].rearrange("p i t -> p t i")
                    sv = pb.a.rearrange("p (t i) -> p t i", t=4)
                    if nb % 2 == 0:
                        kb.op("act", lambda e, dv=dv, sv=sv: e.copy(dv, sv), r=[pb], w=[Wv])
                    else:
                        kb.op("dve", lambda e, dv=dv, sv=sv: e.tensor_copy(dv, sv), r=[pb], w=[Wv])
                kb.dma("sp", Wd.a[g], Wv.a, r=[Wv], w=[Wd])

    def stage_peer_experts(self, l, ubf, vbf, fsrc, hsrc, hdst, Wd, segs, dst_off=0):
        kb = self.kb
        P = self.P
        T = 256
        NBG = 4
        with kb.scope():
            uts = [kb.sb("ut%d" % i, [128, 8, NBG * 128], BF16) for i in range(3)]
            vts = [kb.sb("vt%d" % i, [128, NBG, 1024], BF16) for i in range(3)]
            wts = [kb.sb("wt%d" % i, [128, 2, NBG, 128], BF16) for i in range(3)]
            fts = [kb.sb("eft%d" % i, [128, 8, T], BF16) for i in range(2)]
            hts = [kb.sb("eht%d" % i, [128, 8, T], F32) for i in range(2)]
            gzs = [kb.sb("gz%d" % i, [128, T], F32) for i in range(2)]
            As = [kb.sb("A%d" % i, [128, T], BF16) for i in range(3)]
            nslot = 0
            nz = 0
            for ig, (s0, s1, row, t0) in enumerate(self.tiles(segs, T)):
                ft, ht = fts[ig % 2], hts[ig % 2]
                kb.dma("sp", ft.a, fsrc.a[:, :, t0:t0 + T], r=[fsrc], w=[ft])
                kb.dma("sp", ht.a, hsrc.a[:, :, t0:t0 + T], r=[hsrc], w=[ht])
                g0 = t0 // 128
                for bg in range(128 // NBG):
                    ut, vt, wt = uts[nslot % 3], vts[nslot % 3], wts[nslot % 3]
                    nslot += 1
                    kb.dma("sp", ut.a, ubf.a[:, :, bg * NBG * 128:(bg + 1) * NBG * 128], r=[ubf], w=[ut])
                    kb.dma("sp", vt.a, vbf.a[:, bg * NBG:(bg + 1) * NBG, :], r=[vbf], w=[vt])
                    for tl in range(2):
                        kb.dma("sp", wt.a[:, tl, :, :], Wd.a[g0 + tl][:, bg * NBG:(bg + 1) * NBG, :], r=[Wd], w=[wt])
                    for b in range(NBG):
                        i = bg * NBG + b
                        pz = P[4 + nz % 4]
                        gz, A = gzs[nz % 2], As[nz % 3]
                        nz += 1
                        kb.mm(pz.a[:, 0:T], [(ut.a[:, k, b * 128:(b + 1) * 128], ft.a[:, k, :]) for k in range(8)],
                              r=[ut, ft], w=[pz])
                        kb.op("act", lambda e, gz=gz, pz=pz: e.activation(gz.a, pz.a[:, 0:T], AF.Gelu), r=[pz], w=[gz])
                        kb.op("dve", lambda e, gz=gz, A=A, wt=wt, b=b: e.tensor_tensor(
                            A.a.rearrange("p (a t) -> p a t", a=2), gz.a.rearrange("p (a t) -> p a t", a=2),
                            wt.a[:, :, b, :], op=ALU.mult), r=[gz, wt], w=[A])
                        for dm in range(8):
                            po = P[dm // 2]
                            kb.mm(po.a[:, (dm % 2) * T:(dm % 2 + 1) * T], [(vt.a[:, b, dm * 128:(dm + 1) * 128], A.a)],
                                  r=[vt, A], w=[po], start=(i == 0), stop=(i == 127))
                for dm in range(8):
                    po = P[dm // 2]
                    kb.op("dve", lambda e, dm=dm, po=po, ht=ht: e.scalar_tensor_tensor(
                        ht.a[:, dm, :], po.a[:, (dm % 2) * T:(dm % 2 + 1) * T], self.mod.a[:, l, row, 40 + dm:41 + dm],
                        ht.a[:, dm, :], op0=ALU.mult, op1=ALU.add), r=[po, self.mod, ht], w=[ht])
                kb.dma("sp", hdst.a[:, :, t0 - dst_off:t0 - dst_off + T], ht.a, r=[ht], w=[hdst])

def kmaj(w):
    w = np.asarray(w, np.float32)
    nk = w.shape[0] // 128
    return np.ascontiguousarray(w.reshape(nk, 128, w.shape[1]).transpose(1, 0, 2))


def tok_to_T(X):
    X = np.asarray(X)
    return np.ascontiguousarray(X.T.reshape(8, 128, X.shape[0]).transpose(1, 0, 2))


def T_to_tok(hT):
    return np.ascontiguousarray(hT.transpose(1, 0, 2).reshape(1024, hT.shape[2]).T)


def build_vecs(inp, crows):
    vp = VecPack()
    cT = np.stack([packv(crows[r]) for r in range(3)], axis=2)
    vp.add("cT", cT.reshape(128, 24))
    for l in range(DEPTH):
        vp.add("bmod%d" % l, packv(inp["b_mod"][l]))
        vp.add("n1g%d" % l, packv(inp["norm1_g"][l]))
        vp.add("n2g%d" % l, packv(inp["norm2_g"][l]))
    for j in range(inp["sc_conv_w"].shape[0]):
        cw = inp["sc_conv_w"][j]
        vp.add("scw%d" % j, np.concatenate([packv(cw[k]) for k in range(3)], axis=1))
    if "cf_b_pw1" in inp:
        vp.add("cfb1", packv(inp["cf_b_pw1"][0]))
        dw = inp["cf_dw_w"][0]
        vp.add("cfdw", np.concatenate([packv(dw[k]) for k in range(dw.shape[0])], axis=1))
        for key, nm in (("cf_dw_b", "cfdwb"), ("cf_ln_g", "cflng"), ("cf_ln_b", "cflnb"), ("cf_b_pw2", "cfb2")):
            vp.add(nm, packv(inp[key][0]))
    if "da_q_norm_g" in inp:
        rep = lambda v: np.ascontiguousarray(np.broadcast_to(np.asarray(v, np.float32)[None, :], (128, len(v))))
        vp.add("qg", np.tile(np.asarray(inp["da_q_norm_g"][0], np.float32), 2).reshape(128, 1))
        vp.add("kg", np.tile(np.asarray(inp["da_k_norm_g"][0], np.float32), 2).reshape(128, 1))
        vp.add("subg", np.asarray(inp["da_subln_g"][0], np.float32).reshape(128, 1))
        for key, nm in (("da_lam_q1", "lq1"), ("da_lam_k1", "lk1"), ("da_lam_q2", "lq2"), ("da_lam_k2", "lk2")):
            vp.add(nm, rep(inp[key][0]))
    return vp


def host_consts():
    c = {}
    c["ident"] = np.eye(128, dtype=np.float32)
    blk = np.zeros((128, 128), np.float32)
    blk[:64, :64] = 1.0
    blk[64:, 64:] = 1.0
    c["blkones"] = blk
    prot = np.zeros((128, 128), np.float32)
    cosT = np.zeros((128, NLAT), np.float32)
    sinT = np.zeros((128, NLAT), np.float32)
    t = np.arange(NLAT)
    pos = (np.floor_divide(t, 64).astype(np.float32), np.mod(t, 64).astype(np.float32))
    nf = 16
    inv_freq = (10000.0 ** (-np.arange(nf, dtype=np.float32) / nf)).astype(np.float32)
    for p in range(128):
        dh = p % 64
        axis, half, f = dh // 32, (dh % 32) // 16, dh % 16
        ang = pos[axis] * inv_freq[f]
        cosT[p] = np.cos(ang)
        sinT[p] = np.sin(ang)
        if half == 0:
            prot[p + 16, p] = -1.0
        else:
            prot[p - 16, p] = 1.0
    c["protm"] = prot
    c["ropecos"] = cosT
    c["ropesin"] = sinT
    return c


SEGS_LAT = [(0, NLAT, 0), (NLAT, 2 * NLAT, 1)]
SEGS_CTX = [(2 * NLAT, 2 * NLAT + NCTX, 2), (2 * NLAT + NCTX, 2 * NLAT + 2 * NCTX, 2)]
NTOK = 2 * NLAT + 2 * NCTX


def build_program(voff, nv):
    pg = Prog(SEGS_LAT, SEGS_CTX, voff, nv)
    kb = pg.kb
    pg.prologue()
    pg.stage_mod(range(DEPTH))
    xT = pg.inp("xT", [128, 8, NTOK])
    hX = kb.dram("hX", [128, 8, NTOK], F32)
    hM = kb.dram("hM", [128, 8, NTOK], F32)
    fT = kb.dram("fT", [128, 8, NTOK], BF16)
    qTd = kb.dram("p_qT", [128, 16, NTOK], BF16)
    Wd = kb.dram("p_W", [NTOK // 128, 128, 128, 128], BF16)
    yT = kb.dram("yT", [128, 8, 2 * NLAT], F32, kind="ExternalOutput")
    tabs = {0: pg.peer_convert(0)}
    for l in range(DEPTH):
        src = xT if l == 0 else hX
        kind = l % 3
        if kind == 0:
            segs = SEGS_LAT + (SEGS_CTX if l == 0 else [])
            pg.stage_sconv(l, src, hM, fT, segs)
        elif kind == 1:
            pg.stage_attn(l, src, hM, fT, SEGS_LAT, SEGS_CTX, 0.8 - 0.6 * math.exp(-0.3 * l))
        else:
            pg.stage_conformer(l, src, hM, fT, SEGS_LAT)
        if l + 1 < DEPTH:
            tabs[l + 1] = pg.peer_convert(l + 1)
        psegs = SEGS_LAT + (SEGS_CTX if l == 0 else [])
        t_hi = psegs[-1][1]
        pg.stage_peer_q(l, fT, qTd, 0, t_hi)
        pg.stage_peer_route(l, qTd, Wd, 0, t_hi)
        ubf, vbf = tabs[l]
        pg.stage_peer_experts(l, ubf, vbf, fT, hM, yT if l == DEPTH - 1 else hX, Wd, psegs)
    kb.finish([yT])
    return pg


def kernel(**inputs):
    inp = {k: np.asarray(v) for k, v in inputs.items()}
    ncore = 8
    consts = host_consts()
    shared = dict(consts)
    for l in range(DEPTH):
        shared["w_mod%d" % l] = kmaj(inp["w_mod"][l])
        shared["peer_uT%d" % l] = kmaj(inp["peer_u"][l].T)
        shared["peer_v%d" % l] = np.ascontiguousarray(inp["peer_v"][l].reshape(128, 128, D).transpose(1, 0, 2))
        shared["peer_wq%d" % l] = kmaj(inp["peer_w_query"][l])
        shared["peer_kT%d" % l] = np.ascontiguousarray(inp["peer_sub_keys"][l].reshape(16, 128, 128).transpose(2, 0, 1))
    for j in range(inp["sc_w_in"].shape[0]):
        shared["sc_w_in%d" % j] = kmaj(inp["sc_w_in"][j])
        shared["sc_w_out%d" % j] = kmaj(inp["sc_w_out"][j])
    shared["da_w_qkv"] = kmaj(inp["da_w_qkv"][0])
    shared["da_w_o"] = kmaj(inp["da_w_o"][0])
    shared["cf_w_pw1"] = kmaj(inp["cf_w_pw1"][0])
    shared["cf_w_pw2"] = kmaj(inp["cf_w_pw2"][0])
    in_maps = []
    pg = None
    for c in range(ncore):
        b0, b1 = 2 * c, 2 * c + 1
        crows = np.stack([inp["c"][b0], inp["c"][b1], inp["c_ctx"]], axis=0)
        vp = build_vecs(inp, crows)
        if pg is None:
            pg = build_program(vp.off, vp.n)
        X = np.concatenate([inp["x"][b0], inp["x"][b1], inp["ctx"][b0], inp["ctx"][b1]], axis=0)
        m = dict(shared)
        m["vecs"] = vp.array()
        m["xT"] = tok_to_T(X)
        in_maps.append({k: m[k] for k in pg.din})
    res = run_bass_kernel_spmd(pg.kb.nc, in_maps, core_ids=list(range(ncore)))
    out = np.empty((2 * ncore, NLAT, D), np.float32)
    for c in range(ncore):
        Y = T_to_tok(np.asarray(res.results[c]["yT"]))
        out[2 * c] = Y[:NLAT]
        out[2 * c + 1] = Y[NLAT:]
    return out
```

```python
import contextlib
import math
import numpy as np
import concourse.bass as bass
import concourse.mybir as mybir
from concourse.bass_utils import run_bass_kernel_spmd

F32 = mybir.dt.float32
BF16 = mybir.dt.bfloat16
ALU = mybir.AluOpType
AF = mybir.ActivationFunctionType
AX = mybir.AxisListType

D = 1024
KC = 8
NLAT = 2048
NCTX = 256
NB = 2
DEPTH = 4
RMS_EPS = 1e-6
LN_EPS = 1e-5
PEER_E = 16384


class Buf:
    def __init__(self, name, a=None, space="sb"):
        self.name = name
        self.a = a
        self.space = space
        self.w = {}
        self.r = {}
        self.sem = None
        self.cnt = 0


class KB:
    def __init__(self):
        self.nc = bass.Bass("TRN2", target_bir_lowering=False)
        self.es = contextlib.ExitStack()
        nc = self.nc
        self.sems = []
        self.eng = {}
        for nm, e in (("pe", nc.tensor), ("act", nc.scalar), ("dve", nc.vector),
                      ("pool", nc.gpsimd), ("sp", nc.sync)):
            self.eng[nm] = dict(e=e, sem=self.newsem("s_" + nm), cnt=0, waited={})
        self.nuniq = 0
        self.dcount = {}
        self.freed = []
        self.stack = [self.es]
        self.scoped = []

    def newsem(self, name):
        h = self.es.enter_context(self.nc.semaphore(name))
        self.sems.append(h)
        return len(self.sems) - 1

    def barrier(self):
        deps = {}
        for E in self.eng.values():
            deps[E["sem"]] = E["cnt"]
        deps.update(self.dcount)
        for en in self.eng:
            self._wait(en, {k: v for k, v in deps.items() if v > 0})

    @contextlib.contextmanager
    def scope(self):
        st = contextlib.ExitStack()
        self.stack.append(st)
        self.scoped.append([])
        try:
            yield
        finally:
            self.barrier()
            for b in self.scoped.pop():
                if b.sem is not None:
                    self.freed.append(b.sem)
                    b.sem = None
            self.stack.pop()
            st.close()

    def sb(self, name, shape, dt):
        self.nuniq += 1
        t = self.stack[-1].enter_context(self.nc.sbuf_tensor("%s_%d" % (name, self.nuniq), list(shape), dt))
        b = Buf(name, t[:])
        if self.scoped:
            self.scoped[-1].append(b)
        return b

    def ps(self, name, shape, dt=F32):
        t = self.es.enter_context(self.nc.psum_tensor(name, list(shape), dt))
        return Buf(name, t[:])

    def dram(self, name, shape, dt, kind="Internal"):
        t = self.nc.dram_tensor(name, list(shape), dt, kind=kind)
        return Buf(name, t.ap(), space="dram")

    def _deps(self, r, w):
        deps = {}
        for b in r:
            for k, v in b.w.items():
                if deps.get(k, 0) < v:
                    deps[k] = v
        for b in w:
            for dd in (b.w, b.r):
                for k, v in dd.items():
                    if deps.get(k, 0) < v:
                        deps[k] = v
        return deps

    def _wait(self, en, deps):
        E = self.eng[en]
        for k, v in deps.items():
            if E["waited"].get(k, 0) >= v:
                continue
            E["e"].wait_ge(self.sems[k], v)
            E["waited"][k] = v

    def _done(self, tok, r, w):
        k, v = tok
        for b in r:
            if b.r.get(k, 0) < v:
                b.r[k] = v
        for b in w:
            if b.w.get(k, 0) < v:
                b.w[k] = v
            b.r = {}

    def op(self, en, fn, r=(), w=()):
        deps = self._deps(r, w)
        if en == "pe":
            deps.pop(self.eng["pe"]["sem"], None)
        self._wait(en, deps)
        E = self.eng[en]
        ins = fn(E["e"])
        E["cnt"] += 1
        ins.then_inc(self.sems[E["sem"]], 1)
        self._done((E["sem"], E["cnt"]), r, w)

    def dma(self, q, out, in_, r=(), w=()):
        self._wait(q, self._deps(r, w))
        E = self.eng[q]
        ins = E["e"].dma_start(out=out, in_=in_)
        d = next((b for b in list(w) + list(r) if b.space == "sb"), w[0])
        if d.sem is None:
            if self.freed:
                d.sem = self.freed.pop()
            else:
                d.sem = self.newsem("d%d" % len(self.sems))
        c = self.dcount.get(d.sem, 0) + 16
        self.dcount[d.sem] = c
        ins.then_inc(self.sems[d.sem], 16)
        self._done((d.sem, c), r, w)

    def mm(self, out, pairs, r=(), w=(), start=True, stop=True):
        def fn(pe):
            n = len(pairs)
            ins = None
            for i, (l, rh) in enumerate(pairs):
                ins = pe.matmul(out, lhsT=l, rhs=rh, start=(start and i == 0), stop=(stop and i == n - 1))
            return ins
        self.op("pe", fn, r, w)

    def finish(self, outs):
        deps = {}
        for b in outs:
            for k, v in b.w.items():
                deps[k] = max(deps.get(k, 0), v)
        self._wait("sp", deps)


def packv(v):
    v = np.asarray(v, np.float32).reshape(-1, 128)
    return np.ascontiguousarray(v.T)


class VecPack:
    def __init__(self):
        self.off = {}
        self.n = 0
        self.cols = []

    def add(self, key, arr128xn):
        a = np.asarray(arr128xn, np.float32)
        assert a.shape[0] == 128
        self.off[key] = (self.n, a.shape[1])
        self.n += a.shape[1]
        self.cols.append(a)

    def array(self):
        return np.ascontiguousarray(np.concatenate(self.cols, axis=1))


class Prog:
    def __init__(self, segs_lat, segs_ctx, voff, nv, layers=range(DEPTH)):
        self.kb = KB()
        kb = self.kb
        self.segs_lat = segs_lat
        self.segs_ctx = segs_ctx
        self.ntok = (segs_ctx[-1][1] if segs_ctx else segs_lat[-1][1])
        self.voff = voff
        self.nv = nv
        self.din = {}
        self.vec = kb.sb("vec", [128, nv], F32)
        self.ones_bf = kb.sb("ones_bf", [128, 128], BF16)
        self.ones_f = kb.sb("ones_f", [128, 128], F32)
        self.P = [kb.ps("P%d" % i, [128, 512], F32) for i in range(8)]

    def inp(self, name, shape, dt=F32):
        b = self.kb.dram(name, shape, dt, kind="ExternalInput")
        self.din[name] = b
        return b

    def v(self, key, lo=0, n=None):
        o, w = self.voff[key]
        if n is None:
            n = w - lo
        return self.vec.a[:, o + lo:o + lo + n]

    def prologue(self):
        kb = self.kb
        vecd = self.inp("vecs", [128, self.nv])
        kb.dma("sp", self.vec.a, vecd.a, r=[vecd], w=[self.vec])
        kb.op("dve", lambda e: e.memset(self.ones_bf.a, 1.0), w=[self.ones_bf])
        kb.op("dve", lambda e: e.memset(self.ones_f.a, 1.0), w=[self.ones_f])
        self.mod = kb.sb("mod", [128, DEPTH, 3, 48], F32)
        self.gs = kb.sb("gs", [128, DEPTH, 3, 2, 8], F32)

    def stage_mod(self, layers):
        kb = self.kb
        with kb.scope():
            sc = kb.sb("sc", [128, 8, 4], F32)
            kb.op("act", lambda e: e.activation(sc.a[:, :, 0:3], self.v("cT").rearrange("p (k r) -> p k r", r=3),
                                                AF.Silu), r=[self.vec], w=[sc])
            wts = [kb.sb("wm%d" % i, [128, 8, 512], F32) for i in range(2)]
            it = 0
            for l in layers:
                wd = self.inp("w_mod%d" % l, [128, 8, 6144])
                pm = self.P[l % 2]
                for cb in range(12):
                    wt = wts[it % 2]
                    it += 1
                    kb.dma("sp", wt.a, wd.a[:, :, cb * 512:(cb + 1) * 512], r=[wd], w=[wt])
                    for jj in range(4):
                        j = cb * 4 + jj
                        kb.mm(pm.a[:, j * 4:j * 4 + 3],
                              [(wt.a[:, k, jj * 128:(jj + 1) * 128], sc.a[:, k, 0:3]) for k in range(8)],
                              r=[wt, sc], w=[pm])
                for r in range(3):
                    kb.op("dve", lambda e, r=r, l=l, pm=pm: e.tensor_tensor(
                        self.mod.a[:, l, r, :], pm.a[:, 0:192].rearrange("p (j r) -> p j r", r=4)[:, :, r],
                        self.v("bmod%d" % l), op=ALU.add), r=[pm, self.vec], w=[self.mod])
                    for h, (c0, gk) in enumerate(((8, "n1g%d" % l), (32, "n2g%d" % l))):
                        kb.op("dve", lambda e, r=r, l=l, h=h, c0=c0, gk=gk: e.scalar_tensor_tensor(
                            self.gs.a[:, l, r, h, :], self.mod.a[:, l, r, c0:c0 + 8], 1.0, self.v(gk),
                            op0=ALU.add, op1=ALU.mult), r=[self.mod, self.vec], w=[self.gs])

    def rstd_of(self, x3, n, sq, pss, rstd, rbufs, eps=RMS_EPS, nfeat=D, ones=None, nk=8):
        kb = self.kb
        ones = ones if ones is not None else self.ones_bf
        kb.op("act", lambda e: e.activation(sq.a[:, 0:nk, 0:n], x3, AF.Square), r=rbufs, w=[sq])
        kb.mm(pss.a[:, 0:n], [(ones.a, sq.a[:, k, 0:n]) for k in range(nk)], r=[ones, sq], w=[pss])
        kb.op("act", lambda e: e.activation(rstd.a[:, 0:n], pss.a[:, 0:n], AF.Sqrt, bias=eps, scale=1.0 / nfeat),
              r=[pss], w=[rstd])
        kb.op("dve", lambda e: e.reciprocal(rstd.a[:, 0:n], rstd.a[:, 0:n]), r=[rstd], w=[rstd])

    def modulate(self, x3, n, rstd, tmp, out, gsc, shift, rbufs):
        kb = self.kb
        kb.op("dve", lambda e: e.tensor_tensor(tmp.a[:, :, 0:n], x3,
                                               rstd.a[:, 0:n].unsqueeze(1).to_broadcast([128, 8, n]), op=ALU.mult),
              r=rbufs + [rstd], w=[tmp])
        for k in range(8):
            if k % 2 == 0:
                kb.op("act", lambda e, k=k: e.activation(out.a[:, k, 0:n], tmp.a[:, k, 0:n], AF.Identity,
                                                         bias=shift[:, k:k + 1], scale=gsc[:, k:k + 1]),
                      r=[tmp, self.mod, self.gs], w=[out])
            else:
                kb.op("pool", lambda e, k=k: e.tensor_scalar(out.a[:, k, 0:n], tmp.a[:, k, 0:n], gsc[:, k:k + 1],
                                                             shift[:, k:k + 1], op0=ALU.mult, op1=ALU.add),
                      r=[tmp, self.mod, self.gs], w=[out])

    def load_w_bf(self, dst, src, nk, ncols, step=1024):
        kb = self.kb
        for k in range(nk):
            for c0 in range(0, ncols, step):
                c1 = min(ncols, c0 + step)
                kb.dma("pool", dst.a[:, k, c0:c1], src.a[:, k, c0:c1], r=[src], w=[dst])

    def tiles(self, segs, T):
        for (s0, s1, row) in segs:
            for t0 in range(s0, s1, T):
                yield s0, s1, row, t0

    def post_mixer(self, l, row, hn, T, dst, fdst, t0, bufs):
        kb = self.kb
        sq, tmp, rstd, fbf = bufs
        self.rstd_of(hn.a[:, :, 0:T], T, sq, self.P[1], rstd, [hn])
        self.modulate(hn.a[:, :, 0:T], T, rstd, tmp, fbf, self.gs.a[:, l, row, 1, :], self.mod.a[:, l, row, 24:32], [hn])
        kb.dma("sp", dst.a[:, :, t0:t0 + T], hn.a[:, :, 0:T], r=[hn], w=[dst])
        kb.dma("sp", fdst.a[:, :, t0:t0 + T], fbf.a[:, :, 0:T], r=[fbf], w=[fdst])

    def stage_sconv(self, l, src, dst, fdst, segs):
        kb = self.kb
        T, H = 256, 1
        W = T + 2 * H
        j = l // 3
        with kb.scope():
            win = kb.sb("win", [128, 8, 3072], BF16)
            wout = kb.sb("wout", [128, 8, 1024], BF16)
            self.load_w_bf(win, self.inp("sc_w_in%d" % j, [128, 8, 3072]), 8, 3072)
            self.load_w_bf(wout, self.inp("sc_w_out%d" % j, [128, 8, 1024]), 8, 1024)
            xts = [kb.sb("xt%d" % i, [128, 8, W], F32) for i in range(2)]
            abfs = [kb.sb("abf%d" % i, [128, 8, W], BF16) for i in range(2)]
            gTs = [kb.sb("gT%d" % i, [128, 8, T], BF16) for i in range(2)]
            hns = [kb.sb("hn%d" % i, [128, 8, T], F32) for i in range(2)]
            fbfs = [kb.sb("fbf%d" % i, [128, 8, T], BF16) for i in range(2)]
            sq = kb.sb("sq", [128, 8, W], BF16)
            tmp = kb.sb("tmp", [128, 8, W], F32)
            rstd = kb.sb("rstd", [128, W], F32)
            rstd2 = kb.sb("rstd2", [128, W], F32)
            csbs = [kb.sb("csb%d" % i, [128, W], F32) for i in range(2)]
            cus = [kb.sb("cu%d" % i, [128, W], F32) for i in range(2)]
            accs = [kb.sb("acc%d" % i, [128, T], F32) for i in range(2)]
            scw = self.v("scw%d" % j)
            P = self.P
            for it, (s0, s1, row, t0) in enumerate(self.tiles(segs, T)):
                lo, hi = max(t0 - H, s0), min(t0 + T + H, s1)
                n = hi - lo
                off = lo - (t0 - H)
                xt, abf, gT, hn, fbf = xts[it % 2], abfs[it % 2], gTs[it % 2], hns[it % 2], fbfs[it % 2]
                kb.dma("sp", xt.a[:, :, off:off + n], src.a[:, :, lo:hi], r=[src], w=[xt])
                x3 = xt.a[:, :, off:off + n]
                self.rstd_of(x3, n, sq, P[0], rstd, [xt])
                self.modulate(x3, n, rstd, tmp, abf, self.gs.a[:, l, row, 0, :], self.mod.a[:, l, row, 0:8], [xt])
                for ch in range(8):
                    pc, pu, pb = P[2 + ch % 2], P[4 + ch % 2], P[6 + ch % 2]
                    csb, cu, acc = csbs[ch % 2], cus[ch % 2], accs[ch % 2]
                    for (pp, cc) in ((pc, 8 + ch), (pu, 16 + ch), (pb, ch)):
                        kb.mm(pp.a[:, 0:n], [(win.a[:, k, cc * 128:(cc + 1) * 128], abf.a[:, k, 0:n]) for k in range(8)],
                              r=[win, abf], w=[pp])
                    kb.op("act", lambda e, csb=csb, pc=pc: e.copy(csb.a[:, 0:n], pc.a[:, 0:n]), r=[pc], w=[csb])
                    if off > 0:
                        kb.op("pool", lambda e, cu=cu: e.memset(cu.a[:, 0:off], 0.0), w=[cu])
                    if off + n < W:
                        kb.op("pool", lambda e, cu=cu: e.memset(cu.a[:, off + n:W], 0.0), w=[cu])
                    kb.op("dve", lambda e, cu=cu, csb=csb, pu=pu: e.tensor_tensor(
                        cu.a[:, off:off + n], csb.a[:, 0:n], pu.a[:, 0:n], op=ALU.mult), r=[csb, pu], w=[cu])
                    kb.op("pool", lambda e, cu=cu, acc=acc: e.tensor_scalar(
                        acc.a, cu.a[:, 0:T], scw[:, ch:ch + 1], None, op0=ALU.mult), r=[cu, self.vec], w=[acc])
                    for tap in (1, 2):
                        kb.op("dve", lambda e, cu=cu, acc=acc, tap=tap: e.scalar_tensor_tensor(
                            acc.a, cu.a[:, tap:tap + T], scw[:, tap * 8 + ch:tap * 8 + ch + 1], acc.a,
                            op0=ALU.mult, op1=ALU.add), r=[cu, self.vec, acc], w=[acc])
                    c0 = H - off
                    kb.op("dve", lambda e, acc=acc, pb=pb, gT=gT, ch=ch, c0=c0: e.tensor_tensor(
                        gT.a[:, ch, :], acc.a, pb.a[:, c0:c0 + T], op=ALU.mult), r=[acc, pb], w=[gT])
                for dm in range(8):
                    py = P[2 + dm % 6]
                    kb.mm(py.a[:, 0:T], [(wout.a[:, k, dm * 128:(dm + 1) * 128], gT.a[:, k, :]) for k in range(8)],
                          r=[wout, gT], w=[py])
                    kb.op("dve", lambda e, py=py, dm=dm, hn=hn, xt=xt: e.scalar_tensor_tensor(
                        hn.a[:, dm, :], py.a[:, 0:T], self.mod.a[:, l, row, 16 + dm:17 + dm], xt.a[:, dm, H:H + T],
                        op0=ALU.mult, op1=ALU.add), r=[py, self.mod, xt], w=[hn])
                self.post_mixer(l, row, hn, T, dst, fdst, t0, (sq, tmp, rstd2, fbf))


    def stage_conformer(self, l, src, dst, fdst, segs):
        kb = self.kb
        T, H = 256, 15
        W = T + 2 * H
        P = self.P
        with kb.scope():
            w1 = kb.sb("w1", [128, 8, 2048], BF16)
            w2 = kb.sb("w2", [128, 8, 1024], BF16)
            self.load_w_bf(w1, self.inp("cf_w_pw1", [128, 8, 2048]), 8, 2048)
            self.load_w_bf(w2, self.inp("cf_w_pw2", [128, 8, 1024]), 8, 1024)
            xts = [kb.sb("xt%d" % i, [128, 8, W], F32) for i in range(2)]
            abf = kb.sb("abf", [128, 8, W], BF16)
            sq = kb.sb("sq", [128, 8, W], BF16)
            tmp = kb.sb("tmp", [128, 8, W], F32)
            rstd = kb.sb("rstd", [128, W], F32)
            rstd2 = kb.sb("rstd2", [128, W], F32)
            sgs = [kb.sb("sg%d" % i, [128, W], F32) for i in range(2)]
            us = [kb.sb("u%d" % i, [128, W], F32) for i in range(2)]
            uc = kb.sb("uc", [128, 8, T], F32)
            ucq = kb.sb("ucq", [128, 8, T], F32)
            mean = kb.sb("mean", [128, T], F32)
            var = kb.sb("var", [128, T], F32)
            sT = kb.sb("sT", [128, 8, T], BF16)
            hns = [kb.sb("hn%d" % i, [128, 8, T], F32) for i in range(2)]
            fbfs = [kb.sb("fbf%d" % i, [128, 8, T], BF16) for i in range(2)]
            yt = kb.sb("yt", [128, T], F32)
            b1, dw, dwb = self.v("cfb1"), self.v("cfdw"), self.v("cfdwb")
            lng, lnb, b2 = self.v("cflng"), self.v("cflnb"), self.v("cfb2")
            for it, (s0, s1, row, t0) in enumerate(self.tiles(segs, T)):
                lo, hi = max(t0 - H, s0), min(t0 + T + H, s1)
                n = hi - lo
                off = lo - (t0 - H)
                xt, hn, fbf = xts[it % 2], hns[it % 2], fbfs[it % 2]
                kb.dma("sp", xt.a[:, :, off:off + n], src.a[:, :, lo:hi], r=[src], w=[xt])
                x3 = xt.a[:, :, off:off + n]
                self.rstd_of(x3, n, sq, P[0], rstd, [xt])
                self.modulate(x3, n, rstd, tmp, abf, self.gs.a[:, l, row, 0, :], self.mod.a[:, l, row, 0:8], [xt])
                for ch in range(8):
                    pa, pg = P[2 + ch % 2], P[4 + ch % 2]
                    sg, u = sgs[ch % 2], us[ch % 2]
                    for (pp, cc) in ((pa, ch), (pg, 8 + ch)):
                        kb.mm(pp.a[:, 0:n], [(w1.a[:, k, cc * 128:(cc + 1) * 128], abf.a[:, k, 0:n]) for k in range(8)],
                              r=[w1, abf], w=[pp])
                    kb.op("act", lambda e, sg=sg, pg=pg, ch=ch: e.activation(sg.a[:, 0:n], pg.a[:, 0:n], AF.Sigmoid,
                                                                             bias=b1[:, 8 + ch:9 + ch]), r=[pg, self.vec], w=[sg])
                    if off > 0:
                        kb.op("pool", lambda e, u=u: e.memset(u.a[:, 0:off], 0.0), w=[u])
                    if off + n < W:
                        kb.op("pool", lambda e, u=u: e.memset(u.a[:, off + n:W], 0.0), w=[u])
                    kb.op("dve", lambda e, u=u, pa=pa, sg=sg, ch=ch: e.scalar_tensor_tensor(
                        u.a[:, off:off + n], pa.a[:, 0:n], b1[:, ch:ch + 1], sg.a[:, 0:n], op0=ALU.add, op1=ALU.mult),
                        r=[pa, sg, self.vec], w=[u])
                    kb.op("dve", lambda e, u=u, ch=ch: e.tensor_scalar(
                        uc.a[:, ch, :], u.a[:, 0:T], dw[:, ch:ch + 1], dwb[:, ch:ch + 1], op0=ALU.mult, op1=ALU.add),
                        r=[u, self.vec], w=[uc])
                    for tap in range(1, 31):
                        kb.op("dve", lambda e, u=u, ch=ch, tap=tap: e.scalar_tensor_tensor(
                            uc.a[:, ch, :], u.a[:, tap:tap + T], dw[:, tap * 8 + ch:tap * 8 + ch + 1], uc.a[:, ch, :],
                            op0=ALU.mult, op1=ALU.add), r=[u, self.vec, uc], w=[uc])
                kb.op("act", lambda e: e.activation(ucq.a, uc.a, AF.Square), r=[uc], w=[ucq])
                kb.mm(P[6].a[:, 0:T], [(self.ones_f.a, uc.a[:, k, :]) for k in range(8)], r=[self.ones_f, uc], w=[P[6]])
                kb.mm(P[7].a[:, 0:T], [(self.ones_f.a, ucq.a[:, k, :]) for k in range(8)], r=[self.ones_f, ucq], w=[P[7]])
                kb.op("act", lambda e: e.activation(mean.a, P[6].a[:, 0:T], AF.Identity, scale=1.0 / D), r=[P[6]], w=[mean])
                kb.op("dve", lambda e: e.tensor_tensor(var.a, mean.a, mean.a, op=ALU.mult), r=[mean], w=[var])
                kb.op("dve", lambda e: e.scalar_tensor_tensor(var.a, P[7].a[:, 0:T], 1.0 / D, var.a,
                                                              op0=ALU.mult, op1=ALU.subtract), r=[P[7], var], w=[var])
                kb.op("act", lambda e: e.activation(var.a, var.a, AF.Sqrt, bias=LN_EPS, scale=1.0), r=[var], w=[var])
                kb.op("dve", lambda e: e.reciprocal(var.a, var.a), r=[var], w=[var])
                kb.op("dve", lambda e: e.tensor_tensor(uc.a, uc.a, mean.a.unsqueeze(1).to_broadcast([128, 8, T]),
                                                       op=ALU.subtract), r=[uc, mean], w=[uc])
                kb.op("dve", lambda e: e.tensor_tensor(uc.a, uc.a, var.a.unsqueeze(1).to_broadcast([128, 8, T]),
                                                       op=ALU.mult), r=[uc, var], w=[uc])
                for k in range(8):
                    kb.op("act", lambda e, k=k: e.activation(sT.a[:, k, :], uc.a[:, k, :], AF.Silu,
                                                             bias=lnb[:, k:k + 1], scale=lng[:, k:k + 1]),
                          r=[uc, self.vec], w=[sT])
                for dm in range(8):
                    py = P[2 + dm % 4]
                    kb.mm(py.a[:, 0:T], [(w2.a[:, k, dm * 128:(dm + 1) * 128], sT.a[:, k, :]) for k in range(8)],
                          r=[w2, sT], w=[py])
                    kb.op("dve", lambda e, py=py, dm=dm: e.tensor_scalar(
                        yt.a, py.a[:, 0:T], b2[:, dm:dm + 1], self.mod.a[:, l, row, 16 + dm:17 + dm],
                        op0=ALU.add, op1=ALU.mult), r=[py, self.vec, self.mod], w=[yt])
                    kb.op("dve", lambda e, dm=dm, hn=hn, xt=xt: e.tensor_tensor(
                        hn.a[:, dm, :], yt.a, xt.a[:, dm, H:H + T], op=ALU.add), r=[yt, xt], w=[hn])
                self.post_mixer(l, row, hn, T, dst, fdst, t0, (sq, tmp, rstd2, fbf))

    def stage_attn(self, l, src, dst, fdst, segs_lat, segs_ctx, lambda_init):
        kb = self.kb
        P = self.P
        T = 512
        nlat = segs_lat[-1][1]
        ntok = segs_ctx[-1][1]
        qTd = kb.dram("a_qT", [128, 8, nlat], BF16)
        kTd = kb.dram("a_kT", [128, 8, ntok], BF16)
        vTd = kb.dram("a_v", [ntok // 128, 128, 1024], BF16)
        oTd = kb.dram("a_oT", [128, 8, nlat], BF16)
        with kb.scope():
            wqkv = kb.sb("wqkv", [128, 8, 3072], BF16)
            self.load_w_bf(wqkv, self.inp("da_w_qkv", [128, 8, 3072]), 8, 3072)
            cosT = kb.sb("cosT", [128, NLAT], F32)
            sinT = kb.sb("sinT", [128, NLAT], F32)
            blk = kb.sb("blk", [128, 128], BF16)
            prot = kb.sb("prot", [128, 128], F32)
            kb.dma("sp", cosT.a, self.inp("ropecos", [128, NLAT]).a, r=[self.din["ropecos"]], w=[cosT])
            kb.dma("sp", sinT.a, self.inp("ropesin", [128, NLAT]).a, r=[self.din["ropesin"]], w=[sinT])
            kb.dma("pool", blk.a, self.inp("blkones", [128, 128]).a, r=[self.din["blkones"]], w=[blk])
            kb.dma("sp", prot.a, self.inp("protm", [128, 128]).a, r=[self.din["protm"]], w=[prot])
            xts = [kb.sb("xt%d" % i, [128, 8, T], F32) for i in range(2)]
            abf = kb.sb("abf", [128, 8, T], BF16)
            sq = kb.sb("sq", [128, 8, T], BF16)
            tmp = kb.sb("tmp", [128, 8, T], F32)
            rstd = kb.sb("rstd", [128, T], F32)
            sqh = kb.sb("sqh", [128, T], BF16)
            rs = kb.sb("rs", [128, T], F32)
            qn = kb.sb("qn", [128, T], F32)
            t1 = kb.sb("t1", [128, T], F32)
            t2 = kb.sb("t2", [128, T], F32)
            qks = [kb.sb("qk%d" % i, [128, 8, T], BF16) for i in range(2)]
            vsb = kb.sb("vsb", [128, 1024], BF16)
            for it, (s0, s1, row, t0) in enumerate(self.tiles(list(segs_lat) + list(segs_ctx), T)):
                n = min(T, s1 - t0)
                is_lat = t0 < nlat
                pos0 = t0 - s0
                xt = xts[it % 2]
                kb.dma("sp", xt.a[:, :, 0:n], src.a[:, :, t0:t0 + n], r=[src], w=[xt])
                x3 = xt.a[:, :, 0:n]
                self.rstd_of(x3, n, sq, P[0], rstd, [xt])
                self.modulate(x3, n, rstd, tmp, abf, self.gs.a[:, l, row, 0, :], self.mod.a[:, l, row, 0:8], [xt])
                for qi, (base, gk, dd, stage) in enumerate(((0, "qg", qTd, qks[0]), (8, "kg", kTd, qks[1]))):
                    if qi == 0 and not is_lat:
                        continue
                    for h in range(8):
                        pq, pss, pr = P[2 + h % 2], P[4 + h % 2], P[6 + h % 2]
                        cc = base + h
                        kb.mm(pq.a[:, 0:n], [(wqkv.a[:, k, cc * 128:(cc + 1) * 128], abf.a[:, k, 0:n]) for k in range(8)],
                              r=[wqkv, abf], w=[pq])
                        kb.op("act", lambda e, pq=pq: e.activation(sqh.a[:, 0:n], pq.a[:, 0:n], AF.Square), r=[pq], w=[sqh])
                        kb.mm(pss.a[:, 0:n], [(blk.a, sqh.a[:, 0:n])], r=[blk, sqh], w=[pss])
                        kb.op("act", lambda e, pss=pss: e.activation(rs.a[:, 0:n], pss.a[:, 0:n], AF.Sqrt, bias=RMS_EPS,
                                                                     scale=1.0 / 64), r=[pss], w=[rs])
                        kb.op("dve", lambda e: e.reciprocal(rs.a[:, 0:n], rs.a[:, 0:n]), r=[rs], w=[rs])
                        kb.op("dve", lambda e, pq=pq, gk=gk: e.scalar_tensor_tensor(
                            qn.a[:, 0:n], pq.a[:, 0:n], self.v(gk), rs.a[:, 0:n], op0=ALU.mult, op1=ALU.mult),
                            r=[pq, rs, self.vec], w=[qn])
                        if is_lat:
                            kb.mm(pr.a[:, 0:n], [(prot.a, qn.a[:, 0:n])], r=[prot, qn], w=[pr])
                            kb.op("pool", lambda e: e.tensor_tensor(t1.a[:, 0:n], qn.a[:, 0:n], cosT.a[:, pos0:pos0 + n],
                                                                    op=ALU.mult), r=[qn, cosT], w=[t1])
                            kb.op("dve", lambda e, pr=pr: e.tensor_tensor(t2.a[:, 0:n], pr.a[:, 0:n], sinT.a[:, pos0:pos0 + n],
                                                                          op=ALU.mult), r=[pr, sinT], w=[t2])
                            kb.op("dve", lambda e, stage=stage, h=h: e.tensor_tensor(stage.a[:, h, 0:n], t1.a[:, 0:n], t2.a[:, 0:n],
                                                                                     op=ALU.add), r=[t1, t2], w=[stage])
                        else:
                            kb.op("act", lambda e, stage=stage, h=h: e.copy(stage.a[:, h, 0:n], qn.a[:, 0:n]), r=[qn], w=[stage])
                    kb.dma("sp", dd.a[:, :, t0:t0 + n], stage.a[:, :, 0:n], r=[stage], w=[dd])
                for sub in range(n // 128):
                    for half in range(2):
                        pv = P[2 + half]
                        kb.mm(pv.a, [(abf.a[:, k, sub * 128:(sub + 1) * 128],
                                      wqkv.a[:, k, 2048 + half * 512:2048 + (half + 1) * 512]) for k in range(8)],
                              r=[abf, wqkv], w=[pv])
                        if half == 0:
                            kb.op("act", lambda e, pv=pv: e.copy(vsb.a[:, 0:512], pv.a), r=[pv], w=[vsb])
                        else:
                            kb.op("dve", lambda e, pv=pv: e.tensor_copy(vsb.a[:, 512:1024], pv.a), r=[pv], w=[vsb])
                    kb.dma("sp", vTd.a[(t0 + sub * 128) // 128], vsb.a, r=[vsb], w=[vTd])
        with kb.scope():
            lam = kb.sb("lam", [128, 4], F32)
            lt = kb.sb("lt", [128, 64], F32)
            for ii, (ka, kb_) in enumerate((("lq1", "lk1"), ("lq2", "lk2"))):
                kb.op("dve", lambda e, ka=ka, kb_=kb_: e.tensor_tensor(lt.a, self.v(ka), self.v(kb_), op=ALU.mult),
                      r=[self.vec], w=[lt])
                kb.op("dve", lambda e, ii=ii: e.tensor_reduce(out=lam.a[:, ii:ii + 1], in_=lt.a, axis=AX.X, op=ALU.add),
                      r=[lt], w=[lam])
            kb.op("act", lambda e: e.activation(lam.a[:, 0:2], lam.a[:, 0:2], AF.Exp), r=[lam], w=[lam])
            kb.op("dve", lambda e: e.tensor_tensor(lam.a[:, 2:3], lam.a[:, 1:2], lam.a[:, 0:1], op=ALU.subtract), r=[lam], w=[lam])
            kb.op("dve", lambda e: e.tensor_scalar(lam.a[:, 2:3], lam.a[:, 2:3], -float(lambda_init), None, op0=ALU.add),
                  r=[lam], w=[lam])
            kb.op("dve", lambda e: e.tensor_scalar(lam.a[:, 3:4], self.v("subg"), 1.0 - float(lambda_init), None, op0=ALU.mult),
                  r=[self.vec], w=[lam])
            kts = [kb.sb("kt%d" % i, [128, NLAT + NCTX], BF16) for i in range(2)]
            vts = [kb.sb("vt%d" % i, [128, 18, 128], BF16) for i in range(2)]
            qts = [kb.sb("qt%d" % i, [128, T], BF16) for i in range(2)]
            pTs = [kb.sb("pT%d" % i, [128, T], BF16) for i in range(4)]
            rz = kb.sb("rz", [128, 2, T], F32)
            o1 = kb.sb("o1", [128, T], F32)
            o2 = kb.sb("o2", [128, T], F32)
            osq = kb.sb("osq", [128, T], BF16)
            ors = kb.sb("ors", [128, T], F32)
            ofs = [kb.sb("of%d" % i, [128, T], BF16) for i in range(2)]
            scale = 64 ** -0.5
            nit = 0
            npt = 0
            for bi, ((l0, l1, _), (c0, c1, _)) in enumerate(zip(segs_lat, segs_ctx)):
                nkl = (l1 - l0) // 128
                nkc = (c1 - c0) // 128
                nk = nkl + nkc
                for h in range(8):
                    kt, vt = kts[(bi * 8 + h) % 2], vts[(bi * 8 + h) % 2]
                    kb.dma("sp", kt.a[:, 0:l1 - l0], kTd.a[:, h, l0:l1], r=[kTd], w=[kt])
                    kb.dma("sp", kt.a[:, l1 - l0:l1 - l0 + c1 - c0], kTd.a[:, h, c0:c1], r=[kTd], w=[kt])
                    kb.dma("sp", vt.a[:, 0:nkl, :], vTd.a[l0 // 128:l1 // 128, :, h * 128:(h + 1) * 128].rearrange("c p d -> p c d"),
                           r=[vTd], w=[vt])
                    kb.dma("sp", vt.a[:, nkl:nk, :], vTd.a[c0 // 128:c1 // 128, :, h * 128:(h + 1) * 128].rearrange("c p d -> p c d"),
                           r=[vTd], w=[vt])
                    for q0 in range(l0, l1, T):
                        qt = qts[nit % 2]
                        of = ofs[nit % 2]
                        nit += 1
                        kb.dma("sp", qt.a, qTd.a[:, h, q0:q0 + T], r=[qTd], w=[qt])
                        for kc in range(nk):
                            for comp in range(2):
                                pS = P[4 + npt % 4]
                                pT = pTs[npt % 4]
                                npt += 1
                                kb.mm(pS.a, [(kt.a[comp * 64:(comp + 1) * 64, kc * 128:(kc + 1) * 128],
                                              qt.a[comp * 64:(comp + 1) * 64, :])], r=[kt, qt], w=[pS])
                                kb.op("act", lambda e, pS=pS, pT=pT: e.activation(pT.a, pS.a, AF.Exp, scale=scale), r=[pS], w=[pT])
                                kb.mm(P[comp].a, [(vt.a[:, kc, :], pT.a)], r=[vt, pT], w=[P[comp]],
                                      start=(kc == 0), stop=(kc == nk - 1))
                                kb.mm(P[2 + comp].a, [(self.ones_bf.a, pT.a)], r=[self.ones_bf, pT], w=[P[2 + comp]],
                                      start=(kc == 0), stop=(kc == nk - 1))
                        for comp in range(2):
                            kb.op("dve", lambda e, comp=comp: e.reciprocal(rz.a[:, comp, :], P[2 + comp].a), r=[P[2 + comp]], w=[rz])
                        kb.op("dve", lambda e: e.tensor_tensor(o1.a, P[0].a, rz.a[:, 0, :], op=ALU.mult), r=[P[0], rz], w=[o1])
                        kb.op("dve", lambda e: e.tensor_tensor(o2.a, P[1].a, rz.a[:, 1, :], op=ALU.mult), r=[P[1], rz], w=[o2])
                        kb.op("dve", lambda e: e.scalar_tensor_tensor(o1.a, o2.a, lam.a[:, 2:3], o1.a, op0=ALU.mult, op1=ALU.add),
                              r=[o1, o2, lam], w=[o1])
                        kb.op("act", lambda e: e.activation(osq.a, o1.a, AF.Square), r=[o1], w=[osq])
                        pss = P[4 + npt % 4]
                        npt += 1
                        kb.mm(pss.a, [(self.ones_bf.a, osq.a)], r=[self.ones_bf, osq], w=[pss])
                        kb.op("act", lambda e, pss=pss: e.activation(ors.a, pss.a, AF.Sqrt, bias=RMS_EPS, scale=1.0 / 128),
                              r=[pss], w=[ors])
                        kb.op("dve", lambda e: e.reciprocal(ors.a, ors.a), r=[ors], w=[ors])
                        kb.op("dve", lambda e, of=of: e.scalar_tensor_tensor(of.a, o1.a, lam.a[:, 3:4], ors.a, op0=ALU.mult, op1=ALU.mult),
                              r=[o1, lam, ors], w=[of])
                        kb.dma("sp", oTd.a[:, h, q0:q0 + T], of.a, r=[of], w=[oTd])
        with kb.scope():
            wo = kb.sb("wo", [128, 8, 1024], BF16)
            self.load_w_bf(wo, self.inp("da_w_o", [128, 8, 1024]), 8, 1024)
            T3 = 256
            xts = [kb.sb("xt%d" % i, [128, 8, T3], F32) for i in range(2)]
            ots = [kb.sb("ot%d" % i, [128, 8, T3], BF16) for i in range(2)]
            hns = [kb.sb("hn%d" % i, [128, 8, T3], F32) for i in range(2)]
            fbfs = [kb.sb("fbf%d" % i, [128, 8, T3], BF16) for i in range(2)]
            sq = kb.sb("sq", [128, 8, T3], BF16)
            tmp = kb.sb("tmp", [128, 8, T3], F32)
            rstd2 = kb.sb("rstd2", [128, T3], F32)
            for it, (s0, s1, row, t0) in enumerate(self.tiles(segs_lat, T3)):
                xt, ot, hn, fbf = xts[it % 2], ots[it % 2], hns[it % 2], fbfs[it % 2]
                kb.dma("sp", xt.a, src.a[:, :, t0:t0 + T3], r=[src], w=[xt])
                kb.dma("sp", ot.a, oTd.a[:, :, t0:t0 + T3], r=[oTd], w=[ot])
                for dm in range(8):
                    py = P[2 + dm % 6]
                    kb.mm(py.a[:, 0:T3], [(wo.a[:, k, dm * 128:(dm + 1) * 128], ot.a[:, k, :]) for k in range(8)],
                          r=[wo, ot], w=[py])
                    kb.op("dve", lambda e, py=py, dm=dm, hn=hn, xt=xt: e.scalar_tensor_tensor(
                        hn.a[:, dm, :], py.a[:, 0:T3], self.mod.a[:, l, row, 16 + dm:17 + dm], xt.a[:, dm, :],
                        op0=ALU.mult, op1=ALU.add), r=[py, self.mod, xt], w=[hn])
                self.post_mixer(l, row, hn, T3, dst, fdst, t0, (sq, tmp, rstd2, fbf))

    def peer_convert(self, l):
        kb = self.kb
        uT = self.inp("peer_uT%d" % l, [128, 8, PEER_E])
        vv = self.inp("peer_v%d" % l, [128, 128, 1024])
        ubf = kb.dram("ubf%d" % l, [128, 8, PEER_E], BF16)
        vbf = kb.dram("vbf%d" % l, [128, 128, 1024], BF16)
        for k in range(8):
            for c in range(0, PEER_E, 4096):
                kb.dma("pool", ubf.a[:, k, c:c + 4096], uT.a[:, k, c:c + 4096], r=[uT], w=[ubf])
        for i0 in range(0, 128, 4):
            kb.dma("pool", vbf.a[:, i0:i0 + 4, :], vv.a[:, i0:i0 + 4, :], r=[vv], w=[vbf])
        return ubf, vbf

    def stage_peer_q(self, l, fsrc, qTd, t_lo, t_hi):
        kb = self.kb
        T = 512
        with kb.scope():
            wq = kb.sb("wq", [128, 8, 2048], BF16)
            self.load_w_bf(wq, self.inp("peer_wq%d" % l, [128, 8, 2048]), 8, 2048)
            fts = [kb.sb("ft%d" % i, [128, 8, T], BF16) for i in range(2)]
            qts = [kb.sb("qt%d" % i, [128, 16, T], BF16) for i in range(2)]
            for it, t0 in enumerate(range(t_lo, t_hi, T)):
                n = min(T, t_hi - t0)
                ft, qt = fts[it % 2], qts[it % 2]
                kb.dma("sp", ft.a[:, :, 0:n], fsrc.a[:, :, t0:t0 + n], r=[fsrc], w=[ft])
                for c in range(16):
                    pq = self.P[c % 8]
                    kb.mm(pq.a[:, 0:n], [(wq.a[:, k, c * 128:(c + 1) * 128], ft.a[:, k, 0:n]) for k in range(8)],
                          r=[wq, ft], w=[pq])
                    en = "act" if c % 2 == 0 else "dve"
                    if en == "act":
                        kb.op("act", lambda e, c=c, pq=pq, qt=qt: e.copy(qt.a[:, c, 0:n], pq.a[:, 0:n]), r=[pq], w=[qt])
                    else:
                        kb.op("dve", lambda e, c=c, pq=pq, qt=qt: e.tensor_copy(qt.a[:, c, 0:n], pq.a[:, 0:n]), r=[pq], w=[qt])
                kb.dma("sp", qTd.a[:, :, t0:t0 + n], qt.a[:, :, 0:n], r=[qt], w=[qTd])

    def stage_peer_route(self, l, qTd, Wd, t_lo, t_hi):
        kb = self.kb
        P = self.P
        NEG = -1.0e30
        with kb.scope():
            KT = kb.sb("KT", [128, 16, 128], BF16)
            kb.dma("pool", KT.a, self.inp("peer_kT%d" % l, [128, 16, 128]).a, r=[self.din["peer_kT%d" % l]], w=[KT])
            ident = kb.sb("ident", [128, 128], BF16)
            if "ident" not in self.din:
                self.inp("ident", [128, 128])
            kb.dma("pool", ident.a, self.din["ident"].a, r=[self.din["ident"]], w=[ident])
            qts = [kb.sb("rq%d" % i, [128, 16, 128], BF16) for i in range(2)]
            S = kb.sb("S", [128, 16, 128], F32)
            Sx = kb.sb("Sx", [128, 256], F32)
            M = kb.sb("M", [128, 16, 16], F32)
            cand = kb.sb("cand", [128, 8, 256], F32)
            C16 = kb.sb("C16", [128, 8, 16], F32)
            sm = kb.sb("sm", [128, 8, 16], F32)
            Zs = kb.sb("Zs", [128, 8], F32)
            thr = kb.sb("thr", [128, 8], F32)
            E1 = kb.sb("E1", [128, 8, 16], F32)
            cc = kb.sb("cc", [128, 8, 16], F32)
            E2 = kb.sb("E2", [128, 8, 128], F32)
            tms = [kb.sb("tm%d" % i, [128, 128], F32) for i in range(4)]
            R = kb.sb("R", [128, 128, 128], BF16)
            OH = kb.sb("OH", [128, 128, 128], BF16)
            RT = kb.sb("RT", [128, 128, 128], BF16)
            OHT = kb.sb("OHT", [128, 128, 128], BF16)
            S4 = S.a.rearrange("p (h two) n -> p h two n", two=2)
            M4 = M.a.rearrange("p (h two) n -> p h two n", two=2)
            for it, t0 in enumerate(range(t_lo, t_hi, 128)):
                g = t0 // 128
                qt = qts[it % 2]
                kb.dma("sp", qt.a, qTd.a[:, :, t0:t0 + 128], r=[qTd], w=[qt])
                for b in range(4):
                    for cI in range(4):
                        c = b * 4 + cI
                        kb.mm(P[b].a[:, cI * 128:(cI + 1) * 128], [(qt.a[:, c, :], KT.a[:, c, :])], r=[qt, KT], w=[P[b]])
                    kb.op("act", lambda e, b=b: e.copy(S.a[:, b * 4:(b + 1) * 4, :].rearrange("p c n -> p (c n)"), P[b].a),
                          r=[P[b]], w=[S])
                for c in range(16):
                    kb.op("dve", lambda e, c=c: e.max(out=M.a[:, c, 0:8], in_=S.a[:, c, :]), r=[S], w=[M])
                    kb.op("dve", lambda e, c=c: e.match_replace(out=Sx.a[:, 0:128], in_to_replace=M.a[:, c, 0:8],
                                                                in_values=S.a[:, c, :], imm_value=NEG), r=[S, M], w=[Sx])
                    kb.op("dve", lambda e, c=c: e.max(out=M.a[:, c, 8:16], in_=Sx.a[:, 0:128]), r=[Sx], w=[M])
                for h in range(8):
                    kb.op("pool", lambda e, h=h: e.tensor_tensor(
                        cand.a[:, h, :].rearrange("p (a b) -> p a b", a=16),
                        M.a[:, 2 * h, :].unsqueeze(2).to_broadcast([128, 16, 16]),
                        M.a[:, 2 * h + 1, :].unsqueeze(1).to_broadcast([128, 16, 16]), op=ALU.add), r=[M], w=[cand])
                for h in range(8):
                    kb.op("dve", lambda e, h=h: e.max(out=C16.a[:, h, 0:8], in_=cand.a[:, h, :]), r=[cand], w=[C16])
                    kb.op("dve", lambda e, h=h: e.match_replace(out=Sx.a, in_to_replace=C16.a[:, h, 0:8],
                                                                in_values=cand.a[:, h, :], imm_value=NEG), r=[cand, C16], w=[Sx])
                    kb.op("dve", lambda e, h=h: e.max(out=C16.a[:, h, 8:16], in_=Sx.a), r=[Sx], w=[C16])
                kb.op("dve", lambda e: e.tensor_tensor(sm.a, C16.a, C16.a[:, :, 0:1].to_broadcast([128, 8, 16]),
                                                       op=ALU.subtract), r=[C16], w=[sm])
                kb.op("act", lambda e: e.activation(sm.a, sm.a, AF.Exp), r=[sm], w=[sm])
                kb.op("dve", lambda e: e.tensor_reduce(out=Zs.a, in_=sm.a, axis=AX.X, op=ALU.add), r=[sm], w=[Zs])
                kb.op("dve", lambda e: e.reciprocal(Zs.a, Zs.a), r=[Zs], w=[Zs])
                kb.op("dve", lambda e: e.scalar_tensor_tensor(thr.a, C16.a[:, :, 15], -1.0, C16.a[:, :, 0],
                                                              op0=ALU.mult, op1=ALU.max), r=[C16], w=[thr])
                kb.op("dve", lambda e: e.scalar_tensor_tensor(thr.a, thr.a, -2.0e-5, C16.a[:, :, 15],
                                                              op0=ALU.mult, op1=ALU.add), r=[thr, C16], w=[thr])
                kb.op("dve", lambda e: e.tensor_tensor(E1.a, M4[:, :, 0, :], M4[:, :, 0, 0:1].to_broadcast([128, 8, 16]),
                                                       op=ALU.subtract), r=[M], w=[E1])
                kb.op("act", lambda e: e.activation(E1.a, E1.a, AF.Exp), r=[E1], w=[E1])
                kb.op("dve", lambda e: e.tensor_tensor(E1.a, E1.a, Zs.a.unsqueeze(2).to_broadcast([128, 8, 16]),
                                                       op=ALU.mult), r=[E1, Zs], w=[E1])
                kb.op("dve", lambda e: e.scalar_tensor_tensor(cc.a, M4[:, :, 0, :], -1.0,
                                                              thr.a.unsqueeze(2).to_broadcast([128, 8, 16]),
                                                              op0=ALU.mult, op1=ALU.add), r=[M, thr], w=[cc])
                kb.op("dve", lambda e: e.tensor_tensor(E2.a, S4[:, :, 1, :], M4[:, :, 1, 0:1].to_broadcast([128, 8, 128]),
                                                       op=ALU.subtract), r=[S, M], w=[E2])
                kb.op("act", lambda e: e.activation(E2.a, E2.a, AF.Exp), r=[E2], w=[E2])
                for h in range(8):
                    kb.op("dve", lambda e, h=h: e.tensor_tensor(
                        OH.a[:, h * 16:(h + 1) * 16, :],
                        S.a[:, 2 * h, :].unsqueeze(1).to_broadcast([128, 16, 128]),
                        M.a[:, 2 * h, :].unsqueeze(2).to_broadcast([128, 16, 128]), op=ALU.is_equal), r=[S, M], w=[OH])
                    for r in range(16):
                        c = h * 16 + r
                        tm = tms[c % 4]
                        kb.op("dve", lambda e, h=h, r=r, tm=tm: e.scalar_tensor_tensor(
                            tm.a, S.a[:, 2 * h + 1, :], cc.a[:, h, r:r + 1], E2.a[:, h, :], op0=ALU.is_ge, op1=ALU.mult),
                            r=[S, cc, E2], w=[tm])
                        kb.op("act", lambda e, h=h, r=r, c=c, tm=tm: e.activation(
                            R.a[:, c, :], tm.a, AF.Identity, scale=E1.a[:, h, r:r + 1]), r=[tm, E1], w=[R])
                nb = 0
                for (srcb, dstb) in ((R, RT), (OH, OHT)):
                    for j0 in range(0, 128, 4):
                        pb = P[nb % 8]
                        nb += 1
                        for jj in range(4):
                            kb.mm(pb.a[:, jj * 128:(jj + 1) * 128], [(srcb.a[:, :, j0 + jj], ident.a)],
                                  r=[srcb, ident], w=[pb])
                        dv = dstb.a[:, j0:j0 + 4, :].rearrange("p j t -> p (j t)")
                        if nb % 2 == 0:
                            kb.op("act", lambda e, dv=dv, pb=pb: e.copy(dv, pb.a), r=[pb], w=[dstb])
                        else:
                            kb.op("dve", lambda e, dv=dv, pb=pb: e.tensor_copy(dv, pb.a), r=[pb], w=[dstb])
                Wv = R
                for tq in range(0, 128, 4):
                    pb = P[nb % 8]
                    nb += 1
                    for tt in range(4):
                        kb.mm(pb.a[:, tt * 128:(tt + 1) * 128], [(RT.a[:, :, tq + tt], OHT.a[:, :, tq + tt])],
                              r=[RT, OHT], w=[pb])
                    dv = Wv.a[:, :, tq:tq + 4].rearrange("p i t -> p t i")
                    sv = pb.a.rearrange("p (t i) -> p t i", t=4)
                    if nb % 2 == 0:
                        kb.op("act", lambda e, dv=dv, sv=sv: e.copy(dv, sv), r=[pb], w=[Wv])
                    else:
                        kb.op("dve", lambda e, dv=dv, sv=sv: e.tensor_copy(dv, sv), r=[pb], w=[Wv])
                kb.dma("sp", Wd.a[g], Wv.a, r=[Wv], w=[Wd])

    def stage_peer_experts(self, l, ubf, vbf, fsrc, hsrc, hdst, Wd, segs, dst_off=0):
        kb = self.kb
        P = self.P
        T = 256
        NBG = 4
        with kb.scope():
            uts = [kb.sb("ut%d" % i, [128, 8, NBG * 128], BF16) for i in range(3)]
            vts = [kb.sb("vt%d" % i, [128, NBG, 1024], BF16) for i in range(3)]
            wts = [kb.sb("wt%d" % i, [128, 2, NBG, 128], BF16) for i in range(3)]
            fts = [kb.sb("eft%d" % i, [128, 8, T], BF16) for i in range(2)]
            hts = [kb.sb("eht%d" % i, [128, 8, T], F32) for i in range(2)]
            gzs = [kb.sb("gz%d" % i, [128, T], F32) for i in range(2)]
            As = [kb.sb("A%d" % i, [128, T], BF16) for i in range(3)]
            nslot = 0
            nz = 0
            for ig, (s0, s1, row, t0) in enumerate(self.tiles(segs, T)):
                ft, ht = fts[ig % 2], hts[ig % 2]
                kb.dma("sp", ft.a, fsrc.a[:, :, t0:t0 + T], r=[fsrc], w=[ft])
                kb.dma("sp", ht.a, hsrc.a[:, :, t0:t0 + T], r=[hsrc], w=[ht])
                g0 = t0 // 128
                for bg in range(128 // NBG):
                    ut, vt, wt = uts[nslot % 3], vts[nslot % 3], wts[nslot % 3]
                    nslot += 1
                    kb.dma("sp", ut.a, ubf.a[:, :, bg * NBG * 128:(bg + 1) * NBG * 128], r=[ubf], w=[ut])
                    kb.dma("sp", vt.a, vbf.a[:, bg * NBG:(bg + 1) * NBG, :], r=[vbf], w=[vt])
                    for tl in range(2):
                        kb.dma("sp", wt.a[:, tl, :, :], Wd.a[g0 + tl][:, bg * NBG:(bg + 1) * NBG, :], r=[Wd], w=[wt])
                    for b in range(NBG):
                        i = bg * NBG + b
                        pz = P[4 + nz % 4]
                        gz, A = gzs[nz % 2], As[nz % 3]
                        nz += 1
                        kb.mm(pz.a[:, 0:T], [(ut.a[:, k, b * 128:(b + 1) * 128], ft.a[:, k, :]) for k in range(8)],
                              r=[ut, ft], w=[pz])
                        kb.op("act", lambda e, gz=gz, pz=pz: e.activation(gz.a, pz.a[:, 0:T], AF.Gelu), r=[pz], w=[gz])
                        kb.op("dve", lambda e, gz=gz, A=A, wt=wt, b=b: e.tensor_tensor(
                            A.a.rearrange("p (a t) -> p a t", a=2), gz.a.rearrange("p (a t) -> p a t", a=2),
                            wt.a[:, :, b, :], op=ALU.mult), r=[gz, wt], w=[A])
                        for dm in range(8):
                            po = P[dm // 2]
                            kb.mm(po.a[:, (dm % 2) * T:(dm % 2 + 1) * T], [(vt.a[:, b, dm * 128:(dm + 1) * 128], A.a)],
                                  r=[vt, A], w=[po], start=(i == 0), stop=(i == 127))
                for dm in range(8):
                    po = P[dm // 2]
                    kb.op("dve", lambda e, dm=dm, po=po, ht=ht: e.scalar_tensor_tensor(
                        ht.a[:, dm, :], po.a[:, (dm % 2) * T:(dm % 2 + 1) * T], self.mod.a[:, l, row, 40 + dm:41 + dm],
                        ht.a[:, dm, :], op0=ALU.mult, op1=ALU.add), r=[po, self.mod, ht], w=[ht])
                kb.dma("sp", hdst.a[:, :, t0 - dst_off:t0 - dst_off + T], ht.a, r=[ht], w=[hdst])

def kmaj(w):
    w = np.asarray(w, np.float32)
    nk = w.shape[0] // 128
    return np.ascontiguousarray(w.reshape(nk, 128, w.shape[1]).transpose(1, 0, 2))


def tok_to_T(X):
    X = np.asarray(X)
    return np.ascontiguousarray(X.T.reshape(8, 128, X.shape[0]).transpose(1, 0, 2))


def T_to_tok(hT):
    return np.ascontiguousarray(hT.transpose(1, 0, 2).reshape(1024, hT.shape[2]).T)


def build_vecs(inp, crows):
    vp = VecPack()
    cT = np.stack([packv(crows[r]) for r in range(3)], axis=2)
    vp.add("cT", cT.reshape(128, 24))
    for l in range(DEPTH):
        vp.add("bmod%d" % l, packv(inp["b_mod"][l]))
        vp.add("n1g%d" % l, packv(inp["norm1_g"][l]))
        vp.add("n2g%d" % l, packv(inp["norm2_g"][l]))
    for j in range(inp["sc_conv_w"].shape[0]):
        cw = inp["sc_conv_w"][j]
        vp.add("scw%d" % j, np.concatenate([packv(cw[k]) for k in range(3)], axis=1))
    if "cf_b_pw1" in inp:
        vp.add("cfb1", packv(inp["cf_b_pw1"][0]))
        dw = inp["cf_dw_w"][0]
        vp.add("cfdw", np.concatenate([packv(dw[k]) for k in range(dw.shape[0])], axis=1))
        for key, nm in (("cf_dw_b", "cfdwb"), ("cf_ln_g", "cflng"), ("cf_ln_b", "cflnb"), ("cf_b_pw2", "cfb2")):
            vp.add(nm, packv(inp[key][0]))
    if "da_q_norm_g" in inp:
        rep = lambda v: np.ascontiguousarray(np.broadcast_to(np.asarray(v, np.float32)[None, :], (128, len(v))))
        vp.add("qg", np.tile(np.asarray(inp["da_q_norm_g"][0], np.float32), 2).reshape(128, 1))
        vp.add("kg", np.tile(np.asarray(inp["da_k_norm_g"][0], np.float32), 2).reshape(128, 1))
        vp.add("subg", np.asarray(inp["da_subln_g"][0], np.float32).reshape(128, 1))
        for key, nm in (("da_lam_q1", "lq1"), ("da_lam_k1", "lk1"), ("da_lam_q2", "lq2"), ("da_lam_k2", "lk2")):
            vp.add(nm, rep(inp[key][0]))
    return vp


def host_consts():
    c = {}
    c["ident"] = np.eye(128, dtype=np.float32)
    blk = np.zeros((128, 128), np.float32)
    blk[:64, :64] = 1.0
    blk[64:, 64:] = 1.0
    c["blkones"] = blk
    prot = np.zeros((128, 128), np.float32)
    cosT = np.zeros((128, NLAT), np.float32)
    sinT = np.zeros((128, NLAT), np.float32)
    t = np.arange(NLAT)
    pos = (np.floor_divide(t, 64).astype(np.float32), np.mod(t, 64).astype(np.float32))
    nf = 16
    inv_freq = (10000.0 ** (-np.arange(nf, dtype=np.float32) / nf)).astype(np.float32)
    for p in range(128):
        dh = p % 64
        axis, half, f = dh // 32, (dh % 32) // 16, dh % 16
        ang = pos[axis] * inv_freq[f]
        cosT[p] = np.cos(ang)
        sinT[p] = np.sin(ang)
        if half == 0:
            prot[p + 16, p] = -1.0
        else:
            prot[p - 16, p] = 1.0
    c["protm"] = prot
    c["ropecos"] = cosT
    c["ropesin"] = sinT
    return c


SEGS_LAT = [(0, NLAT, 0), (NLAT, 2 * NLAT, 1)]
SEGS_CTX = [(2 * NLAT, 2 * NLAT + NCTX, 2), (2 * NLAT + NCTX, 2 * NLAT + 2 * NCTX, 2)]
NTOK = 2 * NLAT + 2 * NCTX


def build_program(voff, nv):
    pg = Prog(SEGS_LAT, SEGS_CTX, voff, nv)
    kb = pg.kb
    pg.prologue()
    pg.stage_mod(range(DEPTH))
    xT = pg.inp("xT", [128, 8, NTOK])
    hX = kb.dram("hX", [128, 8, NTOK], F32)
    hM = kb.dram("hM", [128, 8, NTOK], F32)
    fT = kb.dram("fT", [128, 8, NTOK], BF16)
    qTd = kb.dram("p_qT", [128, 16, NTOK], BF16)
    Wd = kb.dram("p_W", [NTOK // 128, 128, 128, 128], BF16)
    yT = kb.dram("yT", [128, 8, 2 * NLAT], F32, kind="ExternalOutput")
    tabs = {0: pg.peer_convert(0)}
    for l in range(DEPTH):
        src = xT if l == 0 else hX
        kind = l % 3
        if kind == 0:
            segs = SEGS_LAT + (SEGS_CTX if l == 0 else [])
            pg.stage_sconv(l, src, hM, fT, segs)
        elif kind == 1:
            pg.stage_attn(l, src, hM, fT, SEGS_LAT, SEGS_CTX, 0.8 - 0.6 * math.exp(-0.3 * l))
        else:
            pg.stage_conformer(l, src, hM, fT, SEGS_LAT)
        if l + 1 < DEPTH:
            tabs[l + 1] = pg.peer_convert(l + 1)
        psegs = SEGS_LAT + (SEGS_CTX if l == 0 else [])
        t_hi = psegs[-1][1]
        pg.stage_peer_q(l, fT, qTd, 0, t_hi)
        pg.stage_peer_route(l, qTd, Wd, 0, t_hi)
        ubf, vbf = tabs[l]
        pg.stage_peer_experts(l, ubf, vbf, fT, hM, yT if l == DEPTH - 1 else hX, Wd, psegs)
    kb.finish([yT])
    return pg


def kernel(**inputs):
    inp = {k: np.asarray(v) for k, v in inputs.items()}
    ncore = 8
    consts = host_consts()
    shared = dict(consts)
    for l in range(DEPTH):
        shared["w_mod%d" % l] = kmaj(inp["w_mod"][l])
        shared["peer_uT%d" % l] = kmaj(inp["peer_u"][l].T)
        shared["peer_v%d" % l] = np.ascontiguousarray(inp["peer_v"][l].reshape(128, 128, D).transpose(1, 0, 2))
        shared["peer_wq%d" % l] = kmaj(inp["peer_w_query"][l])
        shared["peer_kT%d" % l] = np.ascontiguousarray(inp["peer_sub_keys"][l].reshape(16, 128, 128).transpose(2, 0, 1))
    for j in range(inp["sc_w_in"].shape[0]):
        shared["sc_w_in%d" % j] = kmaj(inp["sc_w_in"][j])
        shared["sc_w_out%d" % j] = kmaj(inp["sc_w_out"][j])
    shared["da_w_qkv"] = kmaj(inp["da_w_qkv"][0])
    shared["da_w_o"] = kmaj(inp["da_w_o"][0])
    shared["cf_w_pw1"] = kmaj(inp["cf_w_pw1"][0])
    shared["cf_w_pw2"] = kmaj(inp["cf_w_pw2"][0])
    in_maps = []
    pg = None
    for c in range(ncore):
        b0, b1 = 2 * c, 2 * c + 1
        crows = np.stack([inp["c"][b0], inp["c"][b1], inp["c_ctx"]], axis=0)
        vp = build_vecs(inp, crows)
        if pg is None:
            pg = build_program(vp.off, vp.n)
        X = np.concatenate([inp["x"][b0], inp["x"][b1], inp["ctx"][b0], inp["ctx"][b1]], axis=0)
        m = dict(shared)
        m["vecs"] = vp.array()
        m["xT"] = tok_to_T(X)
        in_maps.append({k: m[k] for k in pg.din})
    res = run_bass_kernel_spmd(pg.kb.nc, in_maps, core_ids=list(range(ncore)))
    out = np.empty((2 * ncore, NLAT, D), np.float32)
    for c in range(ncore):
        Y = T_to_tok(np.asarray(res.results[c]["yT"]))
        out[2 * c] = Y[:NLAT]
        out[2 * c + 1] = Y[NLAT:]
    return out
```

```python
import contextlib
import math
import numpy as np
import concourse.bass as bass
import concourse.mybir as mybir
from concourse.bass_utils import run_bass_kernel_spmd

F32 = mybir.dt.float32
BF16 = mybir.dt.bfloat16
ALU = mybir.AluOpType
AF = mybir.ActivationFunctionType
AX = mybir.AxisListType

D = 1024
KC = 8
NLAT = 2048
NCTX = 256
NB = 2
DEPTH = 4
RMS_EPS = 1e-6
LN_EPS = 1e-5
PEER_E = 16384


class Buf:
    def __init__(self, name, a=None, space="sb"):
        self.name = name
        self.a = a
        self.space = space
        self.w = {}
        self.r = {}
        self.sem = None
        self.cnt = 0


class KB:
    def __init__(self):
        self.nc = bass.Bass("TRN2", target_bir_lowering=False)
        self.es = contextlib.ExitStack()
        nc = self.nc
        self.sems = []
        self.eng = {}
        for nm, e in (("pe", nc.tensor), ("act", nc.scalar), ("dve", nc.vector),
                      ("pool", nc.gpsimd), ("sp", nc.sync)):
            self.eng[nm] = dict(e=e, sem=self.newsem("s_" + nm), cnt=0, waited={})
        self.nuniq = 0
        self.dcount = {}
        self.freed = []
        self.stack = [self.es]
        self.scoped = []

    def newsem(self, name):
        h = self.es.enter_context(self.nc.semaphore(name))
        self.sems.append(h)
        return len(self.sems) - 1

    def barrier(self):
        deps = {}
        for E in self.eng.values():
            deps[E["sem"]] = E["cnt"]
        deps.update(self.dcount)
        for en in self.eng:
            self._wait(en, {k: v for k, v in deps.items() if v > 0})

    @contextlib.contextmanager
    def scope(self):
        st = contextlib.ExitStack()
        self.stack.append(st)
        self.scoped.append([])
        try:
            yield
        finally:
            self.barrier()
            for b in self.scoped.pop():
                if b.sem is not None:
                    self.freed.append(b.sem)
                    b.sem = None
            self.stack.pop()
            st.close()

    def sb(self, name, shape, dt):
        self.nuniq += 1
        t = self.stack[-1].enter_context(self.nc.sbuf_tensor("%s_%d" % (name, self.nuniq), list(shape), dt))
        b = Buf(name, t[:])
        if self.scoped:
            self.scoped[-1].append(b)
        return b

    def ps(self, name, shape, dt=F32):
        t = self.es.enter_context(self.nc.psum_tensor(name, list(shape), dt))
        return Buf(name, t[:])

    def dram(self, name, shape, dt, kind="Internal"):
        t = self.nc.dram_tensor(name, list(shape), dt, kind=kind)
        return Buf(name, t.ap(), space="dram")

    def _deps(self, r, w):
        deps = {}
        for b in r:
            for k, v in b.w.items():
                if deps.get(k, 0) < v:
                    deps[k] = v
        for b in w:
            for dd in (b.w, b.r):
                for k, v in dd.items():
                    if deps.get(k, 0) < v:
                        deps[k] = v
        return deps

    def _wait(self, en, deps):
        E = self.eng[en]
        for k, v in deps.items():
            if E["waited"].get(k, 0) >= v:
                continue
            E["e"].wait_ge(self.sems[k], v)
            E["waited"][k] = v

    def _done(self, tok, r, w):
        k, v = tok
        for b in r:
            if b.r.get(k, 0) < v:
                b.r[k] = v
        for b in w:
            if b.w.get(k, 0) < v:
                b.w[k] = v
            b.r = {}

    def op(self, en, fn, r=(), w=()):
        deps = self._deps(r, w)
        if en == "pe":
            deps.pop(self.eng["pe"]["sem"], None)
        self._wait(en, deps)
        E = self.eng[en]
        ins = fn(E["e"])
        E["cnt"] += 1
        ins.then_inc(self.sems[E["sem"]], 1)
        self._done((E["sem"], E["cnt"]), r, w)

    def dma(self, q, out, in_, r=(), w=()):
        self._wait(q, self._deps(r, w))
        E = self.eng[q]
        ins = E["e"].dma_start(out=out, in_=in_)
        d = next((b for b in list(w) + list(r) if b.space == "sb"), w[0])
        if d.sem is None:
            if self.freed:
                d.sem = self.freed.pop()
            else:
                d.sem = self.newsem("d%d" % len(self.sems))
        c = self.dcount.get(d.sem, 0) + 16
        self.dcount[d.sem] = c
        ins.then_inc(self.sems[d.sem], 16)
        self._done((d.sem, c), r, w)

    def mm(self, out, pairs, r=(), w=(), start=True, stop=True):
        def fn(pe):
            n = len(pairs)
            ins = None
            for i, (l, rh) in enumerate(pairs):
                ins = pe.matmul(out, lhsT=l, rhs=rh, start=(start and i == 0), stop=(stop and i == n - 1))
            return ins
        self.op("pe", fn, r, w)

    def slots(self, base, n):
        out = [Buf("%s.%d" % (base.name, i), None) for i in range(n)]
        if self.scoped:
            self.scoped[-1].extend(out)
        return out

    @staticmethod
    def _merge(dst, src):
        for k, v in src.items():
            if dst.get(k, 0) < v:
                dst[k] = v

    def join(self, dst, srcs):
        for b in srcs:
            self._merge(dst.w, b.w)
            self._merge(dst.r, b.r)

    def fork(self, src, dsts):
        for d in dsts:
            self._merge(d.w, src.w)
            self._merge(d.r, src.r)

    def finish(self, outs):
        deps = {}
        for b in outs:
            for k, v in b.w.items():
                deps[k] = max(deps.get(k, 0), v)
        self._wait("sp", deps)


def packv(v):
    v = np.asarray(v, np.float32).reshape(-1, 128)
    return np.ascontiguousarray(v.T)


class VecPack:
    def __init__(self):
        self.off = {}
        self.n = 0
        self.cols = []

    def add(self, key, arr128xn):
        a = np.asarray(arr128xn, np.float32)
        assert a.shape[0] == 128
        self.off[key] = (self.n, a.shape[1])
        self.n += a.shape[1]
        self.cols.append(a)

    def array(self):
        return np.ascontiguousarray(np.concatenate(self.cols, axis=1))


class Prog:
    def __init__(self, segs_lat, segs_ctx, voff, nv, layers=range(DEPTH)):
        self.kb = KB()
        kb = self.kb
        self.segs_lat = segs_lat
        self.segs_ctx = segs_ctx
        self.ntok = (segs_ctx[-1][1] if segs_ctx else segs_lat[-1][1])
        self.voff = voff
        self.nv = nv
        self.din = {}
        self.vec = kb.sb("vec", [128, nv], F32)
        self.ones_bf = kb.sb("ones_bf", [128, 128], BF16)
        self.ones_f = kb.sb("ones_f", [128, 128], F32)
        self.P = [kb.ps("P%d" % i, [128, 512], F32) for i in range(8)]

    def inp(self, name, shape, dt=F32):
        b = self.kb.dram(name, shape, dt, kind="ExternalInput")
        self.din[name] = b
        return b

    def v(self, key, lo=0, n=None):
        o, w = self.voff[key]
        if n is None:
            n = w - lo
        return self.vec.a[:, o + lo:o + lo + n]

    def prologue(self):
        kb = self.kb
        vecd = self.inp("vecs", [128, self.nv])
        kb.dma("sp", self.vec.a, vecd.a, r=[vecd], w=[self.vec])
        kb.op("dve", lambda e: e.memset(self.ones_bf.a, 1.0), w=[self.ones_bf])
        kb.op("dve", lambda e: e.memset(self.ones_f.a, 1.0), w=[self.ones_f])
        self.mod = kb.sb("mod", [128, DEPTH, 3, 48], F32)
        self.gs = kb.sb("gs", [128, DEPTH, 3, 2, 8], F32)

    def stage_mod(self, layers):
        kb = self.kb
        with kb.scope():
            sc = kb.sb("sc", [128, 8, 4], F32)
            kb.op("act", lambda e: e.activation(sc.a[:, :, 0:3], self.v("cT").rearrange("p (k r) -> p k r", r=3),
                                                AF.Silu), r=[self.vec], w=[sc])
            wts = [kb.sb("wm%d" % i, [128, 8, 512], F32) for i in range(2)]
            it = 0
            for l in layers:
                wd = self.inp("w_mod%d" % l, [128, 8, 6144])
                pm = self.P[l % 2]
                for cb in range(12):
                    wt = wts[it % 2]
                    it += 1
                    kb.dma("sp", wt.a, wd.a[:, :, cb * 512:(cb + 1) * 512], r=[wd], w=[wt])
                    for jj in range(4):
                        j = cb * 4 + jj
                        kb.mm(pm.a[:, j * 4:j * 4 + 3],
                              [(wt.a[:, k, jj * 128:(jj + 1) * 128], sc.a[:, k, 0:3]) for k in range(8)],
                              r=[wt, sc], w=[pm])
                for r in range(3):
                    kb.op("dve", lambda e, r=r, l=l, pm=pm: e.tensor_tensor(
                        self.mod.a[:, l, r, :], pm.a[:, 0:192].rearrange("p (j r) -> p j r", r=4)[:, :, r],
                        self.v("bmod%d" % l), op=ALU.add), r=[pm, self.vec], w=[self.mod])
                    for h, (c0, gk) in enumerate(((8, "n1g%d" % l), (32, "n2g%d" % l))):
                        kb.op("dve", lambda e, r=r, l=l, h=h, c0=c0, gk=gk: e.scalar_tensor_tensor(
                            self.gs.a[:, l, r, h, :], self.mod.a[:, l, r, c0:c0 + 8], 1.0, self.v(gk),
                            op0=ALU.add, op1=ALU.mult), r=[self.mod, self.vec], w=[self.gs])

    def rstd_of(self, x3, n, sq, pss, rstd, rbufs, eps=RMS_EPS, nfeat=D, ones=None, nk=8):
        kb = self.kb
        ones = ones if ones is not None else self.ones_bf
        kb.op("act", lambda e: e.activation(sq.a[:, 0:nk, 0:n], x3, AF.Square), r=rbufs, w=[sq])
        kb.mm(pss.a[:, 0:n], [(ones.a, sq.a[:, k, 0:n]) for k in range(nk)], r=[ones, sq], w=[pss])
        kb.op("act", lambda e: e.activation(rstd.a[:, 0:n], pss.a[:, 0:n], AF.Sqrt, bias=eps, scale=1.0 / nfeat),
              r=[pss], w=[rstd])
        kb.op("dve", lambda e: e.reciprocal(rstd.a[:, 0:n], rstd.a[:, 0:n]), r=[rstd], w=[rstd])

    def modulate(self, x3, n, rstd, tmp, out, gsc, shift, rbufs):
        kb = self.kb
        kb.op("dve", lambda e: e.tensor_tensor(tmp.a[:, :, 0:n], x3,
                                               rstd.a[:, 0:n].unsqueeze(1).to_broadcast([128, 8, n]), op=ALU.mult),
              r=rbufs + [rstd], w=[tmp])
        for k in range(8):
            if k % 2 == 0:
                kb.op("act", lambda e, k=k: e.activation(out.a[:, k, 0:n], tmp.a[:, k, 0:n], AF.Identity,
                                                         bias=shift[:, k:k + 1], scale=gsc[:, k:k + 1]),
                      r=[tmp, self.mod, self.gs], w=[out])
            else:
                kb.op("pool", lambda e, k=k: e.tensor_scalar(out.a[:, k, 0:n], tmp.a[:, k, 0:n], gsc[:, k:k + 1],
                                                             shift[:, k:k + 1], op0=ALU.mult, op1=ALU.add),
                      r=[tmp, self.mod, self.gs], w=[out])

    def load_w_bf(self, dst, src, nk, ncols, step=1024):
        kb = self.kb
        for k in range(nk):
            for c0 in range(0, ncols, step):
                c1 = min(ncols, c0 + step)
                kb.dma("pool", dst.a[:, k, c0:c1], src.a[:, k, c0:c1], r=[src], w=[dst])

    def tiles(self, segs, T):
        for (s0, s1, row) in segs:
            for t0 in range(s0, s1, T):
                yield s0, s1, row, t0

    def post_mixer(self, l, row, hn, T, dst, fdst, t0, bufs):
        kb = self.kb
        sq, tmp, rstd, fbf = bufs
        self.rstd_of(hn.a[:, :, 0:T], T, sq, self.P[1], rstd, [hn])
        self.modulate(hn.a[:, :, 0:T], T, rstd, tmp, fbf, self.gs.a[:, l, row, 1, :], self.mod.a[:, l, row, 24:32], [hn])
        kb.dma("sp", dst.a[:, :, t0:t0 + T], hn.a[:, :, 0:T], r=[hn], w=[dst])
        kb.dma("sp", fdst.a[:, :, t0:t0 + T], fbf.a[:, :, 0:T], r=[fbf], w=[fdst])

    def stage_sconv(self, l, src, dst, fdst, segs):
        kb = self.kb
        T, H = 256, 1
        W = T + 2 * H
        j = l // 3
        with kb.scope():
            win = kb.sb("win", [128, 8, 3072], BF16)
            wout = kb.sb("wout", [128, 8, 1024], BF16)
            self.load_w_bf(win, self.inp("sc_w_in%d" % j, [128, 8, 3072]), 8, 3072)
            self.load_w_bf(wout, self.inp("sc_w_out%d" % j, [128, 8, 1024]), 8, 1024)
            xts = [kb.sb("xt%d" % i, [128, 8, W], F32) for i in range(2)]
            abfs = [kb.sb("abf%d" % i, [128, 8, W], BF16) for i in range(2)]
            gTs = [kb.sb("gT%d" % i, [128, 8, T], BF16) for i in range(2)]
            hns = [kb.sb("hn%d" % i, [128, 8, T], F32) for i in range(2)]
            fbfs = [kb.sb("fbf%d" % i, [128, 8, T], BF16) for i in range(2)]
            sq = kb.sb("sq", [128, 8, W], BF16)
            tmp = kb.sb("tmp", [128, 8, W], F32)
            rstd = kb.sb("rstd", [128, W], F32)
            rstd2 = kb.sb("rstd2", [128, W], F32)
            csbs = [kb.sb("csb%d" % i, [128, W], F32) for i in range(2)]
            cus = [kb.sb("cu%d" % i, [128, W], F32) for i in range(2)]
            accs = [kb.sb("acc%d" % i, [128, T], F32) for i in range(2)]
            scw = self.v("scw%d" % j)
            P = self.P
            for it, (s0, s1, row, t0) in enumerate(self.tiles(segs, T)):
                lo, hi = max(t0 - H, s0), min(t0 + T + H, s1)
                n = hi - lo
                off = lo - (t0 - H)
                xt, abf, gT, hn, fbf = xts[it % 2], abfs[it % 2], gTs[it % 2], hns[it % 2], fbfs[it % 2]
                kb.dma("sp", xt.a[:, :, off:off + n], src.a[:, :, lo:hi], r=[src], w=[xt])
                x3 = xt.a[:, :, off:off + n]
                self.rstd_of(x3, n, sq, P[0], rstd, [xt])
                self.modulate(x3, n, rstd, tmp, abf, self.gs.a[:, l, row, 0, :], self.mod.a[:, l, row, 0:8], [xt])
                for ch in range(8):
                    pc, pu, pb = P[2 + ch % 2], P[4 + ch % 2], P[6 + ch % 2]
                    csb, cu, acc = csbs[ch % 2], cus[ch % 2], accs[ch % 2]
                    for (pp, cc) in ((pc, 8 + ch), (pu, 16 + ch), (pb, ch)):
                        kb.mm(pp.a[:, 0:n], [(win.a[:, k, cc * 128:(cc + 1) * 128], abf.a[:, k, 0:n]) for k in range(8)],
                              r=[win, abf], w=[pp])
                    kb.op("act", lambda e, csb=csb, pc=pc: e.copy(csb.a[:, 0:n], pc.a[:, 0:n]), r=[pc], w=[csb])
                    if off > 0:
                        kb.op("pool", lambda e, cu=cu: e.memset(cu.a[:, 0:off], 0.0), w=[cu])
                    if off + n < W:
                        kb.op("pool", lambda e, cu=cu: e.memset(cu.a[:, off + n:W], 0.0), w=[cu])
                    kb.op("dve", lambda e, cu=cu, csb=csb, pu=pu: e.tensor_tensor(
                        cu.a[:, off:off + n], csb.a[:, 0:n], pu.a[:, 0:n], op=ALU.mult), r=[csb, pu], w=[cu])
                    kb.op("pool", lambda e, cu=cu, acc=acc: e.tensor_scalar(
                        acc.a, cu.a[:, 0:T], scw[:, ch:ch + 1], None, op0=ALU.mult), r=[cu, self.vec], w=[acc])
                    for tap in (1, 2):
                        kb.op("dve", lambda e, cu=cu, acc=acc, tap=tap: e.scalar_tensor_tensor(
                            acc.a, cu.a[:, tap:tap + T], scw[:, tap * 8 + ch:tap * 8 + ch + 1], acc.a,
                            op0=ALU.mult, op1=ALU.add), r=[cu, self.vec, acc], w=[acc])
                    c0 = H - off
                    kb.op("dve", lambda e, acc=acc, pb=pb, gT=gT, ch=ch, c0=c0: e.tensor_tensor(
                        gT.a[:, ch, :], acc.a, pb.a[:, c0:c0 + T], op=ALU.mult), r=[acc, pb], w=[gT])
                for dm in range(8):
                    py = P[2 + dm % 6]
                    kb.mm(py.a[:, 0:T], [(wout.a[:, k, dm * 128:(dm + 1) * 128], gT.a[:, k, :]) for k in range(8)],
                          r=[wout, gT], w=[py])
                    kb.op("dve", lambda e, py=py, dm=dm, hn=hn, xt=xt: e.scalar_tensor_tensor(
                        hn.a[:, dm, :], py.a[:, 0:T], self.mod.a[:, l, row, 16 + dm:17 + dm], xt.a[:, dm, H:H + T],
                        op0=ALU.mult, op1=ALU.add), r=[py, self.mod, xt], w=[hn])
                self.post_mixer(l, row, hn, T, dst, fdst, t0, (sq, tmp, rstd2, fbf))


    def stage_conformer(self, l, src, dst, fdst, segs):
        kb = self.kb
        T, H = 256, 15
        W = T + 2 * H
        P = self.P
        with kb.scope():
            w1 = kb.sb("w1", [128, 8, 2048], BF16)
            w2 = kb.sb("w2", [128, 8, 1024], BF16)
            self.load_w_bf(w1, self.inp("cf_w_pw1", [128, 8, 2048]), 8, 2048)
            self.load_w_bf(w2, self.inp("cf_w_pw2", [128, 8, 1024]), 8, 1024)
            xts = [kb.sb("xt%d" % i, [128, 8, W], F32) for i in range(2)]
            abf = kb.sb("abf", [128, 8, W], BF16)
            sq = kb.sb("sq", [128, 8, W], BF16)
            tmp = kb.sb("tmp", [128, 8, W], F32)
            rstd = kb.sb("rstd", [128, W], F32)
            rstd2 = kb.sb("rstd2", [128, W], F32)
            sgs = [kb.sb("sg%d" % i, [128, W], F32) for i in range(2)]
            us = [kb.sb("u%d" % i, [128, W], F32) for i in range(2)]
            uc = kb.sb("uc", [128, 8, T], F32)
            ucq = kb.sb("ucq", [128, 8, T], F32)
            mean = kb.sb("mean", [128, T], F32)
            var = kb.sb("var", [128, T], F32)
            sT = kb.sb("sT", [128, 8, T], BF16)
            hns = [kb.sb("hn%d" % i, [128, 8, T], F32) for i in range(2)]
            fbfs = [kb.sb("fbf%d" % i, [128, 8, T], BF16) for i in range(2)]
            yt = kb.sb("yt", [128, T], F32)
            b1, dw, dwb = self.v("cfb1"), self.v("cfdw"), self.v("cfdwb")
            lng, lnb, b2 = self.v("cflng"), self.v("cflnb"), self.v("cfb2")
            for it, (s0, s1, row, t0) in enumerate(self.tiles(segs, T)):
                lo, hi = max(t0 - H, s0), min(t0 + T + H, s1)
                n = hi - lo
                off = lo - (t0 - H)
                xt, hn, fbf = xts[it % 2], hns[it % 2], fbfs[it % 2]
                kb.dma("sp", xt.a[:, :, off:off + n], src.a[:, :, lo:hi], r=[src], w=[xt])
                x3 = xt.a[:, :, off:off + n]
                self.rstd_of(x3, n, sq, P[0], rstd, [xt])
                self.modulate(x3, n, rstd, tmp, abf, self.gs.a[:, l, row, 0, :], self.mod.a[:, l, row, 0:8], [xt])
                for ch in range(8):
                    pa, pg = P[2 + ch % 2], P[4 + ch % 2]
                    sg, u = sgs[ch % 2], us[ch % 2]
                    for (pp, cc) in ((pa, ch), (pg, 8 + ch)):
                        kb.mm(pp.a[:, 0:n], [(w1.a[:, k, cc * 128:(cc + 1) * 128], abf.a[:, k, 0:n]) for k in range(8)],
                              r=[w1, abf], w=[pp])
                    kb.op("act", lambda e, sg=sg, pg=pg, ch=ch: e.activation(sg.a[:, 0:n], pg.a[:, 0:n], AF.Sigmoid,
                                                                             bias=b1[:, 8 + ch:9 + ch]), r=[pg, self.vec], w=[sg])
                    if off > 0:
                        kb.op("pool", lambda e, u=u: e.memset(u.a[:, 0:off], 0.0), w=[u])
                    if off + n < W:
                        kb.op("pool", lambda e, u=u: e.memset(u.a[:, off + n:W], 0.0), w=[u])
                    kb.op("dve", lambda e, u=u, pa=pa, sg=sg, ch=ch: e.scalar_tensor_tensor(
                        u.a[:, off:off + n], pa.a[:, 0:n], b1[:, ch:ch + 1], sg.a[:, 0:n], op0=ALU.add, op1=ALU.mult),
                        r=[pa, sg, self.vec], w=[u])
                    kb.op("dve", lambda e, u=u, ch=ch: e.tensor_scalar(
                        uc.a[:, ch, :], u.a[:, 0:T], dw[:, ch:ch + 1], dwb[:, ch:ch + 1], op0=ALU.mult, op1=ALU.add),
                        r=[u, self.vec], w=[uc])
                    for tap in range(1, 31):
                        kb.op("dve", lambda e, u=u, ch=ch, tap=tap: e.scalar_tensor_tensor(
                            uc.a[:, ch, :], u.a[:, tap:tap + T], dw[:, tap * 8 + ch:tap * 8 + ch + 1], uc.a[:, ch, :],
                            op0=ALU.mult, op1=ALU.add), r=[u, self.vec, uc], w=[uc])
                kb.op("act", lambda e: e.activation(ucq.a, uc.a, AF.Square), r=[uc], w=[ucq])
                kb.mm(P[6].a[:, 0:T], [(self.ones_f.a, uc.a[:, k, :]) for k in range(8)], r=[self.ones_f, uc], w=[P[6]])
                kb.mm(P[7].a[:, 0:T], [(self.ones_f.a, ucq.a[:, k, :]) for k in range(8)], r=[self.ones_f, ucq], w=[P[7]])
                kb.op("act", lambda e: e.activation(mean.a, P[6].a[:, 0:T], AF.Identity, scale=1.0 / D), r=[P[6]], w=[mean])
                kb.op("dve", lambda e: e.tensor_tensor(var.a, mean.a, mean.a, op=ALU.mult), r=[mean], w=[var])
                kb.op("dve", lambda e: e.scalar_tensor_tensor(var.a, P[7].a[:, 0:T], 1.0 / D, var.a,
                                                              op0=ALU.mult, op1=ALU.subtract), r=[P[7], var], w=[var])
                kb.op("act", lambda e: e.activation(var.a, var.a, AF.Sqrt, bias=LN_EPS, scale=1.0), r=[var], w=[var])
                kb.op("dve", lambda e: e.reciprocal(var.a, var.a), r=[var], w=[var])
                kb.op("dve", lambda e: e.tensor_tensor(uc.a, uc.a, mean.a.unsqueeze(1).to_broadcast([128, 8, T]),
                                                       op=ALU.subtract), r=[uc, mean], w=[uc])
                kb.op("dve", lambda e: e.tensor_tensor(uc.a, uc.a, var.a.unsqueeze(1).to_broadcast([128, 8, T]),
                                                       op=ALU.mult), r=[uc, var], w=[uc])
                for k in range(8):
                    kb.op("act", lambda e, k=k: e.activation(sT.a[:, k, :], uc.a[:, k, :], AF.Silu,
                                                             bias=lnb[:, k:k + 1], scale=lng[:, k:k + 1]),
                          r=[uc, self.vec], w=[sT])
                for dm in range(8):
                    py = P[2 + dm % 4]
                    kb.mm(py.a[:, 0:T], [(w2.a[:, k, dm * 128:(dm + 1) * 128], sT.a[:, k, :]) for k in range(8)],
                          r=[w2, sT], w=[py])
                    kb.op("dve", lambda e, py=py, dm=dm: e.tensor_scalar(
                        yt.a, py.a[:, 0:T], b2[:, dm:dm + 1], self.mod.a[:, l, row, 16 + dm:17 + dm],
                        op0=ALU.add, op1=ALU.mult), r=[py, self.vec, self.mod], w=[yt])
                    kb.op("dve", lambda e, dm=dm, hn=hn, xt=xt: e.tensor_tensor(
                        hn.a[:, dm, :], yt.a, xt.a[:, dm, H:H + T], op=ALU.add), r=[yt, xt], w=[hn])
                self.post_mixer(l, row, hn, T, dst, fdst, t0, (sq, tmp, rstd2, fbf))

    def stage_attn(self, l, src, dst, fdst, segs_lat, segs_ctx, lambda_init):
        kb = self.kb
        P = self.P
        T = 512
        nlat = segs_lat[-1][1]
        ntok = segs_ctx[-1][1]
        qTd = kb.dram("a_qT", [128, 8, nlat], BF16)
        kTd = kb.dram("a_kT", [128, 8, ntok], BF16)
        vTd = kb.dram("a_v", [ntok // 128, 128, 1024], BF16)
        oTd = kb.dram("a_oT", [128, 8, nlat], BF16)
        with kb.scope():
            wqkv = kb.sb("wqkv", [128, 8, 3072], BF16)
            self.load_w_bf(wqkv, self.inp("da_w_qkv", [128, 8, 3072]), 8, 3072)
            cosT = kb.sb("cosT", [128, NLAT], F32)
            sinT = kb.sb("sinT", [128, NLAT], F32)
            blk = kb.sb("blk", [128, 128], BF16)
            prot = kb.sb("prot", [128, 128], F32)
            kb.dma("sp", cosT.a, self.inp("ropecos", [128, NLAT]).a, r=[self.din["ropecos"]], w=[cosT])
            kb.dma("sp", sinT.a, self.inp("ropesin", [128, NLAT]).a, r=[self.din["ropesin"]], w=[sinT])
            kb.dma("pool", blk.a, self.inp("blkones", [128, 128]).a, r=[self.din["blkones"]], w=[blk])
            kb.dma("sp", prot.a, self.inp("protm", [128, 128]).a, r=[self.din["protm"]], w=[prot])
            xts = [kb.sb("xt%d" % i, [128, 8, T], F32) for i in range(2)]
            abf = kb.sb("abf", [128, 8, T], BF16)
            sq = kb.sb("sq", [128, 8, T], BF16)
            tmp = kb.sb("tmp", [128, 8, T], F32)
            rstd = kb.sb("rstd", [128, T], F32)
            sqh = kb.sb("sqh", [128, T], BF16)
            rs = kb.sb("rs", [128, T], F32)
            qn = kb.sb("qn", [128, T], F32)
            t1 = kb.sb("t1", [128, T], F32)
            t2 = kb.sb("t2", [128, T], F32)
            qks = [kb.sb("qk%d" % i, [128, 8, T], BF16) for i in range(2)]
            vsb = kb.sb("vsb", [128, 1024], BF16)
            for it, (s0, s1, row, t0) in enumerate(self.tiles(list(segs_lat) + list(segs_ctx), T)):
                n = min(T, s1 - t0)
                is_lat = t0 < nlat
                pos0 = t0 - s0
                xt = xts[it % 2]
                kb.dma("sp", xt.a[:, :, 0:n], src.a[:, :, t0:t0 + n], r=[src], w=[xt])
                x3 = xt.a[:, :, 0:n]
                self.rstd_of(x3, n, sq, P[0], rstd, [xt])
                self.modulate(x3, n, rstd, tmp, abf, self.gs.a[:, l, row, 0, :], self.mod.a[:, l, row, 0:8], [xt])
                for qi, (base, gk, dd, stage) in enumerate(((0, "qg", qTd, qks[0]), (8, "kg", kTd, qks[1]))):
                    if qi == 0 and not is_lat:
                        continue
                    for h in range(8):
                        pq, pss, pr = P[2 + h % 2], P[4 + h % 2], P[6 + h % 2]
                        cc = base + h
                        kb.mm(pq.a[:, 0:n], [(wqkv.a[:, k, cc * 128:(cc + 1) * 128], abf.a[:, k, 0:n]) for k in range(8)],
                              r=[wqkv, abf], w=[pq])
                        kb.op("act", lambda e, pq=pq: e.activation(sqh.a[:, 0:n], pq.a[:, 0:n], AF.Square), r=[pq], w=[sqh])
                        kb.mm(pss.a[:, 0:n], [(blk.a, sqh.a[:, 0:n])], r=[blk, sqh], w=[pss])
                        kb.op("act", lambda e, pss=pss: e.activation(rs.a[:, 0:n], pss.a[:, 0:n], AF.Sqrt, bias=RMS_EPS,
                                                                     scale=1.0 / 64), r=[pss], w=[rs])
                        kb.op("dve", lambda e: e.reciprocal(rs.a[:, 0:n], rs.a[:, 0:n]), r=[rs], w=[rs])
                        kb.op("dve", lambda e, pq=pq, gk=gk: e.scalar_tensor_tensor(
                            qn.a[:, 0:n], pq.a[:, 0:n], self.v(gk), rs.a[:, 0:n], op0=ALU.mult, op1=ALU.mult),
                            r=[pq, rs, self.vec], w=[qn])
                        if is_lat:
                            kb.mm(pr.a[:, 0:n], [(prot.a, qn.a[:, 0:n])], r=[prot, qn], w=[pr])
                            kb.op("pool", lambda e: e.tensor_tensor(t1.a[:, 0:n], qn.a[:, 0:n], cosT.a[:, pos0:pos0 + n],
                                                                    op=ALU.mult), r=[qn, cosT], w=[t1])
                            kb.op("dve", lambda e, pr=pr: e.tensor_tensor(t2.a[:, 0:n], pr.a[:, 0:n], sinT.a[:, pos0:pos0 + n],
                                                                          op=ALU.mult), r=[pr, sinT], w=[t2])
                            kb.op("dve", lambda e, stage=stage, h=h: e.tensor_tensor(stage.a[:, h, 0:n], t1.a[:, 0:n], t2.a[:, 0:n],
                                                                                     op=ALU.add), r=[t1, t2], w=[stage])
                        else:
                            kb.op("act", lambda e, stage=stage, h=h: e.copy(stage.a[:, h, 0:n], qn.a[:, 0:n]), r=[qn], w=[stage])
                    kb.dma("sp", dd.a[:, :, t0:t0 + n], stage.a[:, :, 0:n], r=[stage], w=[dd])
                for sub in range(n // 128):
                    for half in range(2):
                        pv = P[2 + half]
                        kb.mm(pv.a, [(abf.a[:, k, sub * 128:(sub + 1) * 128],
                                      wqkv.a[:, k, 2048 + half * 512:2048 + (half + 1) * 512]) for k in range(8)],
                              r=[abf, wqkv], w=[pv])
                        if half == 0:
                            kb.op("act", lambda e, pv=pv: e.copy(vsb.a[:, 0:512], pv.a), r=[pv], w=[vsb])
                        else:
                            kb.op("dve", lambda e, pv=pv: e.tensor_copy(vsb.a[:, 512:1024], pv.a), r=[pv], w=[vsb])
                    kb.dma("sp", vTd.a[(t0 + sub * 128) // 128], vsb.a, r=[vsb], w=[vTd])
        with kb.scope():
            lam = kb.sb("lam", [128, 4], F32)
            lt = kb.sb("lt", [128, 64], F32)
            for ii, (ka, kb_) in enumerate((("lq1", "lk1"), ("lq2", "lk2"))):
                kb.op("dve", lambda e, ka=ka, kb_=kb_: e.tensor_tensor(lt.a, self.v(ka), self.v(kb_), op=ALU.mult),
                      r=[self.vec], w=[lt])
                kb.op("dve", lambda e, ii=ii: e.tensor_reduce(out=lam.a[:, ii:ii + 1], in_=lt.a, axis=AX.X, op=ALU.add),
                      r=[lt], w=[lam])
            kb.op("act", lambda e: e.activation(lam.a[:, 0:2], lam.a[:, 0:2], AF.Exp), r=[lam], w=[lam])
            kb.op("dve", lambda e: e.tensor_tensor(lam.a[:, 2:3], lam.a[:, 1:2], lam.a[:, 0:1], op=ALU.subtract), r=[lam], w=[lam])
            kb.op("dve", lambda e: e.tensor_scalar(lam.a[:, 2:3], lam.a[:, 2:3], -float(lambda_init), None, op0=ALU.add),
                  r=[lam], w=[lam])
            kb.op("dve", lambda e: e.tensor_scalar(lam.a[:, 3:4], self.v("subg"), 1.0 - float(lambda_init), None, op0=ALU.mult),
                  r=[self.vec], w=[lam])
            kts = [kb.sb("kt%d" % i, [128, NLAT + NCTX], BF16) for i in range(2)]
            vts = [kb.sb("vt%d" % i, [128, 18, 128], BF16) for i in range(2)]
            qts = [kb.sb("qt%d" % i, [128, T], BF16) for i in range(2)]
            pTs = [kb.sb("pT%d" % i, [128, T], BF16) for i in range(4)]
            rz = kb.sb("rz", [128, 2, T], F32)
            o1 = kb.sb("o1", [128, T], F32)
            o2 = kb.sb("o2", [128, T], F32)
            osq = kb.sb("osq", [128, T], BF16)
            ors = kb.sb("ors", [128, T], F32)
            ofs = [kb.sb("of%d" % i, [128, T], BF16) for i in range(2)]
            scale = 64 ** -0.5
            nit = 0
            npt = 0
            for bi, ((l0, l1, _), (c0, c1, _)) in enumerate(zip(segs_lat, segs_ctx)):
                nkl = (l1 - l0) // 128
                nkc = (c1 - c0) // 128
                nk = nkl + nkc
                for h in range(8):
                    kt, vt = kts[(bi * 8 + h) % 2], vts[(bi * 8 + h) % 2]
                    kb.dma("sp", kt.a[:, 0:l1 - l0], kTd.a[:, h, l0:l1], r=[kTd], w=[kt])
                    kb.dma("sp", kt.a[:, l1 - l0:l1 - l0 + c1 - c0], kTd.a[:, h, c0:c1], r=[kTd], w=[kt])
                    kb.dma("sp", vt.a[:, 0:nkl, :], vTd.a[l0 // 128:l1 // 128, :, h * 128:(h + 1) * 128].rearrange("c p d -> p c d"),
                           r=[vTd], w=[vt])
                    kb.dma("sp", vt.a[:, nkl:nk, :], vTd.a[c0 // 128:c1 // 128, :, h * 128:(h + 1) * 128].rearrange("c p d -> p c d"),
                           r=[vTd], w=[vt])
                    for q0 in range(l0, l1, T):
                        qt = qts[nit % 2]
                        of = ofs[nit % 2]
                        nit += 1
                        kb.dma("sp", qt.a, qTd.a[:, h, q0:q0 + T], r=[qTd], w=[qt])
                        pend = []

                        def emit_pv(item):
                            kc, comp, pT = item
                            kb.mm(P[comp].a, [(vt.a[:, kc, :], pT.a)], r=[vt, pT], w=[P[comp]],
                                  start=(kc == 0), stop=(kc == nk - 1))
                            kb.mm(P[2 + comp].a, [(self.ones_bf.a, pT.a)], r=[self.ones_bf, pT], w=[P[2 + comp]],
                                  start=(kc == 0), stop=(kc == nk - 1))

                        for kc in range(nk):
                            for comp in range(2):
                                pS = P[4 + npt % 4]
                                pT = pTs[npt % 4]
                                npt += 1
                                kb.mm(pS.a, [(kt.a[comp * 64:(comp + 1) * 64, kc * 128:(kc + 1) * 128],
                                              qt.a[comp * 64:(comp + 1) * 64, :])], r=[kt, qt], w=[pS])
                                kb.op("act", lambda e, pS=pS, pT=pT: e.activation(pT.a, pS.a, AF.Exp, scale=scale), r=[pS], w=[pT])
                                pend.append((kc, comp, pT))
                                if len(pend) > 2:
                                    emit_pv(pend.pop(0))
                        while pend:
                            emit_pv(pend.pop(0))
                        for comp in range(2):
                            kb.op("dve", lambda e, comp=comp: e.reciprocal(rz.a[:, comp, :], P[2 + comp].a), r=[P[2 + comp]], w=[rz])
                        kb.op("dve", lambda e: e.tensor_tensor(o1.a, P[0].a, rz.a[:, 0, :], op=ALU.mult), r=[P[0], rz], w=[o1])
                        kb.op("dve", lambda e: e.tensor_tensor(o2.a, P[1].a, rz.a[:, 1, :], op=ALU.mult), r=[P[1], rz], w=[o2])
                        kb.op("dve", lambda e: e.scalar_tensor_tensor(o1.a, o2.a, lam.a[:, 2:3], o1.a, op0=ALU.mult, op1=ALU.add),
                              r=[o1, o2, lam], w=[o1])
                        kb.op("act", lambda e: e.activation(osq.a, o1.a, AF.Square), r=[o1], w=[osq])
                        pss = P[4 + npt % 4]
                        npt += 1
                        kb.mm(pss.a, [(self.ones_bf.a, osq.a)], r=[self.ones_bf, osq], w=[pss])
                        kb.op("act", lambda e, pss=pss: e.activation(ors.a, pss.a, AF.Sqrt, bias=RMS_EPS, scale=1.0 / 128),
                              r=[pss], w=[ors])
                        kb.op("dve", lambda e: e.reciprocal(ors.a, ors.a), r=[ors], w=[ors])
                        kb.op("dve", lambda e, of=of: e.scalar_tensor_tensor(of.a, o1.a, lam.a[:, 3:4], ors.a, op0=ALU.mult, op1=ALU.mult),
                              r=[o1, lam, ors], w=[of])
                        kb.dma("sp", oTd.a[:, h, q0:q0 + T], of.a, r=[of], w=[oTd])
        with kb.scope():
            wo = kb.sb("wo", [128, 8, 1024], BF16)
            self.load_w_bf(wo, self.inp("da_w_o", [128, 8, 1024]), 8, 1024)
            T3 = 256
            xts = [kb.sb("xt%d" % i, [128, 8, T3], F32) for i in range(2)]
            ots = [kb.sb("ot%d" % i, [128, 8, T3], BF16) for i in range(2)]
            hns = [kb.sb("hn%d" % i, [128, 8, T3], F32) for i in range(2)]
            fbfs = [kb.sb("fbf%d" % i, [128, 8, T3], BF16) for i in range(2)]
            sq = kb.sb("sq", [128, 8, T3], BF16)
            tmp = kb.sb("tmp", [128, 8, T3], F32)
            rstd2 = kb.sb("rstd2", [128, T3], F32)
            for it, (s0, s1, row, t0) in enumerate(self.tiles(segs_lat, T3)):
                xt, ot, hn, fbf = xts[it % 2], ots[it % 2], hns[it % 2], fbfs[it % 2]
                kb.dma("sp", xt.a, src.a[:, :, t0:t0 + T3], r=[src], w=[xt])
                kb.dma("sp", ot.a, oTd.a[:, :, t0:t0 + T3], r=[oTd], w=[ot])
                for dm in range(8):
                    py = P[2 + dm % 6]
                    kb.mm(py.a[:, 0:T3], [(wo.a[:, k, dm * 128:(dm + 1) * 128], ot.a[:, k, :]) for k in range(8)],
                          r=[wo, ot], w=[py])
                    kb.op("dve", lambda e, py=py, dm=dm, hn=hn, xt=xt: e.scalar_tensor_tensor(
                        hn.a[:, dm, :], py.a[:, 0:T3], self.mod.a[:, l, row, 16 + dm:17 + dm], xt.a[:, dm, :],
                        op0=ALU.mult, op1=ALU.add), r=[py, self.mod, xt], w=[hn])
                self.post_mixer(l, row, hn, T3, dst, fdst, t0, (sq, tmp, rstd2, fbf))

    def peer_convert(self, l):
        kb = self.kb
        uT = self.inp("peer_uT%d" % l, [128, 8, PEER_E])
        vv = self.inp("peer_v%d" % l, [128, 128, 1024])
        ubf = kb.dram("ubf%d" % l, [128, 8, PEER_E], BF16)
        vbf = kb.dram("vbf%d" % l, [128, 128, 1024], BF16)
        for k in range(8):
            for c in range(0, PEER_E, 4096):
                kb.dma("pool", ubf.a[:, k, c:c + 4096], uT.a[:, k, c:c + 4096], r=[uT], w=[ubf])
        for i0 in range(0, 128, 4):
            kb.dma("pool", vbf.a[:, i0:i0 + 4, :], vv.a[:, i0:i0 + 4, :], r=[vv], w=[vbf])
        return ubf, vbf

    def stage_peer_q(self, l, fsrc, qTd, t_lo, t_hi):
        kb = self.kb
        T = 512
        with kb.scope():
            wq = kb.sb("wq", [128, 8, 2048], BF16)
            self.load_w_bf(wq, self.inp("peer_wq%d" % l, [128, 8, 2048]), 8, 2048)
            fts = [kb.sb("ft%d" % i, [128, 8, T], BF16) for i in range(2)]
            qts = [kb.sb("qt%d" % i, [128, 16, T], BF16) for i in range(2)]
            for it, t0 in enumerate(range(t_lo, t_hi, T)):
                n = min(T, t_hi - t0)
                ft, qt = fts[it % 2], qts[it % 2]
                kb.dma("sp", ft.a[:, :, 0:n], fsrc.a[:, :, t0:t0 + n], r=[fsrc], w=[ft])
                for c in range(16):
                    pq = self.P[c % 8]
                    kb.mm(pq.a[:, 0:n], [(wq.a[:, k, c * 128:(c + 1) * 128], ft.a[:, k, 0:n]) for k in range(8)],
                          r=[wq, ft], w=[pq])
                    en = "act" if c % 2 == 0 else "dve"
                    if en == "act":
                        kb.op("act", lambda e, c=c, pq=pq, qt=qt: e.copy(qt.a[:, c, 0:n], pq.a[:, 0:n]), r=[pq], w=[qt])
                    else:
                        kb.op("dve", lambda e, c=c, pq=pq, qt=qt: e.tensor_copy(qt.a[:, c, 0:n], pq.a[:, 0:n]), r=[pq], w=[qt])
                kb.dma("sp", qTd.a[:, :, t0:t0 + n], qt.a[:, :, 0:n], r=[qt], w=[qTd])

    def stage_peer_route(self, l, qTd, Wd, t_lo, t_hi):
        kb = self.kb
        P = self.P
        NEG = -1.0e30
        with kb.scope():
            KT = kb.sb("KT", [128, 16, 128], BF16)
            kb.dma("pool", KT.a, self.inp("peer_kT%d" % l, [128, 16, 128]).a, r=[self.din["peer_kT%d" % l]], w=[KT])
            ident = kb.sb("ident", [128, 128], BF16)
            if "ident" not in self.din:
                self.inp("ident", [128, 128])
            kb.dma("pool", ident.a, self.din["ident"].a, r=[self.din["ident"]], w=[ident])
            qts = [kb.sb("rq%d" % i, [128, 16, 128], BF16) for i in range(2)]
            S = kb.sb("S", [128, 16, 128], F32)
            Ss = kb.slots(S, 16)
            Sx = kb.sb("Sx", [128, 16, 128], F32)
            Sxs = kb.slots(Sx, 16)
            M = kb.sb("M", [128, 16, 16], F32)
            Ms = kb.slots(M, 32)
            cand = kb.sb("cand", [128, 8, 256], F32)
            cands = kb.slots(cand, 8)
            Cx = kb.sb("Cx", [128, 8, 256], F32)
            Cxs = kb.slots(Cx, 8)
            C16 = kb.sb("C16", [128, 8, 16], F32)
            C16s = kb.slots(C16, 16)
            sm = kb.sb("sm", [128, 8, 16], F32)
            Zs = kb.sb("Zs", [128, 8], F32)
            thr = kb.sb("thr", [128, 8], F32)
            E1 = kb.sb("E1", [128, 8, 16], F32)
            cc = kb.sb("cc", [128, 8, 16], F32)
            E2 = kb.sb("E2", [128, 8, 128], F32)
            tms = [kb.sb("tm%d" % i, [128, 128], F32) for i in range(8)]
            R = kb.sb("R", [128, 128, 128], BF16)
            Rs = kb.slots(R, 128)
            Ws = kb.slots(R, 32)
            OH = kb.sb("OH", [128, 128, 128], BF16)
            OHs = kb.slots(OH, 8)
            RT = kb.sb("RT", [128, 128, 128], BF16)
            RTs = kb.slots(RT, 32)
            OHT = kb.sb("OHT", [128, 128, 128], BF16)
            OHTs = kb.slots(OHT, 32)
            S4 = S.a.rearrange("p (h two) n -> p h two n", two=2)
            M4 = M.a.rearrange("p (h two) n -> p h two n", two=2)
            nb = 0
            for it, t0 in enumerate(range(t_lo, t_hi, 128)):
                g = t0 // 128
                qt = qts[it % 2]
                kb.dma("sp", qt.a, qTd.a[:, :, t0:t0 + 128], r=[qTd], w=[qt])
                kb.fork(S, Ss)
                for b in range(4):
                    pb = P[nb % 8]
                    nb += 1
                    for cI in range(4):
                        c = b * 4 + cI
                        kb.mm(pb.a[:, cI * 128:(cI + 1) * 128], [(qt.a[:, c, :], KT.a[:, c, :])], r=[qt, KT], w=[pb])
                    kb.op("act", lambda e: e.copy(S.a[:, b * 4:(b + 1) * 4, :].rearrange("p c n -> p (c n)"), pb.a),
                          r=[pb], w=Ss[b * 4:(b + 1) * 4])
                kb.fork(M, Ms)
                for c in range(16):
                    kb.op("dve", lambda e: e.max(out=M.a[:, c, 0:8], in_=S.a[:, c, :]), r=[Ss[c]], w=[Ms[2 * c]])
                for c in range(16):
                    kb.op("dve", lambda e: e.match_replace(out=Sx.a[:, c, :], in_to_replace=M.a[:, c, 0:8],
                                                           in_values=S.a[:, c, :], imm_value=NEG),
                          r=[Ss[c], Ms[2 * c]], w=[Sxs[c]])
                for c in range(16):
                    kb.op("dve", lambda e: e.max(out=M.a[:, c, 8:16], in_=Sx.a[:, c, :]), r=[Sxs[c]], w=[Ms[2 * c + 1]])
                kb.join(S, Ss)
                kb.join(M, Ms)
                for h in range(8):
                    kb.op("pool", lambda e: e.tensor_tensor(
                        cand.a[:, h, :].rearrange("p (a b) -> p a b", a=16),
                        M.a[:, 2 * h, :].unsqueeze(2).to_broadcast([128, 16, 16]),
                        M.a[:, 2 * h + 1, :].unsqueeze(1).to_broadcast([128, 16, 16]), op=ALU.add),
                        r=Ms[4 * h:4 * h + 4], w=[cands[h]])
                kb.fork(C16, C16s)
                for h in range(8):
                    kb.op("dve", lambda e: e.max(out=C16.a[:, h, 0:8], in_=cand.a[:, h, :]), r=[cands[h]], w=[C16s[2 * h]])
                for h in range(8):
                    kb.op("dve", lambda e: e.match_replace(out=Cx.a[:, h, :], in_to_replace=C16.a[:, h, 0:8],
                                                           in_values=cand.a[:, h, :], imm_value=NEG),
                          r=[cands[h], C16s[2 * h]], w=[Cxs[h]])
                for h in range(8):
                    kb.op("dve", lambda e: e.max(out=C16.a[:, h, 8:16], in_=Cx.a[:, h, :]), r=[Cxs[h]], w=[C16s[2 * h + 1]])
                kb.join(C16, C16s)
                kb.op("dve", lambda e: e.tensor_tensor(sm.a, C16.a, C16.a[:, :, 0:1].to_broadcast([128, 8, 16]),
                                                       op=ALU.subtract), r=[C16], w=[sm])
                kb.op("act", lambda e: e.activation(sm.a, sm.a, AF.Exp), r=[sm], w=[sm])
                kb.op("dve", lambda e: e.tensor_reduce(out=Zs.a, in_=sm.a, axis=AX.X, op=ALU.add), r=[sm], w=[Zs])
                kb.op("dve", lambda e: e.reciprocal(Zs.a, Zs.a), r=[Zs], w=[Zs])
                kb.op("dve", lambda e: e.scalar_tensor_tensor(thr.a, C16.a[:, :, 15], -1.0, C16.a[:, :, 0],
                                                              op0=ALU.mult, op1=ALU.max), r=[C16], w=[thr])
                kb.op("dve", lambda e: e.scalar_tensor_tensor(thr.a, thr.a, -2.0e-5, C16.a[:, :, 15],
                                                              op0=ALU.mult, op1=ALU.add), r=[thr, C16], w=[thr])
                kb.op("dve", lambda e: e.tensor_tensor(E1.a, M4[:, :, 0, :], M4[:, :, 0, 0:1].to_broadcast([128, 8, 16]),
                                                       op=ALU.subtract), r=[M], w=[E1])
                kb.op("act", lambda e: e.activation(E1.a, E1.a, AF.Exp), r=[E1], w=[E1])
                kb.op("dve", lambda e: e.tensor_tensor(E1.a, E1.a, Zs.a.unsqueeze(2).to_broadcast([128, 8, 16]),
                                                       op=ALU.mult), r=[E1, Zs], w=[E1])
                kb.op("dve", lambda e: e.scalar_tensor_tensor(cc.a, M4[:, :, 0, :], -1.0,
                                                              thr.a.unsqueeze(2).to_broadcast([128, 8, 16]),
                                                              op0=ALU.mult, op1=ALU.add), r=[M, thr], w=[cc])
                kb.op("dve", lambda e: e.tensor_tensor(E2.a, S4[:, :, 1, :], M4[:, :, 1, 0:1].to_broadcast([128, 8, 128]),
                                                       op=ALU.subtract), r=[S, M], w=[E2])
                kb.op("act", lambda e: e.activation(E2.a, E2.a, AF.Exp), r=[E2], w=[E2])
                kb.fork(R, Rs)
                kb.fork(OH, OHs)
                for h in range(8):
                    kb.op("dve", lambda e: e.tensor_tensor(
                        OH.a[:, h * 16:(h + 1) * 16, :],
                        S.a[:, 2 * h, :].unsqueeze(1).to_broadcast([128, 16, 128]),
                        M.a[:, 2 * h, :].unsqueeze(2).to_broadcast([128, 16, 128]), op=ALU.is_equal),
                        r=[S, M], w=[OHs[h]])
                    for r in range(16):
                        c = h * 16 + r
                        tm = tms[c % 8]
                        kb.op("dve", lambda e: e.scalar_tensor_tensor(
                            tm.a, S.a[:, 2 * h + 1, :], cc.a[:, h, r:r + 1], E2.a[:, h, :], op0=ALU.is_ge, op1=ALU.mult),
                            r=[S, cc, E2], w=[tm])
                        kb.op("act", lambda e: e.activation(
                            R.a[:, c, :], tm.a, AF.Identity, scale=E1.a[:, h, r:r + 1]), r=[tm, E1], w=[Rs[c]])
                kb.join(R, Rs)
                kb.join(OH, OHs)
                kb.fork(RT, RTs)
                kb.fork(OHT, OHTs)
                for (srcb, dstb, dsl) in ((R, RT, RTs), (OH, OHT, OHTs)):
                    for j0 in range(0, 128, 4):
                        pb = P[nb % 8]
                        nb += 1
                        for jj in range(4):
                            kb.mm(pb.a[:, jj * 128:(jj + 1) * 128], [(srcb.a[:, :, j0 + jj], ident.a)],
                                  r=[srcb, ident], w=[pb])
                        dv = dstb.a[:, j0:j0 + 4, :].rearrange("p j t -> p (j t)")
                        if nb % 2 == 0:
                            kb.op("act", lambda e: e.copy(dv, pb.a), r=[pb], w=[dsl[j0 // 4]])
                        else:
                            kb.op("dve", lambda e: e.tensor_copy(dv, pb.a), r=[pb], w=[dsl[j0 // 4]])
                kb.join(RT, RTs)
                kb.join(OHT, OHTs)
                kb.fork(R, Ws)
                for tq in range(0, 128, 4):
                    pb = P[nb % 8]
                    nb += 1
                    for tt in range(4):
                        kb.mm(pb.a[:, tt * 128:(tt + 1) * 128], [(RT.a[:, :, tq + tt], OHT.a[:, :, tq + tt])],
                              r=[RT, OHT], w=[pb])
                    dv = R.a[:, :, tq:tq + 4].rearrange("p i t -> p t i")
                    sv = pb.a.rearrange("p (t i) -> p t i", t=4)
                    if nb % 2 == 0:
                        kb.op("act", lambda e: e.copy(dv, sv), r=[pb], w=[Ws[tq // 4]])
                    else:
                        kb.op("dve", lambda e: e.tensor_copy(dv, sv), r=[pb], w=[Ws[tq // 4]])
                kb.join(R, Ws)
                kb.dma("sp", Wd.a[g], R.a, r=[R], w=[Wd])

    def stage_peer_experts(self, l, ubf, vbf, fsrc, hsrc, hdst, Wd, segs, dst_off=0):
        kb = self.kb
        P = self.P
        T = 256
        NBG = 4
        with kb.scope():
            uts = [kb.sb("ut%d" % i, [128, 8, NBG * 128], BF16) for i in range(3)]
            vts = [kb.sb("vt%d" % i, [128, NBG, 1024], BF16) for i in range(3)]
            wts = [kb.sb("wt%d" % i, [128, 2, NBG, 128], BF16) for i in range(3)]
            fts = [kb.sb("eft%d" % i, [128, 8, T], BF16) for i in range(2)]
            hts = [kb.sb("eht%d" % i, [128, 8, T], F32) for i in range(2)]
            gzs = [kb.sb("gz%d" % i, [128, T], F32) for i in range(3)]
            As = [kb.sb("A%d" % i, [128, T], BF16) for i in range(4)]
            nslot = 0
            nz = 0
            LOOK = 2
            for ig, (s0, s1, row, t0) in enumerate(self.tiles(segs, T)):
                ft, ht = fts[ig % 2], hts[ig % 2]
                kb.dma("sp", ft.a, fsrc.a[:, :, t0:t0 + T], r=[fsrc], w=[ft])
                kb.dma("sp", ht.a, hsrc.a[:, :, t0:t0 + T], r=[hsrc], w=[ht])
                g0 = t0 // 128
                pend = []

                def emit_out(item):
                    i, vt, b, A = item
                    for dm in range(8):
                        po = P[dm // 2]
                        kb.mm(po.a[:, (dm % 2) * T:(dm % 2 + 1) * T], [(vt.a[:, b, dm * 128:(dm + 1) * 128], A.a)],
                              r=[vt, A], w=[po], start=(i == 0), stop=(i == 127))

                for bg in range(128 // NBG):
                    ut, vt, wt = uts[nslot % 3], vts[nslot % 3], wts[nslot % 3]
                    nslot += 1
                    kb.dma("sp", ut.a, ubf.a[:, :, bg * NBG * 128:(bg + 1) * NBG * 128], r=[ubf], w=[ut])
                    kb.dma("sp", vt.a, vbf.a[:, bg * NBG:(bg + 1) * NBG, :], r=[vbf], w=[vt])
                    for tl in range(2):
                        kb.dma("sp", wt.a[:, tl, :, :], Wd.a[g0 + tl][:, bg * NBG:(bg + 1) * NBG, :], r=[Wd], w=[wt])
                    for b in range(NBG):
                        i = bg * NBG + b
                        pz = P[4 + nz % 4]
                        gz, A = gzs[nz % 3], As[nz % 4]
                        nz += 1
                        kb.mm(pz.a[:, 0:T], [(ut.a[:, k, b * 128:(b + 1) * 128], ft.a[:, k, :]) for k in range(8)],
                              r=[ut, ft], w=[pz])
                        kb.op("act", lambda e: e.activation(gz.a, pz.a[:, 0:T], AF.Gelu), r=[pz], w=[gz])
                        kb.op("dve", lambda e: e.tensor_tensor(
                            A.a.rearrange("p (a t) -> p a t", a=2), gz.a.rearrange("p (a t) -> p a t", a=2),
                            wt.a[:, :, b, :], op=ALU.mult), r=[gz, wt], w=[A])
                        pend.append((i, vt, b, A))
                        if len(pend) > LOOK:
                            emit_out(pend.pop(0))
                while pend:
                    emit_out(pend.pop(0))
                for dm in range(8):
                    po = P[dm // 2]
                    kb.op("dve", lambda e, dm=dm, po=po, ht=ht: e.scalar_tensor_tensor(
                        ht.a[:, dm, :], po.a[:, (dm % 2) * T:(dm % 2 + 1) * T], self.mod.a[:, l, row, 40 + dm:41 + dm],
                        ht.a[:, dm, :], op0=ALU.mult, op1=ALU.add), r=[po, self.mod, ht], w=[ht])
                kb.dma("sp", hdst.a[:, :, t0 - dst_off:t0 - dst_off + T], ht.a, r=[ht], w=[hdst])

def kmaj(w):
    w = np.asarray(w, np.float32)
    nk = w.shape[0] // 128
    return np.ascontiguousarray(w.reshape(nk, 128, w.shape[1]).transpose(1, 0, 2))


def tok_to_T(X):
    X = np.asarray(X)
    return np.ascontiguousarray(X.T.reshape(8, 128, X.shape[0]).transpose(1, 0, 2))


def T_to_tok(hT):
    return np.ascontiguousarray(hT.transpose(1, 0, 2).reshape(1024, hT.shape[2]).T)


def build_vecs(inp, crows):
    vp = VecPack()
    cT = np.stack([packv(crows[r]) for r in range(3)], axis=2)
    vp.add("cT", cT.reshape(128, 24))
    for l in range(DEPTH):
        vp.add("bmod%d" % l, packv(inp["b_mod"][l]))
        vp.add("n1g%d" % l, packv(inp["norm1_g"][l]))
        vp.add("n2g%d" % l, packv(inp["norm2_g"][l]))
    for j in range(inp["sc_conv_w"].shape[0]):
        cw = inp["sc_conv_w"][j]
        vp.add("scw%d" % j, np.concatenate([packv(cw[k]) for k in range(3)], axis=1))
    if "cf_b_pw1" in inp:
        vp.add("cfb1", packv(inp["cf_b_pw1"][0]))
        dw = inp["cf_dw_w"][0]
        vp.add("cfdw", np.concatenate([packv(dw[k]) for k in range(dw.shape[0])], axis=1))
        for key, nm in (("cf_dw_b", "cfdwb"), ("cf_ln_g", "cflng"), ("cf_ln_b", "cflnb"), ("cf_b_pw2", "cfb2")):
            vp.add(nm, packv(inp[key][0]))
    if "da_q_norm_g" in inp:
        rep = lambda v: np.ascontiguousarray(np.broadcast_to(np.asarray(v, np.float32)[None, :], (128, len(v))))
        vp.add("qg", np.tile(np.asarray(inp["da_q_norm_g"][0], np.float32), 2).reshape(128, 1))
        vp.add("kg", np.tile(np.asarray(inp["da_k_norm_g"][0], np.float32), 2).reshape(128, 1))
        vp.add("subg", np.asarray(inp["da_subln_g"][0], np.float32).reshape(128, 1))
        for key, nm in (("da_lam_q1", "lq1"), ("da_lam_k1", "lk1"), ("da_lam_q2", "lq2"), ("da_lam_k2", "lk2")):
            vp.add(nm, rep(inp[key][0]))
    return vp


def host_consts():
    c = {}
    c["ident"] = np.eye(128, dtype=np.float32)
    blk = np.zeros((128, 128), np.float32)
    blk[:64, :64] = 1.0
    blk[64:, 64:] = 1.0
    c["blkones"] = blk
    prot = np.zeros((128, 128), np.float32)
    cosT = np.zeros((128, NLAT), np.float32)
    sinT = np.zeros((128, NLAT), np.float32)
    t = np.arange(NLAT)
    pos = (np.floor_divide(t, 64).astype(np.float32), np.mod(t, 64).astype(np.float32))
    nf = 16
    inv_freq = (10000.0 ** (-np.arange(nf, dtype=np.float32) / nf)).astype(np.float32)
    for p in range(128):
        dh = p % 64
        axis, half, f = dh // 32, (dh % 32) // 16, dh % 16
        ang = pos[axis] * inv_freq[f]
        cosT[p] = np.cos(ang)
        sinT[p] = np.sin(ang)
        if half == 0:
            prot[p + 16, p] = -1.0
        else:
            prot[p - 16, p] = 1.0
    c["protm"] = prot
    c["ropecos"] = cosT
    c["ropesin"] = sinT
    return c


SEGS_LAT = [(0, NLAT, 0), (NLAT, 2 * NLAT, 1)]
SEGS_CTX = [(2 * NLAT, 2 * NLAT + NCTX, 2), (2 * NLAT + NCTX, 2 * NLAT + 2 * NCTX, 2)]
NTOK = 2 * NLAT + 2 * NCTX


def build_program(voff, nv):
    pg = Prog(SEGS_LAT, SEGS_CTX, voff, nv)
    kb = pg.kb
    pg.prologue()
    pg.stage_mod(range(DEPTH))
    xT = pg.inp("xT", [128, 8, NTOK])
    hX = kb.dram("hX", [128, 8, NTOK], F32)
    hM = kb.dram("hM", [128, 8, NTOK], F32)
    fT = kb.dram("fT", [128, 8, NTOK], BF16)
    qTd = kb.dram("p_qT", [128, 16, NTOK], BF16)
    Wd = kb.dram("p_W", [NTOK // 128, 128, 128, 128], BF16)
    yT = kb.dram("yT", [128, 8, 2 * NLAT], F32, kind="ExternalOutput")
    tabs = {0: pg.peer_convert(0)}
    for l in range(DEPTH):
        src = xT if l == 0 else hX
        kind = l % 3
        if kind == 0:
            segs = SEGS_LAT + (SEGS_CTX if l == 0 else [])
            pg.stage_sconv(l, src, hM, fT, segs)
        elif kind == 1:
            pg.stage_attn(l, src, hM, fT, SEGS_LAT, SEGS_CTX, 0.8 - 0.6 * math.exp(-0.3 * l))
        else:
            pg.stage_conformer(l, src, hM, fT, SEGS_LAT)
        if l + 1 < DEPTH:
            tabs[l + 1] = pg.peer_convert(l + 1)
        psegs = SEGS_LAT + (SEGS_CTX if l == 0 else [])
        t_hi = psegs[-1][1]
        pg.stage_peer_q(l, fT, qTd, 0, t_hi)
        pg.stage_peer_route(l, qTd, Wd, 0, t_hi)
        ubf, vbf = tabs[l]
        pg.stage_peer_experts(l, ubf, vbf, fT, hM, yT if l == DEPTH - 1 else hX, Wd, psegs)
    kb.finish([yT])
    return pg


def kernel(**inputs):
    inp = {k: np.asarray(v) for k, v in inputs.items()}
    ncore = 8
    consts = host_consts()
    shared = dict(consts)
    for l in range(DEPTH):
        shared["w_mod%d" % l] = kmaj(inp["w_mod"][l])
        shared["peer_uT%d" % l] = kmaj(inp["peer_u"][l].T)
        shared["peer_v%d" % l] = np.ascontiguousarray(inp["peer_v"][l].reshape(128, 128, D).transpose(1, 0, 2))
        shared["peer_wq%d" % l] = kmaj(inp["peer_w_query"][l])
        shared["peer_kT%d" % l] = np.ascontiguousarray(inp["peer_sub_keys"][l].reshape(16, 128, 128).transpose(2, 0, 1))
    for j in range(inp["sc_w_in"].shape[0]):
        shared["sc_w_in%d" % j] = kmaj(inp["sc_w_in"][j])
        shared["sc_w_out%d" % j] = kmaj(inp["sc_w_out"][j])
    shared["da_w_qkv"] = kmaj(inp["da_w_qkv"][0])
    shared["da_w_o"] = kmaj(inp["da_w_o"][0])
    shared["cf_w_pw1"] = kmaj(inp["cf_w_pw1"][0])
    shared["cf_w_pw2"] = kmaj(inp["cf_w_pw2"][0])
    in_maps = []
    pg = None
    for c in range(ncore):
        b0, b1 = 2 * c, 2 * c + 1
        crows = np.stack([inp["c"][b0], inp["c"][b1], inp["c_ctx"]], axis=0)
        vp = build_vecs(inp, crows)
        if pg is None:
            pg = build_program(vp.off, vp.n)
        X = np.concatenate([inp["x"][b0], inp["x"][b1], inp["ctx"][b0], inp["ctx"][b1]], axis=0)
        m = dict(shared)
        m["vecs"] = vp.array()
        m["xT"] = tok_to_T(X)
        in_maps.append({k: m[k] for k in pg.din})
    res = run_bass_kernel_spmd(pg.kb.nc, in_maps, core_ids=list(range(ncore)))
    out = np.empty((2 * ncore, NLAT, D), np.float32)
    for c in range(ncore):
        Y = T_to_tok(np.asarray(res.results[c]["yT"]))
        out[2 * c] = Y[:NLAT]
        out[2 * c + 1] = Y[NLAT:]
    return out
```

```python
import contextlib
import math
import numpy as np
import concourse.bass as bass
import concourse.mybir as mybir
from concourse.bass_utils import run_bass_kernel_spmd

F32 = mybir.dt.float32
BF16 = mybir.dt.bfloat16
ALU = mybir.AluOpType
AF = mybir.ActivationFunctionType
AX = mybir.AxisListType

D = 1024
KC = 8
NLAT = 2048
NCTX = 256
NB = 2
DEPTH = 4
RMS_EPS = 1e-6
LN_EPS = 1e-5
PEER_E = 16384


class Buf:
    def __init__(self, name, a=None, space="sb"):
        self.name = name
        self.a = a
        self.space = space
        self.w = {}
        self.r = {}
        self.sem = None
        self.cnt = 0


class KB:
    def __init__(self):
        self.nc = bass.Bass("TRN2", target_bir_lowering=False)
        self.es = contextlib.ExitStack()
        nc = self.nc
        self.sems = []
        self.eng = {}
        for nm, e in (("pe", nc.tensor), ("act", nc.scalar), ("dve", nc.vector),
                      ("pool", nc.gpsimd), ("sp", nc.sync)):
            self.eng[nm] = dict(e=e, sem=self.newsem("s_" + nm), cnt=0, waited={})
        self.nuniq = 0
        self.dcount = {}
        self.freed = []
        self.stack = [self.es]
        self.scoped = []

    def newsem(self, name):
        h = self.es.enter_context(self.nc.semaphore(name))
        self.sems.append(h)
        return len(self.sems) - 1

    def barrier(self):
        deps = {}
        for E in self.eng.values():
            deps[E["sem"]] = E["cnt"]
        deps.update(self.dcount)
        for en in self.eng:
            self._wait(en, {k: v for k, v in deps.items() if v > 0})

    @contextlib.contextmanager
    def scope(self):
        st = contextlib.ExitStack()
        self.stack.append(st)
        self.scoped.append([])
        try:
            yield
        finally:
            self.barrier()
            for b in self.scoped.pop():
                if b.sem is not None:
                    self.freed.append(b.sem)
                    b.sem = None
            self.stack.pop()
            st.close()

    def sb(self, name, shape, dt):
        self.nuniq += 1
        t = self.stack[-1].enter_context(self.nc.sbuf_tensor("%s_%d" % (name, self.nuniq), list(shape), dt))
        b = Buf(name, t[:])
        if self.scoped:
            self.scoped[-1].append(b)
        return b

    def ps(self, name, shape, dt=F32):
        t = self.es.enter_context(self.nc.psum_tensor(name, list(shape), dt))
        return Buf(name, t[:])

    def dram(self, name, shape, dt, kind="Internal"):
        t = self.nc.dram_tensor(name, list(shape), dt, kind=kind)
        return Buf(name, t.ap(), space="dram")

    def _deps(self, r, w):
        deps = {}
        for b in r:
            for k, v in b.w.items():
                if deps.get(k, 0) < v:
                    deps[k] = v
        for b in w:
            for dd in (b.w, b.r):
                for k, v in dd.items():
                    if deps.get(k, 0) < v:
                        deps[k] = v
        return deps

    def _wait(self, en, deps):
        E = self.eng[en]
        for k, v in deps.items():
            if E["waited"].get(k, 0) >= v:
                continue
            E["e"].wait_ge(self.sems[k], v)
            E["waited"][k] = v

    def _done(self, tok, r, w):
        k, v = tok
        for b in r:
            if b.r.get(k, 0) < v:
                b.r[k] = v
        for b in w:
            if b.w.get(k, 0) < v:
                b.w[k] = v
            b.r = {}

    def op(self, en, fn, r=(), w=()):
        deps = self._deps(r, w)
        if en == "pe":
            deps.pop(self.eng["pe"]["sem"], None)
        self._wait(en, deps)
        E = self.eng[en]
        ins = fn(E["e"])
        E["cnt"] += 1
        ins.then_inc(self.sems[E["sem"]], 1)
        self._done((E["sem"], E["cnt"]), r, w)

    def dma(self, q, out, in_, r=(), w=()):
        self._wait(q, self._deps(r, w))
        E = self.eng[q]
        ins = E["e"].dma_start(out=out, in_=in_)
        d = next((b for b in list(w) + list(r) if b.space == "sb"), w[0])
        if d.sem is None:
            if self.freed:
                d.sem = self.freed.pop()
            else:
                d.sem = self.newsem("d%d" % len(self.sems))
        c = self.dcount.get(d.sem, 0) + 16
        self.dcount[d.sem] = c
        ins.then_inc(self.sems[d.sem], 16)
        self._done((d.sem, c), r, w)

    def mm(self, out, pairs, r=(), w=(), start=True, stop=True):
        def fn(pe):
            n = len(pairs)
            ins = None
            for i, (l, rh) in enumerate(pairs):
                ins = pe.matmul(out, lhsT=l, rhs=rh, start=(start and i == 0), stop=(stop and i == n - 1))
            return ins
        self.op("pe", fn, r, w)

    def slots(self, base, n):
        out = [Buf("%s.%d" % (base.name, i), None) for i in range(n)]
        if self.scoped:
            self.scoped[-1].extend(out)
        return out

    @staticmethod
    def _merge(dst, src):
        for k, v in src.items():
            if dst.get(k, 0) < v:
                dst[k] = v

    def join(self, dst, srcs):
        for b in srcs:
            self._merge(dst.w, b.w)
            self._merge(dst.r, b.r)

    def fork(self, src, dsts):
        for d in dsts:
            self._merge(d.w, src.w)
            self._merge(d.r, src.r)

    def finish(self, outs):
        deps = {}
        for b in outs:
            for k, v in b.w.items():
                deps[k] = max(deps.get(k, 0), v)
        self._wait("sp", deps)


def packv(v):
    v = np.asarray(v, np.float32).reshape(-1, 128)
    return np.ascontiguousarray(v.T)


class VecPack:
    def __init__(self):
        self.off = {}
        self.n = 0
        self.cols = []

    def add(self, key, arr128xn):
        a = np.asarray(arr128xn, np.float32)
        assert a.shape[0] == 128
        self.off[key] = (self.n, a.shape[1])
        self.n += a.shape[1]
        self.cols.append(a)

    def array(self):
        return np.ascontiguousarray(np.concatenate(self.cols, axis=1))


class Prog:
    def __init__(self, segs_lat, segs_ctx, voff, nv, layers=range(DEPTH)):
        self.kb = KB()
        kb = self.kb
        self.segs_lat = segs_lat
        self.segs_ctx = segs_ctx
        self.ntok = (segs_ctx[-1][1] if segs_ctx else segs_lat[-1][1])
        self.voff = voff
        self.nv = nv
        self.din = {}
        self.vec = kb.sb("vec", [128, nv], F32)
        self.ones_bf = kb.sb("ones_bf", [128, 128], BF16)
        self.ones_f = kb.sb("ones_f", [128, 128], F32)
        self.P = [kb.ps("P%d" % i, [128, 512], F32) for i in range(8)]

    def inp(self, name, shape, dt=F32):
        b = self.kb.dram(name, shape, dt, kind="ExternalInput")
        self.din[name] = b
        return b

    def v(self, key, lo=0, n=None):
        o, w = self.voff[key]
        if n is None:
            n = w - lo
        return self.vec.a[:, o + lo:o + lo + n]

    def prologue(self):
        kb = self.kb
        vecd = self.inp("vecs", [128, self.nv])
        kb.dma("sp", self.vec.a, vecd.a, r=[vecd], w=[self.vec])
        kb.op("dve", lambda e: e.memset(self.ones_bf.a, 1.0), w=[self.ones_bf])
        kb.op("dve", lambda e: e.memset(self.ones_f.a, 1.0), w=[self.ones_f])
        self.mod = kb.sb("mod", [128, DEPTH, 3, 48], F32)
        self.gs = kb.sb("gs", [128, DEPTH, 3, 2, 8], F32)

    def stage_mod(self, layers):
        kb = self.kb
        with kb.scope():
            sc = kb.sb("sc", [128, 8, 4], F32)
            kb.op("act", lambda e: e.activation(sc.a[:, :, 0:3], self.v("cT").rearrange("p (k r) -> p k r", r=3),
                                                AF.Silu), r=[self.vec], w=[sc])
            wts = [kb.sb("wm%d" % i, [128, 8, 512], F32) for i in range(2)]
            it = 0
            for l in layers:
                wd = self.inp("w_mod%d" % l, [128, 8, 6144])
                pm = self.P[l % 2]
                for cb in range(12):
                    wt = wts[it % 2]
                    it += 1
                    kb.dma("sp", wt.a, wd.a[:, :, cb * 512:(cb + 1) * 512], r=[wd], w=[wt])
                    for jj in range(4):
                        j = cb * 4 + jj
                        kb.mm(pm.a[:, j * 4:j * 4 + 3],
                              [(wt.a[:, k, jj * 128:(jj + 1) * 128], sc.a[:, k, 0:3]) for k in range(8)],
                              r=[wt, sc], w=[pm])
                for r in range(3):
                    kb.op("dve", lambda e, r=r, l=l, pm=pm: e.tensor_tensor(
                        self.mod.a[:, l, r, :], pm.a[:, 0:192].rearrange("p (j r) -> p j r", r=4)[:, :, r],
                        self.v("bmod%d" % l), op=ALU.add), r=[pm, self.vec], w=[self.mod])
                    for h, (c0, gk) in enumerate(((8, "n1g%d" % l), (32, "n2g%d" % l))):
                        kb.op("dve", lambda e, r=r, l=l, h=h, c0=c0, gk=gk: e.scalar_tensor_tensor(
                            self.gs.a[:, l, r, h, :], self.mod.a[:, l, r, c0:c0 + 8], 1.0, self.v(gk),
                            op0=ALU.add, op1=ALU.mult), r=[self.mod, self.vec], w=[self.gs])

    def rstd_of(self, x3, n, sq, pss, rstd, rbufs, eps=RMS_EPS, nfeat=D, ones=None, nk=8):
        kb = self.kb
        ones = ones if ones is not None else self.ones_bf
        kb.op("act", lambda e: e.activation(sq.a[:, 0:nk, 0:n], x3, AF.Square), r=rbufs, w=[sq])
        kb.mm(pss.a[:, 0:n], [(ones.a, sq.a[:, k, 0:n]) for k in range(nk)], r=[ones, sq], w=[pss])
        kb.op("act", lambda e: e.activation(rstd.a[:, 0:n], pss.a[:, 0:n], AF.Sqrt, bias=eps, scale=1.0 / nfeat),
              r=[pss], w=[rstd])
        kb.op("dve", lambda e: e.reciprocal(rstd.a[:, 0:n], rstd.a[:, 0:n]), r=[rstd], w=[rstd])

    def modulate(self, x3, n, rstd, tmp, out, gsc, shift, rbufs):
        kb = self.kb
        kb.op("dve", lambda e: e.tensor_tensor(tmp.a[:, :, 0:n], x3,
                                               rstd.a[:, 0:n].unsqueeze(1).to_broadcast([128, 8, n]), op=ALU.mult),
              r=rbufs + [rstd], w=[tmp])
        for k in range(8):
            if k % 2 == 0:
                kb.op("act", lambda e, k=k: e.activation(out.a[:, k, 0:n], tmp.a[:, k, 0:n], AF.Identity,
                                                         bias=shift[:, k:k + 1], scale=gsc[:, k:k + 1]),
                      r=[tmp, self.mod, self.gs], w=[out])
            else:
                kb.op("pool", lambda e, k=k: e.tensor_scalar(out.a[:, k, 0:n], tmp.a[:, k, 0:n], gsc[:, k:k + 1],
                                                             shift[:, k:k + 1], op0=ALU.mult, op1=ALU.add),
                      r=[tmp, self.mod, self.gs], w=[out])

    def load_w_bf(self, dst, src, nk, ncols, step=1024):
        kb = self.kb
        for k in range(nk):
            for c0 in range(0, ncols, step):
                c1 = min(ncols, c0 + step)
                kb.dma("pool", dst.a[:, k, c0:c1], src.a[:, k, c0:c1], r=[src], w=[dst])

    def tiles(self, segs, T):
        for (s0, s1, row) in segs:
            for t0 in range(s0, s1, T):
                yield s0, s1, row, t0

    def post_mixer(self, l, row, hn, T, dst, fdst, t0, bufs):
        kb = self.kb
        sq, tmp, rstd, fbf = bufs
        self.rstd_of(hn.a[:, :, 0:T], T, sq, self.P[1], rstd, [hn])
        self.modulate(hn.a[:, :, 0:T], T, rstd, tmp, fbf, self.gs.a[:, l, row, 1, :], self.mod.a[:, l, row, 24:32], [hn])
        kb.dma("sp", dst.a[:, :, t0:t0 + T], hn.a[:, :, 0:T], r=[hn], w=[dst])
        kb.dma("sp", fdst.a[:, :, t0:t0 + T], fbf.a[:, :, 0:T], r=[fbf], w=[fdst])

    def stage_sconv(self, l, src, dst, fdst, segs):
        kb = self.kb
        T, H = 256, 1
        W = T + 2 * H
        j = l // 3
        with kb.scope():
            win = kb.sb("win", [128, 8, 3072], BF16)
            wout = kb.sb("wout", [128, 8, 1024], BF16)
            self.load_w_bf(win, self.inp("sc_w_in%d" % j, [128, 8, 3072]), 8, 3072)
            self.load_w_bf(wout, self.inp("sc_w_out%d" % j, [128, 8, 1024]), 8, 1024)
            xts = [kb.sb("xt%d" % i, [128, 8, W], F32) for i in range(2)]
            abfs = [kb.sb("abf%d" % i, [128, 8, W], BF16) for i in range(2)]
            gTs = [kb.sb("gT%d" % i, [128, 8, T], BF16) for i in range(2)]
            hns = [kb.sb("hn%d" % i, [128, 8, T], F32) for i in range(2)]
            fbfs = [kb.sb("fbf%d" % i, [128, 8, T], BF16) for i in range(2)]
            sq = kb.sb("sq", [128, 8, W], BF16)
            tmp = kb.sb("tmp", [128, 8, W], F32)
            rstd = kb.sb("rstd", [128, W], F32)
            rstd2 = kb.sb("rstd2", [128, W], F32)
            csbs = [kb.sb("csb%d" % i, [128, W], F32) for i in range(2)]
            cus = [kb.sb("cu%d" % i, [128, W], F32) for i in range(2)]
            accs = [kb.sb("acc%d" % i, [128, T], F32) for i in range(2)]
            scw = self.v("scw%d" % j)
            P = self.P
            for it, (s0, s1, row, t0) in enumerate(self.tiles(segs, T)):
                lo, hi = max(t0 - H, s0), min(t0 + T + H, s1)
                n = hi - lo
                off = lo - (t0 - H)
                xt, abf, gT, hn, fbf = xts[it % 2], abfs[it % 2], gTs[it % 2], hns[it % 2], fbfs[it % 2]
                kb.dma("sp", xt.a[:, :, off:off + n], src.a[:, :, lo:hi], r=[src], w=[xt])
                x3 = xt.a[:, :, off:off + n]
                self.rstd_of(x3, n, sq, P[0], rstd, [xt])
                self.modulate(x3, n, rstd, tmp, abf, self.gs.a[:, l, row, 0, :], self.mod.a[:, l, row, 0:8], [xt])
                for ch in range(8):
                    pc, pu, pb = P[2 + ch % 2], P[4 + ch % 2], P[6 + ch % 2]
                    csb, cu, acc = csbs[ch % 2], cus[ch % 2], accs[ch % 2]
                    for (pp, cc) in ((pc, 8 + ch), (pu, 16 + ch), (pb, ch)):
                        kb.mm(pp.a[:, 0:n], [(win.a[:, k, cc * 128:(cc + 1) * 128], abf.a[:, k, 0:n]) for k in range(8)],
                              r=[win, abf], w=[pp])
                    kb.op("act", lambda e, csb=csb, pc=pc: e.copy(csb.a[:, 0:n], pc.a[:, 0:n]), r=[pc], w=[csb])
                    if off > 0:
                        kb.op("pool", lambda e, cu=cu: e.memset(cu.a[:, 0:off], 0.0), w=[cu])
                    if off + n < W:
                        kb.op("pool", lambda e, cu=cu: e.memset(cu.a[:, off + n:W], 0.0), w=[cu])
                    kb.op("dve", lambda e, cu=cu, csb=csb, pu=pu: e.tensor_tensor(
                        cu.a[:, off:off + n], csb.a[:, 0:n], pu.a[:, 0:n], op=ALU.mult), r=[csb, pu], w=[cu])
                    kb.op("pool", lambda e, cu=cu, acc=acc: e.tensor_scalar(
                        acc.a, cu.a[:, 0:T], scw[:, ch:ch + 1], None, op0=ALU.mult), r=[cu, self.vec], w=[acc])
                    for tap in (1, 2):
                        kb.op("dve", lambda e, cu=cu, acc=acc, tap=tap: e.scalar_tensor_tensor(
                            acc.a, cu.a[:, tap:tap + T], scw[:, tap * 8 + ch:tap * 8 + ch + 1], acc.a,
                            op0=ALU.mult, op1=ALU.add), r=[cu, self.vec, acc], w=[acc])
                    c0 = H - off
                    kb.op("dve", lambda e, acc=acc, pb=pb, gT=gT, ch=ch, c0=c0: e.tensor_tensor(
                        gT.a[:, ch, :], acc.a, pb.a[:, c0:c0 + T], op=ALU.mult), r=[acc, pb], w=[gT])
                for dm in range(8):
                    py = P[2 + dm % 6]
                    kb.mm(py.a[:, 0:T], [(wout.a[:, k, dm * 128:(dm + 1) * 128], gT.a[:, k, :]) for k in range(8)],
                          r=[wout, gT], w=[py])
                    kb.op("dve", lambda e, py=py, dm=dm, hn=hn, xt=xt: e.scalar_tensor_tensor(
                        hn.a[:, dm, :], py.a[:, 0:T], self.mod.a[:, l, row, 16 + dm:17 + dm], xt.a[:, dm, H:H + T],
                        op0=ALU.mult, op1=ALU.add), r=[py, self.mod, xt], w=[hn])
                self.post_mixer(l, row, hn, T, dst, fdst, t0, (sq, tmp, rstd2, fbf))


    def stage_conformer(self, l, src, dst, fdst, segs):
        kb = self.kb
        T, H = 256, 15
        W = T + 2 * H
        P = self.P
        with kb.scope():
            w1 = kb.sb("w1", [128, 8, 2048], BF16)
            w2 = kb.sb("w2", [128, 8, 1024], BF16)
            self.load_w_bf(w1, self.inp("cf_w_pw1", [128, 8, 2048]), 8, 2048)
            self.load_w_bf(w2, self.inp("cf_w_pw2", [128, 8, 1024]), 8, 1024)
            xts = [kb.sb("xt%d" % i, [128, 8, W], F32) for i in range(2)]
            abf = kb.sb("abf", [128, 8, W], BF16)
            sq = kb.sb("sq", [128, 8, W], BF16)
            tmp = kb.sb("tmp", [128, 8, W], F32)
            rstd = kb.sb("rstd", [128, W], F32)
            rstd2 = kb.sb("rstd2", [128, W], F32)
            sgs = [kb.sb("sg%d" % i, [128, W], F32) for i in range(2)]
            us = [kb.sb("u%d" % i, [128, W], F32) for i in range(8)]
            uc = kb.sb("uc", [128, 8, T], F32)
            ucs = kb.slots(uc, 8)
            ucq = kb.sb("ucq", [128, 8, T], F32)
            mean = kb.sb("mean", [128, T], F32)
            var = kb.sb("var", [128, T], F32)
            sT = kb.sb("sT", [128, 8, T], BF16)
            hns = [kb.sb("hn%d" % i, [128, 8, T], F32) for i in range(2)]
            fbfs = [kb.sb("fbf%d" % i, [128, 8, T], BF16) for i in range(2)]
            yt = kb.sb("yt", [128, T], F32)
            b1, dw, dwb = self.v("cfb1"), self.v("cfdw"), self.v("cfdwb")
            lng, lnb, b2 = self.v("cflng"), self.v("cflnb"), self.v("cfb2")
            for it, (s0, s1, row, t0) in enumerate(self.tiles(segs, T)):
                lo, hi = max(t0 - H, s0), min(t0 + T + H, s1)
                n = hi - lo
                off = lo - (t0 - H)
                xt, hn, fbf = xts[it % 2], hns[it % 2], fbfs[it % 2]
                kb.dma("sp", xt.a[:, :, off:off + n], src.a[:, :, lo:hi], r=[src], w=[xt])
                x3 = xt.a[:, :, off:off + n]
                self.rstd_of(x3, n, sq, P[0], rstd, [xt])
                self.modulate(x3, n, rstd, tmp, abf, self.gs.a[:, l, row, 0, :], self.mod.a[:, l, row, 0:8], [xt])
                for ch in range(8):
                    pa, pg = P[2 + ch % 2], P[4 + ch % 2]
                    sg, u = sgs[ch % 2], us[ch]
                    for (pp, cc) in ((pa, ch), (pg, 8 + ch)):
                        kb.mm(pp.a[:, 0:n], [(w1.a[:, k, cc * 128:(cc + 1) * 128], abf.a[:, k, 0:n]) for k in range(8)],
                              r=[w1, abf], w=[pp])
                    kb.op("act", lambda e, sg=sg, pg=pg, ch=ch: e.activation(sg.a[:, 0:n], pg.a[:, 0:n], AF.Sigmoid,
                                                                             bias=b1[:, 8 + ch:9 + ch]), r=[pg, self.vec], w=[sg])
                    if off > 0:
                        kb.op("pool", lambda e, u=u: e.memset(u.a[:, 0:off], 0.0), w=[u])
                    if off + n < W:
                        kb.op("pool", lambda e, u=u: e.memset(u.a[:, off + n:W], 0.0), w=[u])
                    kb.op("dve", lambda e, u=u, pa=pa, sg=sg, ch=ch: e.scalar_tensor_tensor(
                        u.a[:, off:off + n], pa.a[:, 0:n], b1[:, ch:ch + 1], sg.a[:, 0:n], op0=ALU.add, op1=ALU.mult),
                        r=[pa, sg, self.vec], w=[u])
                kb.fork(uc, ucs)
                for ch in range(8):
                    u = us[ch]
                    kb.op("dve", lambda e: e.tensor_scalar(
                        uc.a[:, ch, :], u.a[:, 0:T], dw[:, ch:ch + 1], dwb[:, ch:ch + 1], op0=ALU.mult, op1=ALU.add),
                        r=[u, self.vec], w=[ucs[ch]])
                for tap in range(1, 31):
                    for ch in range(8):
                        u = us[ch]
                        kb.op("dve", lambda e: e.scalar_tensor_tensor(
                            uc.a[:, ch, :], u.a[:, tap:tap + T], dw[:, tap * 8 + ch:tap * 8 + ch + 1], uc.a[:, ch, :],
                            op0=ALU.mult, op1=ALU.add), r=[u, self.vec, ucs[ch]], w=[ucs[ch]])
                kb.join(uc, ucs)
                kb.op("act", lambda e: e.activation(ucq.a, uc.a, AF.Square), r=[uc], w=[ucq])
                kb.mm(P[6].a[:, 0:T], [(self.ones_f.a, uc.a[:, k, :]) for k in range(8)], r=[self.ones_f, uc], w=[P[6]])
                kb.mm(P[7].a[:, 0:T], [(self.ones_f.a, ucq.a[:, k, :]) for k in range(8)], r=[self.ones_f, ucq], w=[P[7]])
                kb.op("act", lambda e: e.activation(mean.a, P[6].a[:, 0:T], AF.Identity, scale=1.0 / D), r=[P[6]], w=[mean])
                kb.op("dve", lambda e: e.tensor_tensor(var.a, mean.a, mean.a, op=ALU.mult), r=[mean], w=[var])
                kb.op("dve", lambda e: e.scalar_tensor_tensor(var.a, P[7].a[:, 0:T], 1.0 / D, var.a,
                                                              op0=ALU.mult, op1=ALU.subtract), r=[P[7], var], w=[var])
                kb.op("act", lambda e: e.activation(var.a, var.a, AF.Sqrt, bias=LN_EPS, scale=1.0), r=[var], w=[var])
                kb.op("dve", lambda e: e.reciprocal(var.a, var.a), r=[var], w=[var])
                kb.op("dve", lambda e: e.tensor_tensor(uc.a, uc.a, mean.a.unsqueeze(1).to_broadcast([128, 8, T]),
                                                       op=ALU.subtract), r=[uc, mean], w=[uc])
                kb.op("dve", lambda e: e.tensor_tensor(uc.a, uc.a, var.a.unsqueeze(1).to_broadcast([128, 8, T]),
                                                       op=ALU.mult), r=[uc, var], w=[uc])
                for k in range(8):
                    kb.op("act", lambda e, k=k: e.activation(sT.a[:, k, :], uc.a[:, k, :], AF.Silu,
                                                             bias=lnb[:, k:k + 1], scale=lng[:, k:k + 1]),
                          r=[uc, self.vec], w=[sT])
                for dm in range(8):
                    py = P[2 + dm % 4]
                    kb.mm(py.a[:, 0:T], [(w2.a[:, k, dm * 128:(dm + 1) * 128], sT.a[:, k, :]) for k in range(8)],
                          r=[w2, sT], w=[py])
                    kb.op("dve", lambda e, py=py, dm=dm: e.tensor_scalar(
                        yt.a, py.a[:, 0:T], b2[:, dm:dm + 1], self.mod.a[:, l, row, 16 + dm:17 + dm],
                        op0=ALU.add, op1=ALU.mult), r=[py, self.vec, self.mod], w=[yt])
                    kb.op("dve", lambda e, dm=dm, hn=hn, xt=xt: e.tensor_tensor(
                        hn.a[:, dm, :], yt.a, xt.a[:, dm, H:H + T], op=ALU.add), r=[yt, xt], w=[hn])
                self.post_mixer(l, row, hn, T, dst, fdst, t0, (sq, tmp, rstd2, fbf))

    def stage_attn(self, l, src, dst, fdst, segs_lat, segs_ctx, lambda_init):
        kb = self.kb
        P = self.P
        T = 512
        nlat = segs_lat[-1][1]
        ntok = segs_ctx[-1][1]
        qTd = kb.dram("a_qT", [128, 8, nlat], BF16)
        kTd = kb.dram("a_kT", [128, 8, ntok], BF16)
        vTd = kb.dram("a_v", [ntok // 128, 128, 1024], BF16)
        oTd = kb.dram("a_oT", [128, 8, nlat], BF16)
        with kb.scope():
            wqkv = kb.sb("wqkv", [128, 8, 3072], BF16)
            self.load_w_bf(wqkv, self.inp("da_w_qkv", [128, 8, 3072]), 8, 3072)
            cosT = kb.sb("cosT", [128, NLAT], F32)
            sinT = kb.sb("sinT", [128, NLAT], F32)
            blk = kb.sb("blk", [128, 128], BF16)
            prot = kb.sb("prot", [128, 128], F32)
            kb.dma("sp", cosT.a, self.inp("ropecos", [128, NLAT]).a, r=[self.din["ropecos"]], w=[cosT])
            kb.dma("sp", sinT.a, self.inp("ropesin", [128, NLAT]).a, r=[self.din["ropesin"]], w=[sinT])
            kb.dma("pool", blk.a, self.inp("blkones", [128, 128]).a, r=[self.din["blkones"]], w=[blk])
            kb.dma("sp", prot.a, self.inp("protm", [128, 128]).a, r=[self.din["protm"]], w=[prot])
            xts = [kb.sb("xt%d" % i, [128, 8, T], F32) for i in range(2)]
            abf = kb.sb("abf", [128, 8, T], BF16)
            sq = kb.sb("sq", [128, 8, T], BF16)
            tmp = kb.sb("tmp", [128, 8, T], F32)
            rstd = kb.sb("rstd", [128, T], F32)
            sqh = kb.sb("sqh", [128, T], BF16)
            rs = kb.sb("rs", [128, T], F32)
            qn = kb.sb("qn", [128, T], F32)
            t1 = kb.sb("t1", [128, T], F32)
            t2 = kb.sb("t2", [128, T], F32)
            qks = [kb.sb("qk%d" % i, [128, 8, T], BF16) for i in range(2)]
            vsb = kb.sb("vsb", [128, 1024], BF16)
            for it, (s0, s1, row, t0) in enumerate(self.tiles(list(segs_lat) + list(segs_ctx), T)):
                n = min(T, s1 - t0)
                is_lat = t0 < nlat
                pos0 = t0 - s0
                xt = xts[it % 2]
                kb.dma("sp", xt.a[:, :, 0:n], src.a[:, :, t0:t0 + n], r=[src], w=[xt])
                x3 = xt.a[:, :, 0:n]
                self.rstd_of(x3, n, sq, P[0], rstd, [xt])
                self.modulate(x3, n, rstd, tmp, abf, self.gs.a[:, l, row, 0, :], self.mod.a[:, l, row, 0:8], [xt])
                for qi, (base, gk, dd, stage) in enumerate(((0, "qg", qTd, qks[0]), (8, "kg", kTd, qks[1]))):
                    if qi == 0 and not is_lat:
                        continue
                    for h in range(8):
                        pq, pss, pr = P[2 + h % 2], P[4 + h % 2], P[6 + h % 2]
                        cc = base + h
                        kb.mm(pq.a[:, 0:n], [(wqkv.a[:, k, cc * 128:(cc + 1) * 128], abf.a[:, k, 0:n]) for k in range(8)],
                              r=[wqkv, abf], w=[pq])
                        kb.op("act", lambda e, pq=pq: e.activation(sqh.a[:, 0:n], pq.a[:, 0:n], AF.Square), r=[pq], w=[sqh])
                        kb.mm(pss.a[:, 0:n], [(blk.a, sqh.a[:, 0:n])], r=[blk, sqh], w=[pss])
                        kb.op("act", lambda e, pss=pss: e.activation(rs.a[:, 0:n], pss.a[:, 0:n], AF.Sqrt, bias=RMS_EPS,
                                                                     scale=1.0 / 64), r=[pss], w=[rs])
                        kb.op("dve", lambda e: e.reciprocal(rs.a[:, 0:n], rs.a[:, 0:n]), r=[rs], w=[rs])
                        kb.op("dve", lambda e, pq=pq, gk=gk: e.scalar_tensor_tensor(
                            qn.a[:, 0:n], pq.a[:, 0:n], self.v(gk), rs.a[:, 0:n], op0=ALU.mult, op1=ALU.mult),
                            r=[pq, rs, self.vec], w=[qn])
                        if is_lat:
                            kb.mm(pr.a[:, 0:n], [(prot.a, qn.a[:, 0:n])], r=[prot, qn], w=[pr])
                            kb.op("pool", lambda e: e.tensor_tensor(t1.a[:, 0:n], qn.a[:, 0:n], cosT.a[:, pos0:pos0 + n],
                                                                    op=ALU.mult), r=[qn, cosT], w=[t1])
                            kb.op("dve", lambda e, pr=pr: e.tensor_tensor(t2.a[:, 0:n], pr.a[:, 0:n], sinT.a[:, pos0:pos0 + n],
                                                                          op=ALU.mult), r=[pr, sinT], w=[t2])
                            kb.op("dve", lambda e, stage=stage, h=h: e.tensor_tensor(stage.a[:, h, 0:n], t1.a[:, 0:n], t2.a[:, 0:n],
                                                                                     op=ALU.add), r=[t1, t2], w=[stage])
                        else:
                            kb.op("act", lambda e, stage=stage, h=h: e.copy(stage.a[:, h, 0:n], qn.a[:, 0:n]), r=[qn], w=[stage])
                    kb.dma("sp", dd.a[:, :, t0:t0 + n], stage.a[:, :, 0:n], r=[stage], w=[dd])
                for sub in range(n // 128):
                    for half in range(2):
                        pv = P[2 + half]
                        kb.mm(pv.a, [(abf.a[:, k, sub * 128:(sub + 1) * 128],
                                      wqkv.a[:, k, 2048 + half * 512:2048 + (half + 1) * 512]) for k in range(8)],
                              r=[abf, wqkv], w=[pv])
                        if half == 0:
                            kb.op("act", lambda e, pv=pv: e.copy(vsb.a[:, 0:512], pv.a), r=[pv], w=[vsb])
                        else:
                            kb.op("dve", lambda e, pv=pv: e.tensor_copy(vsb.a[:, 512:1024], pv.a), r=[pv], w=[vsb])
                    kb.dma("sp", vTd.a[(t0 + sub * 128) // 128], vsb.a, r=[vsb], w=[vTd])
        with kb.scope():
            lam = kb.sb("lam", [128, 4], F32)
            lt = kb.sb("lt", [128, 64], F32)
            for ii, (ka, kb_) in enumerate((("lq1", "lk1"), ("lq2", "lk2"))):
                kb.op("dve", lambda e, ka=ka, kb_=kb_: e.tensor_tensor(lt.a, self.v(ka), self.v(kb_), op=ALU.mult),
                      r=[self.vec], w=[lt])
                kb.op("dve", lambda e, ii=ii: e.tensor_reduce(out=lam.a[:, ii:ii + 1], in_=lt.a, axis=AX.X, op=ALU.add),
                      r=[lt], w=[lam])
            kb.op("act", lambda e: e.activation(lam.a[:, 0:2], lam.a[:, 0:2], AF.Exp), r=[lam], w=[lam])
            kb.op("dve", lambda e: e.tensor_tensor(lam.a[:, 2:3], lam.a[:, 1:2], lam.a[:, 0:1], op=ALU.subtract), r=[lam], w=[lam])
            kb.op("dve", lambda e: e.tensor_scalar(lam.a[:, 2:3], lam.a[:, 2:3], -float(lambda_init), None, op0=ALU.add),
                  r=[lam], w=[lam])
            kb.op("dve", lambda e: e.tensor_scalar(lam.a[:, 3:4], self.v("subg"), 1.0 - float(lambda_init), None, op0=ALU.mult),
                  r=[self.vec], w=[lam])
            kts = [kb.sb("kt%d" % i, [128, NLAT + NCTX], BF16) for i in range(2)]
            vts = [kb.sb("vt%d" % i, [128, 18, 128], BF16) for i in range(2)]
            qts = [kb.sb("qt%d" % i, [128, T], BF16) for i in range(2)]
            pTs = [kb.sb("pT%d" % i, [128, T], BF16) for i in range(4)]
            rz = kb.sb("rz", [128, 2, T], F32)
            o1 = kb.sb("o1", [128, T], F32)
            o2 = kb.sb("o2", [128, T], F32)
            osq = kb.sb("osq", [128, T], BF16)
            ors = kb.sb("ors", [128, T], F32)
            ofs = [kb.sb("of%d" % i, [128, T], BF16) for i in range(2)]
            scale = 64 ** -0.5
            nit = 0
            npt = 0
            for bi, ((l0, l1, _), (c0, c1, _)) in enumerate(zip(segs_lat, segs_ctx)):
                nkl = (l1 - l0) // 128
                nkc = (c1 - c0) // 128
                nk = nkl + nkc
                for h in range(8):
                    kt, vt = kts[(bi * 8 + h) % 2], vts[(bi * 8 + h) % 2]
                    kb.dma("sp", kt.a[:, 0:l1 - l0], kTd.a[:, h, l0:l1], r=[kTd], w=[kt])
                    kb.dma("sp", kt.a[:, l1 - l0:l1 - l0 + c1 - c0], kTd.a[:, h, c0:c1], r=[kTd], w=[kt])
                    kb.dma("sp", vt.a[:, 0:nkl, :], vTd.a[l0 // 128:l1 // 128, :, h * 128:(h + 1) * 128].rearrange("c p d -> p c d"),
                           r=[vTd], w=[vt])
                    kb.dma("sp", vt.a[:, nkl:nk, :], vTd.a[c0 // 128:c1 // 128, :, h * 128:(h + 1) * 128].rearrange("c p d -> p c d"),
                           r=[vTd], w=[vt])
                    for q0 in range(l0, l1, T):
                        qt = qts[nit % 2]
                        of = ofs[nit % 2]
                        nit += 1
                        kb.dma("sp", qt.a, qTd.a[:, h, q0:q0 + T], r=[qTd], w=[qt])
                        pend = []

                        def emit_pv(item):
                            kc, comp, pT = item
                            kb.mm(P[comp].a, [(vt.a[:, kc, :], pT.a)], r=[vt, pT], w=[P[comp]],
                                  start=(kc == 0), stop=(kc == nk - 1))
                            kb.mm(P[2 + comp].a, [(self.ones_bf.a, pT.a)], r=[self.ones_bf, pT], w=[P[2 + comp]],
                                  start=(kc == 0), stop=(kc == nk - 1))

                        for kc in range(nk):
                            for comp in range(2):
                                pS = P[4 + npt % 4]
                                pT = pTs[npt % 4]
                                npt += 1
                                kb.mm(pS.a, [(kt.a[comp * 64:(comp + 1) * 64, kc * 128:(kc + 1) * 128],
                                              qt.a[comp * 64:(comp + 1) * 64, :])], r=[kt, qt], w=[pS])
                                kb.op("act", lambda e, pS=pS, pT=pT: e.activation(pT.a, pS.a, AF.Exp, scale=scale), r=[pS], w=[pT])
                                pend.append((kc, comp, pT))
                                if len(pend) > 2:
                                    emit_pv(pend.pop(0))
                        while pend:
                            emit_pv(pend.pop(0))
                        for comp in range(2):
                            kb.op("dve", lambda e, comp=comp: e.reciprocal(rz.a[:, comp, :], P[2 + comp].a), r=[P[2 + comp]], w=[rz])
                        kb.op("dve", lambda e: e.tensor_tensor(o1.a, P[0].a, rz.a[:, 0, :], op=ALU.mult), r=[P[0], rz], w=[o1])
                        kb.op("dve", lambda e: e.tensor_tensor(o2.a, P[1].a, rz.a[:, 1, :], op=ALU.mult), r=[P[1], rz], w=[o2])
                        kb.op("dve", lambda e: e.scalar_tensor_tensor(o1.a, o2.a, lam.a[:, 2:3], o1.a, op0=ALU.mult, op1=ALU.add),
                              r=[o1, o2, lam], w=[o1])
                        kb.op("act", lambda e: e.activation(osq.a, o1.a, AF.Square), r=[o1], w=[osq])
                        pss = P[4 + npt % 4]
                        npt += 1
                        kb.mm(pss.a, [(self.ones_bf.a, osq.a)], r=[self.ones_bf, osq], w=[pss])
                        kb.op("act", lambda e, pss=pss: e.activation(ors.a, pss.a, AF.Sqrt, bias=RMS_EPS, scale=1.0 / 128),
                              r=[pss], w=[ors])
                        kb.op("dve", lambda e: e.reciprocal(ors.a, ors.a), r=[ors], w=[ors])
                        kb.op("dve", lambda e, of=of: e.scalar_tensor_tensor(of.a, o1.a, lam.a[:, 3:4], ors.a, op0=ALU.mult, op1=ALU.mult),
                              r=[o1, lam, ors], w=[of])
                        kb.dma("sp", oTd.a[:, h, q0:q0 + T], of.a, r=[of], w=[oTd])
        with kb.scope():
            wo = kb.sb("wo", [128, 8, 1024], BF16)
            self.load_w_bf(wo, self.inp("da_w_o", [128, 8, 1024]), 8, 1024)
            T3 = 256
            xts = [kb.sb("xt%d" % i, [128, 8, T3], F32) for i in range(2)]
            ots = [kb.sb("ot%d" % i, [128, 8, T3], BF16) for i in range(2)]
            hns = [kb.sb("hn%d" % i, [128, 8, T3], F32) for i in range(2)]
            fbfs = [kb.sb("fbf%d" % i, [128, 8, T3], BF16) for i in range(2)]
            sq = kb.sb("sq", [128, 8, T3], BF16)
            tmp = kb.sb("tmp", [128, 8, T3], F32)
            rstd2 = kb.sb("rstd2", [128, T3], F32)
            for it, (s0, s1, row, t0) in enumerate(self.tiles(segs_lat, T3)):
                xt, ot, hn, fbf = xts[it % 2], ots[it % 2], hns[it % 2], fbfs[it % 2]
                kb.dma("sp", xt.a, src.a[:, :, t0:t0 + T3], r=[src], w=[xt])
                kb.dma("sp", ot.a, oTd.a[:, :, t0:t0 + T3], r=[oTd], w=[ot])
                for dm in range(8):
                    py = P[2 + dm % 6]
                    kb.mm(py.a[:, 0:T3], [(wo.a[:, k, dm * 128:(dm + 1) * 128], ot.a[:, k, :]) for k in range(8)],
                          r=[wo, ot], w=[py])
                    kb.op("dve", lambda e, py=py, dm=dm, hn=hn, xt=xt: e.scalar_tensor_tensor(
                        hn.a[:, dm, :], py.a[:, 0:T3], self.mod.a[:, l, row, 16 + dm:17 + dm], xt.a[:, dm, :],
                        op0=ALU.mult, op1=ALU.add), r=[py, self.mod, xt], w=[hn])
                self.post_mixer(l, row, hn, T3, dst, fdst, t0, (sq, tmp, rstd2, fbf))

    def peer_convert(self, l):
        kb = self.kb
        uT = self.inp("peer_uT%d" % l, [128, 8, PEER_E])
        vv = self.inp("peer_v%d" % l, [128, 128, 1024])
        ubf = kb.dram("ubf%d" % l, [128, 8, PEER_E], BF16)
        vbf = kb.dram("vbf%d" % l, [128, 128, 1024], BF16)
        for k in range(8):
            for c in range(0, PEER_E, 4096):
                kb.dma("pool", ubf.a[:, k, c:c + 4096], uT.a[:, k, c:c + 4096], r=[uT], w=[ubf])
        for i0 in range(0, 128, 4):
            kb.dma("pool", vbf.a[:, i0:i0 + 4, :], vv.a[:, i0:i0 + 4, :], r=[vv], w=[vbf])
        return ubf, vbf

    def stage_peer_q(self, l, fsrc, qTd, t_lo, t_hi):
        kb = self.kb
        T = 512
        with kb.scope():
            wq = kb.sb("wq", [128, 8, 2048], BF16)
            self.load_w_bf(wq, self.inp("peer_wq%d" % l, [128, 8, 2048]), 8, 2048)
            fts = [kb.sb("ft%d" % i, [128, 8, T], BF16) for i in range(2)]
            qts = [kb.sb("qt%d" % i, [128, 16, T], BF16) for i in range(2)]
            for it, t0 in enumerate(range(t_lo, t_hi, T)):
                n = min(T, t_hi - t0)
                ft, qt = fts[it % 2], qts[it % 2]
                kb.dma("sp", ft.a[:, :, 0:n], fsrc.a[:, :, t0:t0 + n], r=[fsrc], w=[ft])
                for c in range(16):
                    pq = self.P[c % 8]
                    kb.mm(pq.a[:, 0:n], [(wq.a[:, k, c * 128:(c + 1) * 128], ft.a[:, k, 0:n]) for k in range(8)],
                          r=[wq, ft], w=[pq])
                    en = "act" if c % 2 == 0 else "dve"
                    if en == "act":
                        kb.op("act", lambda e, c=c, pq=pq, qt=qt: e.copy(qt.a[:, c, 0:n], pq.a[:, 0:n]), r=[pq], w=[qt])
                    else:
                        kb.op("dve", lambda e, c=c, pq=pq, qt=qt: e.tensor_copy(qt.a[:, c, 0:n], pq.a[:, 0:n]), r=[pq], w=[qt])
                kb.dma("sp", qTd.a[:, :, t0:t0 + n], qt.a[:, :, 0:n], r=[qt], w=[qTd])

    def stage_peer_route(self, l, qTd, Wd, t_lo, t_hi):
        kb = self.kb
        P = self.P
        NEG = -1.0e30
        with kb.scope():
            KT = kb.sb("KT", [128, 16, 128], BF16)
            kb.dma("pool", KT.a, self.inp("peer_kT%d" % l, [128, 16, 128]).a, r=[self.din["peer_kT%d" % l]], w=[KT])
            ident = kb.sb("ident", [128, 128], BF16)
            if "ident" not in self.din:
                self.inp("ident", [128, 128])
            kb.dma("pool", ident.a, self.din["ident"].a, r=[self.din["ident"]], w=[ident])
            qts = [kb.sb("rq%d" % i, [128, 16, 128], BF16) for i in range(2)]
            S = kb.sb("S", [128, 16, 128], F32)
            Ss = kb.slots(S, 16)
            Sx = kb.sb("Sx", [128, 16, 128], F32)
            Sxs = kb.slots(Sx, 16)
            M = kb.sb("M", [128, 16, 16], F32)
            Ms = kb.slots(M, 32)
            cand = kb.sb("cand", [128, 8, 256], F32)
            cands = kb.slots(cand, 8)
            Cx = kb.sb("Cx", [128, 8, 256], F32)
            Cxs = kb.slots(Cx, 8)
            C16 = kb.sb("C16", [128, 8, 16], F32)
            C16s = kb.slots(C16, 16)
            sm = kb.sb("sm", [128, 8, 16], F32)
            Zs = kb.sb("Zs", [128, 8], F32)
            thr = kb.sb("thr", [128, 8], F32)
            E1 = kb.sb("E1", [128, 8, 16], F32)
            cc = kb.sb("cc", [128, 8, 16], F32)
            E2 = kb.sb("E2", [128, 8, 128], F32)
            tms = [kb.sb("tm%d" % i, [128, 128], F32) for i in range(8)]
            R = kb.sb("R", [128, 128, 128], BF16)
            Rs = kb.slots(R, 128)
            Ws = kb.slots(R, 32)
            OH = kb.sb("OH", [128, 128, 128], BF16)
            OHs = kb.slots(OH, 8)
            RT = kb.sb("RT", [128, 128, 128], BF16)
            RTs = kb.slots(RT, 32)
            OHT = kb.sb("OHT", [128, 128, 128], BF16)
            OHTs = kb.slots(OHT, 32)
            S4 = S.a.rearrange("p (h two) n -> p h two n", two=2)
            M4 = M.a.rearrange("p (h two) n -> p h two n", two=2)
            nb = 0
            for it, t0 in enumerate(range(t_lo, t_hi, 128)):
                g = t0 // 128
                qt = qts[it % 2]
                kb.dma("sp", qt.a, qTd.a[:, :, t0:t0 + 128], r=[qTd], w=[qt])
                kb.fork(S, Ss)
                for b in range(4):
                    pb = P[nb % 8]
                    nb += 1
                    for cI in range(4):
                        c = b * 4 + cI
                        kb.mm(pb.a[:, cI * 128:(cI + 1) * 128], [(qt.a[:, c, :], KT.a[:, c, :])], r=[qt, KT], w=[pb])
                    kb.op("act", lambda e: e.copy(S.a[:, b * 4:(b + 1) * 4, :].rearrange("p c n -> p (c n)"), pb.a),
                          r=[pb], w=Ss[b * 4:(b + 1) * 4])
                kb.fork(M, Ms)
                for c in range(16):
                    kb.op("dve", lambda e: e.max(out=M.a[:, c, 0:8], in_=S.a[:, c, :]), r=[Ss[c]], w=[Ms[2 * c]])
                for c in range(16):
                    kb.op("dve", lambda e: e.match_replace(out=Sx.a[:, c, :], in_to_replace=M.a[:, c, 0:8],
                                                           in_values=S.a[:, c, :], imm_value=NEG),
                          r=[Ss[c], Ms[2 * c]], w=[Sxs[c]])
                for c in range(16):
                    kb.op("dve", lambda e: e.max(out=M.a[:, c, 8:16], in_=Sx.a[:, c, :]), r=[Sxs[c]], w=[Ms[2 * c + 1]])
                kb.join(S, Ss)
                kb.join(M, Ms)
                for h in range(8):
                    kb.op("pool", lambda e: e.tensor_tensor(
                        cand.a[:, h, :].rearrange("p (a b) -> p a b", a=16),
                        M.a[:, 2 * h, :].unsqueeze(2).to_broadcast([128, 16, 16]),
                        M.a[:, 2 * h + 1, :].unsqueeze(1).to_broadcast([128, 16, 16]), op=ALU.add),
                        r=Ms[4 * h:4 * h + 4], w=[cands[h]])
                kb.fork(C16, C16s)
                for h in range(8):
                    kb.op("dve", lambda e: e.max(out=C16.a[:, h, 0:8], in_=cand.a[:, h, :]), r=[cands[h]], w=[C16s[2 * h]])
                for h in range(8):
                    kb.op("dve", lambda e: e.match_replace(out=Cx.a[:, h, :], in_to_replace=C16.a[:, h, 0:8],
                                                           in_values=cand.a[:, h, :], imm_value=NEG),
                          r=[cands[h], C16s[2 * h]], w=[Cxs[h]])
                for h in range(8):
                    kb.op("dve", lambda e: e.max(out=C16.a[:, h, 8:16], in_=Cx.a[:, h, :]), r=[Cxs[h]], w=[C16s[2 * h + 1]])
                kb.join(C16, C16s)
                kb.op("dve", lambda e: e.tensor_tensor(sm.a, C16.a, C16.a[:, :, 0:1].to_broadcast([128, 8, 16]),
                                                       op=ALU.subtract), r=[C16], w=[sm])
                kb.op("act", lambda e: e.activation(sm.a, sm.a, AF.Exp), r=[sm], w=[sm])
                kb.op("dve", lambda e: e.tensor_reduce(out=Zs.a, in_=sm.a, axis=AX.X, op=ALU.add), r=[sm], w=[Zs])
                kb.op("dve", lambda e: e.reciprocal(Zs.a, Zs.a), r=[Zs], w=[Zs])
                kb.op("dve", lambda e: e.scalar_tensor_tensor(thr.a, C16.a[:, :, 15], -1.0, C16.a[:, :, 0],
                                                              op0=ALU.mult, op1=ALU.max), r=[C16], w=[thr])
                kb.op("dve", lambda e: e.scalar_tensor_tensor(thr.a, thr.a, -2.0e-5, C16.a[:, :, 15],
                                                              op0=ALU.mult, op1=ALU.add), r=[thr, C16], w=[thr])
                kb.op("dve", lambda e: e.tensor_tensor(E1.a, M4[:, :, 0, :], M4[:, :, 0, 0:1].to_broadcast([128, 8, 16]),
                                                       op=ALU.subtract), r=[M], w=[E1])
                kb.op("act", lambda e: e.activation(E1.a, E1.a, AF.Exp), r=[E1], w=[E1])
                kb.op("dve", lambda e: e.tensor_tensor(E1.a, E1.a, Zs.a.unsqueeze(2).to_broadcast([128, 8, 16]),
                                                       op=ALU.mult), r=[E1, Zs], w=[E1])
                kb.op("dve", lambda e: e.scalar_tensor_tensor(cc.a, M4[:, :, 0, :], -1.0,
                                                              thr.a.unsqueeze(2).to_broadcast([128, 8, 16]),
                                                              op0=ALU.mult, op1=ALU.add), r=[M, thr], w=[cc])
                kb.op("dve", lambda e: e.tensor_tensor(E2.a, S4[:, :, 1, :], M4[:, :, 1, 0:1].to_broadcast([128, 8, 128]),
                                                       op=ALU.subtract), r=[S, M], w=[E2])
                kb.op("act", lambda e: e.activation(E2.a, E2.a, AF.Exp), r=[E2], w=[E2])
                kb.fork(R, Rs)
                kb.fork(OH, OHs)
                for h in range(8):
                    kb.op("dve", lambda e: e.tensor_tensor(
                        OH.a[:, h * 16:(h + 1) * 16, :],
                        S.a[:, 2 * h, :].unsqueeze(1).to_broadcast([128, 16, 128]),
                        M.a[:, 2 * h, :].unsqueeze(2).to_broadcast([128, 16, 128]), op=ALU.is_equal),
                        r=[S, M], w=[OHs[h]])
                    for r in range(16):
                        c = h * 16 + r
                        tm = tms[c % 8]
                        kb.op("dve", lambda e: e.scalar_tensor_tensor(
                            tm.a, S.a[:, 2 * h + 1, :], cc.a[:, h, r:r + 1], E2.a[:, h, :], op0=ALU.is_ge, op1=ALU.mult),
                            r=[S, cc, E2], w=[tm])
                        kb.op("act", lambda e: e.activation(
                            R.a[:, c, :], tm.a, AF.Identity, scale=E1.a[:, h, r:r + 1]), r=[tm, E1], w=[Rs[c]])
                kb.join(R, Rs)
                kb.join(OH, OHs)
                kb.fork(RT, RTs)
                kb.fork(OHT, OHTs)
                for (srcb, dstb, dsl) in ((R, RT, RTs), (OH, OHT, OHTs)):
                    for j0 in range(0, 128, 4):
                        pb = P[nb % 8]
                        nb += 1
                        for jj in range(4):
                            kb.mm(pb.a[:, jj * 128:(jj + 1) * 128], [(srcb.a[:, :, j0 + jj], ident.a)],
                                  r=[srcb, ident], w=[pb])
                        dv = dstb.a[:, j0:j0 + 4, :].rearrange("p j t -> p (j t)")
                        if nb % 2 == 0:
                            kb.op("act", lambda e: e.copy(dv, pb.a), r=[pb], w=[dsl[j0 // 4]])
                        else:
                            kb.op("dve", lambda e: e.tensor_copy(dv, pb.a), r=[pb], w=[dsl[j0 // 4]])
                kb.join(RT, RTs)
                kb.join(OHT, OHTs)
                kb.fork(R, Ws)
                for tq in range(0, 128, 4):
                    pb = P[nb % 8]
                    nb += 1
                    for tt in range(4):
                        kb.mm(pb.a[:, tt * 128:(tt + 1) * 128], [(RT.a[:, :, tq + tt], OHT.a[:, :, tq + tt])],
                              r=[RT, OHT], w=[pb])
                    dv = R.a[:, :, tq:tq + 4].rearrange("p i t -> p t i")
                    sv = pb.a.rearrange("p (t i) -> p t i", t=4)
                    if nb % 2 == 0:
                        kb.op("act", lambda e: e.copy(dv, sv), r=[pb], w=[Ws[tq // 4]])
                    else:
                        kb.op("dve", lambda e: e.tensor_copy(dv, sv), r=[pb], w=[Ws[tq // 4]])
                kb.join(R, Ws)
                kb.dma("sp", Wd.a[g], R.a, r=[R], w=[Wd])

    def stage_peer_experts(self, l, ubf, vbf, fsrc, hsrc, hdst, Wd, segs, dst_off=0):
        kb = self.kb
        P = self.P
        T = 256
        NBG = 4
        with kb.scope():
            uts = [kb.sb("ut%d" % i, [128, 8, NBG * 128], BF16) for i in range(3)]
            vts = [kb.sb("vt%d" % i, [128, NBG, 1024], BF16) for i in range(3)]
            wts = [kb.sb("wt%d" % i, [128, 2, NBG, 128], BF16) for i in range(3)]
            fts = [kb.sb("eft%d" % i, [128, 8, T], BF16) for i in range(2)]
            hts = [kb.sb("eht%d" % i, [128, 8, T], F32) for i in range(2)]
            gzs = [kb.sb("gz%d" % i, [128, T], F32) for i in range(3)]
            As = [kb.sb("A%d" % i, [128, T], BF16) for i in range(4)]
            nslot = 0
            nz = 0
            LOOK = 2
            for ig, (s0, s1, row, t0) in enumerate(self.tiles(segs, T)):
                ft, ht = fts[ig % 2], hts[ig % 2]
                kb.dma("sp", ft.a, fsrc.a[:, :, t0:t0 + T], r=[fsrc], w=[ft])
                kb.dma("sp", ht.a, hsrc.a[:, :, t0:t0 + T], r=[hsrc], w=[ht])
                g0 = t0 // 128
                pend = []

                def emit_out(item):
                    i, vt, b, A = item
                    for dm in range(8):
                        po = P[dm // 2]
                        kb.mm(po.a[:, (dm % 2) * T:(dm % 2 + 1) * T], [(vt.a[:, b, dm * 128:(dm + 1) * 128], A.a)],
                              r=[vt, A], w=[po], start=(i == 0), stop=(i == 127))

                for bg in range(128 // NBG):
                    ut, vt, wt = uts[nslot % 3], vts[nslot % 3], wts[nslot % 3]
                    nslot += 1
                    if not (getattr(self, "dbg_noload", False) and nslot > 3):
                        kb.dma("sp", ut.a, ubf.a[:, :, bg * NBG * 128:(bg + 1) * NBG * 128], r=[ubf], w=[ut])
                        kb.dma(getattr(self, "vq", "pool"), vt.a, vbf.a[:, bg * NBG:(bg + 1) * NBG, :], r=[vbf], w=[vt])
                        for tl in range(2):
                            kb.dma("sp", wt.a[:, tl, :, :], Wd.a[g0 + tl][:, bg * NBG:(bg + 1) * NBG, :], r=[Wd], w=[wt])
                    for b in range(NBG):
                        i = bg * NBG + b
                        pz = P[4 + nz % 4]
                        gz, A = gzs[nz % 3], As[nz % 4]
                        nz += 1
                        kb.mm(pz.a[:, 0:T], [(ut.a[:, k, b * 128:(b + 1) * 128], ft.a[:, k, :]) for k in range(8)],
                              r=[ut, ft], w=[pz])
                        kb.op("act", lambda e: e.activation(gz.a, pz.a[:, 0:T], AF.Gelu), r=[pz], w=[gz])
                        kb.op("dve", lambda e: e.tensor_tensor(
                            A.a.rearrange("p (a t) -> p a t", a=2), gz.a.rearrange("p (a t) -> p a t", a=2),
                            wt.a[:, :, b, :], op=ALU.mult), r=[gz, wt], w=[A])
                        pend.append((i, vt, b, A))
                        if len(pend) > LOOK:
                            emit_out(pend.pop(0))
                while pend:
                    emit_out(pend.pop(0))
                for dm in range(8):
                    po = P[dm // 2]
                    kb.op("dve", lambda e, dm=dm, po=po, ht=ht: e.scalar_tensor_tensor(
                        ht.a[:, dm, :], po.a[:, (dm % 2) * T:(dm % 2 + 1) * T], self.mod.a[:, l, row, 40 + dm:41 + dm],
                        ht.a[:, dm, :], op0=ALU.mult, op1=ALU.add), r=[po, self.mod, ht], w=[ht])
                kb.dma("sp", hdst.a[:, :, t0 - dst_off:t0 - dst_off + T], ht.a, r=[ht], w=[hdst])

def kmaj(w):
    w = np.asarray(w, np.float32)
    nk = w.shape[0] // 128
    return np.ascontiguousarray(w.reshape(nk, 128, w.shape[1]).transpose(1, 0, 2))


def tok_to_T(X):
    X = np.asarray(X)
    return np.ascontiguousarray(X.T.reshape(8, 128, X.shape[0]).transpose(1, 0, 2))


def T_to_tok(hT):
    return np.ascontiguousarray(hT.transpose(1, 0, 2).reshape(1024, hT.shape[2]).T)


def build_vecs(inp, crows):
    vp = VecPack()
    cT = np.stack([packv(crows[r]) for r in range(3)], axis=2)
    vp.add("cT", cT.reshape(128, 24))
    for l in range(DEPTH):
        vp.add("bmod%d" % l, packv(inp["b_mod"][l]))
        vp.add("n1g%d" % l, packv(inp["norm1_g"][l]))
        vp.add("n2g%d" % l, packv(inp["norm2_g"][l]))
    for j in range(inp["sc_conv_w"].shape[0]):
        cw = inp["sc_conv_w"][j]
        vp.add("scw%d" % j, np.concatenate([packv(cw[k]) for k in range(3)], axis=1))
    if "cf_b_pw1" in inp:
        vp.add("cfb1", packv(inp["cf_b_pw1"][0]))
        dw = inp["cf_dw_w"][0]
        vp.add("cfdw", np.concatenate([packv(dw[k]) for k in range(dw.shape[0])], axis=1))
        for key, nm in (("cf_dw_b", "cfdwb"), ("cf_ln_g", "cflng"), ("cf_ln_b", "cflnb"), ("cf_b_pw2", "cfb2")):
            vp.add(nm, packv(inp[key][0]))
    if "da_q_norm_g" in inp:
        rep = lambda v: np.ascontiguousarray(np.broadcast_to(np.asarray(v, np.float32)[None, :], (128, len(v))))
        vp.add("qg", np.tile(np.asarray(inp["da_q_norm_g"][0], np.float32), 2).reshape(128, 1))
        vp.add("kg", np.tile(np.asarray(inp["da_k_norm_g"][0], np.float32), 2).reshape(128, 1))
        vp.add("subg", np.asarray(inp["da_subln_g"][0], np.float32).reshape(128, 1))
        for key, nm in (("da_lam_q1", "lq1"), ("da_lam_k1", "lk1"), ("da_lam_q2", "lq2"), ("da_lam_k2", "lk2")):
            vp.add(nm, rep(inp[key][0]))
    return vp


def host_consts():
    c = {}
    c["ident"] = np.eye(128, dtype=np.float32)
    blk = np.zeros((128, 128), np.float32)
    blk[:64, :64] = 1.0
    blk[64:, 64:] = 1.0
    c["blkones"] = blk
    prot = np.zeros((128, 128), np.float32)
    cosT = np.zeros((128, NLAT), np.float32)
    sinT = np.zeros((128, NLAT), np.float32)
    t = np.arange(NLAT)
    pos = (np.floor_divide(t, 64).astype(np.float32), np.mod(t, 64).astype(np.float32))
    nf = 16
    inv_freq = (10000.0 ** (-np.arange(nf, dtype=np.float32) / nf)).astype(np.float32)
    for p in range(128):
        dh = p % 64
        axis, half, f = dh // 32, (dh % 32) // 16, dh % 16
        ang = pos[axis] * inv_freq[f]
        cosT[p] = np.cos(ang)
        sinT[p] = np.sin(ang)
        if half == 0:
            prot[p + 16, p] = -1.0
        else:
            prot[p - 16, p] = 1.0
    c["protm"] = prot
    c["ropecos"] = cosT
    c["ropesin"] = sinT
    return c


SEGS_LAT = [(0, NLAT, 0), (NLAT, 2 * NLAT, 1)]
SEGS_CTX = [(2 * NLAT, 2 * NLAT + NCTX, 2), (2 * NLAT + NCTX, 2 * NLAT + 2 * NCTX, 2)]
NTOK = 2 * NLAT + 2 * NCTX


def build_program(voff, nv):
    pg = Prog(SEGS_LAT, SEGS_CTX, voff, nv)
    kb = pg.kb
    pg.prologue()
    pg.stage_mod(range(DEPTH))
    xT = pg.inp("xT", [128, 8, NTOK])
    hX = kb.dram("hX", [128, 8, NTOK], F32)
    hM = kb.dram("hM", [128, 8, NTOK], F32)
    fT = kb.dram("fT", [128, 8, NTOK], BF16)
    qTd = kb.dram("p_qT", [128, 16, NTOK], BF16)
    Wd = kb.dram("p_W", [NTOK // 128, 128, 128, 128], BF16)
    yT = kb.dram("yT", [128, 8, 2 * NLAT], F32, kind="ExternalOutput")
    tabs = {0: pg.peer_convert(0)}
    for l in range(DEPTH):
        src = xT if l == 0 else hX
        kind = l % 3
        if kind == 0:
            segs = SEGS_LAT + (SEGS_CTX if l == 0 else [])
            pg.stage_sconv(l, src, hM, fT, segs)
        elif kind == 1:
            pg.stage_attn(l, src, hM, fT, SEGS_LAT, SEGS_CTX, 0.8 - 0.6 * math.exp(-0.3 * l))
        else:
            pg.stage_conformer(l, src, hM, fT, SEGS_LAT)
        if l + 1 < DEPTH:
            tabs[l + 1] = pg.peer_convert(l + 1)
        psegs = SEGS_LAT + (SEGS_CTX if l == 0 else [])
        t_hi = psegs[-1][1]
        pg.stage_peer_q(l, fT, qTd, 0, t_hi)
        pg.stage_peer_route(l, qTd, Wd, 0, t_hi)
        ubf, vbf = tabs[l]
        pg.stage_peer_experts(l, ubf, vbf, fT, hM, yT if l == DEPTH - 1 else hX, Wd, psegs)
    kb.finish([yT])
    return pg


def kernel(**inputs):
    inp = {k: np.asarray(v) for k, v in inputs.items()}
    ncore = 8
    consts = host_consts()
    shared = dict(consts)
    for l in range(DEPTH):
        shared["w_mod%d" % l] = kmaj(inp["w_mod"][l])
        shared["peer_uT%d" % l] = kmaj(inp["peer_u"][l].T)
        shared["peer_v%d" % l] = np.ascontiguousarray(inp["peer_v"][l].reshape(128, 128, D).transpose(1, 0, 2))
        shared["peer_wq%d" % l] = kmaj(inp["peer_w_query"][l])
        shared["peer_kT%d" % l] = np.ascontiguousarray(inp["peer_sub_keys"][l].reshape(16, 128, 128).transpose(2, 0, 1))
    for j in range(inp["sc_w_in"].shape[0]):
        shared["sc_w_in%d" % j] = kmaj(inp["sc_w_in"][j])
        shared["sc_w_out%d" % j] = kmaj(inp["sc_w_out"][j])
    shared["da_w_qkv"] = kmaj(inp["da_w_qkv"][0])
    shared["da_w_o"] = kmaj(inp["da_w_o"][0])
    shared["cf_w_pw1"] = kmaj(inp["cf_w_pw1"][0])
    shared["cf_w_pw2"] = kmaj(inp["cf_w_pw2"][0])
    in_maps = []
    pg = None
    for c in range(ncore):
        b0, b1 = 2 * c, 2 * c + 1
        crows = np.stack([inp["c"][b0], inp["c"][b1], inp["c_ctx"]], axis=0)
        vp = build_vecs(inp, crows)
        if pg is None:
            pg = build_program(vp.off, vp.n)
        X = np.concatenate([inp["x"][b0], inp["x"][b1], inp["ctx"][b0], inp["ctx"][b1]], axis=0)
        m = dict(shared)
        m["vecs"] = vp.array()
        m["xT"] = tok_to_T(X)
        in_maps.append({k: m[k] for k in pg.din})
    res = run_bass_kernel_spmd(pg.kb.nc, in_maps, core_ids=list(range(ncore)))
    out = np.empty((2 * ncore, NLAT, D), np.float32)
    for c in range(ncore):
        Y = T_to_tok(np.asarray(res.results[c]["yT"]))
        out[2 * c] = Y[:NLAT]
        out[2 * c + 1] = Y[NLAT:]
    return out
```

```python
import contextlib
import math
import numpy as np
import concourse.bass as bass
import concourse.mybir as mybir
from concourse.bass_utils import run_bass_kernel_spmd

F32 = mybir.dt.float32
BF16 = mybir.dt.bfloat16
ALU = mybir.AluOpType
AF = mybir.ActivationFunctionType
AX = mybir.AxisListType

D = 1024
KC = 8
NLAT = 2048
NCTX = 256
NB = 2
DEPTH = 4
RMS_EPS = 1e-6
LN_EPS = 1e-5
PEER_E = 16384


class Buf:
    def __init__(self, name, a=None, space="sb"):
        self.name = name
        self.a = a
        self.space = space
        self.w = {}
        self.r = {}
        self.sem = None
        self.cnt = 0


class KB:
    def __init__(self):
        self.nc = bass.Bass("TRN2", target_bir_lowering=False)
        self.es = contextlib.ExitStack()
        nc = self.nc
        self.sems = []
        self.eng = {}
        for nm, e in (("pe", nc.tensor), ("act", nc.scalar), ("dve", nc.vector),
                      ("pool", nc.gpsimd), ("sp", nc.sync)):
            self.eng[nm] = dict(e=e, sem=self.newsem("s_" + nm), cnt=0, waited={})
        self.nuniq = 0
        self.dcount = {}
        self.freed = []
        self.stack = [self.es]
        self.scoped = []

    def newsem(self, name):
        h = self.es.enter_context(self.nc.semaphore(name))
        self.sems.append(h)
        return len(self.sems) - 1

    def barrier(self):
        deps = {}
        for E in self.eng.values():
            deps[E["sem"]] = E["cnt"]
        deps.update(self.dcount)
        for en in self.eng:
            self._wait(en, {k: v for k, v in deps.items() if v > 0})

    @contextlib.contextmanager
    def scope(self):
        st = contextlib.ExitStack()
        self.stack.append(st)
        self.scoped.append([])
        try:
            yield
        finally:
            self.barrier()
            for b in self.scoped.pop():
                if b.sem is not None:
                    self.freed.append(b.sem)
                    b.sem = None
            self.stack.pop()
            st.close()

    def sb(self, name, shape, dt):
        self.nuniq += 1
        t = self.stack[-1].enter_context(self.nc.sbuf_tensor("%s_%d" % (name, self.nuniq), list(shape), dt))
        b = Buf(name, t[:])
        if self.scoped:
            self.scoped[-1].append(b)
        return b

    def ps(self, name, shape, dt=F32):
        t = self.es.enter_context(self.nc.psum_tensor(name, list(shape), dt))
        return Buf(name, t[:])

    def dram(self, name, shape, dt, kind="Internal"):
        t = self.nc.dram_tensor(name, list(shape), dt, kind=kind)
        return Buf(name, t.ap(), space="dram")

    def _deps(self, r, w):
        deps = {}
        for b in r:
            for k, v in b.w.items():
                if deps.get(k, 0) < v:
                    deps[k] = v
        for b in w:
            for dd in (b.w, b.r):
                for k, v in dd.items():
                    if deps.get(k, 0) < v:
                        deps[k] = v
        return deps

    def _wait(self, en, deps):
        E = self.eng[en]
        for k, v in deps.items():
            if E["waited"].get(k, 0) >= v:
                continue
            E["e"].wait_ge(self.sems[k], v)
            E["waited"][k] = v

    def _done(self, tok, r, w):
        k, v = tok
        for b in r:
            if b.r.get(k, 0) < v:
                b.r[k] = v
        for b in w:
            if b.w.get(k, 0) < v:
                b.w[k] = v
            b.r = {}

    def op(self, en, fn, r=(), w=()):
        deps = self._deps(r, w)
        if en == "pe":
            deps.pop(self.eng["pe"]["sem"], None)
        self._wait(en, deps)
        E = self.eng[en]
        ins = fn(E["e"])
        E["cnt"] += 1
        ins.then_inc(self.sems[E["sem"]], 1)
        self._done((E["sem"], E["cnt"]), r, w)

    def dma(self, q, out, in_, r=(), w=()):
        self._wait(q, self._deps(r, w))
        E = self.eng[q]
        ins = E["e"].dma_start(out=out, in_=in_)
        d = next((b for b in list(w) + list(r) if b.space == "sb"), w[0])
        if d.sem is None:
            if self.freed:
                d.sem = self.freed.pop()
            else:
                d.sem = self.newsem("d%d" % len(self.sems))
        c = self.dcount.get(d.sem, 0) + 16
        self.dcount[d.sem] = c
        ins.then_inc(self.sems[d.sem], 16)
        self._done((d.sem, c), r, w)

    def mm(self, out, pairs, r=(), w=(), start=True, stop=True):
        def fn(pe):
            n = len(pairs)
            ins = None
            for i, (l, rh) in enumerate(pairs):
                ins = pe.matmul(out, lhsT=l, rhs=rh, start=(start and i == 0), stop=(stop and i == n - 1))
            return ins
        self.op("pe", fn, r, w)

    def slots(self, base, n):
        out = [Buf("%s.%d" % (base.name, i), None) for i in range(n)]
        if self.scoped:
            self.scoped[-1].extend(out)
        return out

    @staticmethod
    def _merge(dst, src):
        for k, v in src.items():
            if dst.get(k, 0) < v:
                dst[k] = v

    def join(self, dst, srcs):
        for b in srcs:
            self._merge(dst.w, b.w)
            self._merge(dst.r, b.r)

    def fork(self, src, dsts):
        for d in dsts:
            self._merge(d.w, src.w)
            self._merge(d.r, src.r)

    def finish(self, outs):
        deps = {}
        for b in outs:
            for k, v in b.w.items():
                deps[k] = max(deps.get(k, 0), v)
        self._wait("sp", deps)


def packv(v):
    v = np.asarray(v, np.float32).reshape(-1, 128)
    return np.ascontiguousarray(v.T)


class VecPack:
    def __init__(self):
        self.off = {}
        self.n = 0
        self.cols = []

    def add(self, key, arr128xn):
        a = np.asarray(arr128xn, np.float32)
        assert a.shape[0] == 128
        self.off[key] = (self.n, a.shape[1])
        self.n += a.shape[1]
        self.cols.append(a)

    def array(self):
        return np.ascontiguousarray(np.concatenate(self.cols, axis=1))


class Prog:
    def __init__(self, segs_lat, segs_ctx, voff, nv, layers=range(DEPTH)):
        self.kb = KB()
        kb = self.kb
        self.segs_lat = segs_lat
        self.segs_ctx = segs_ctx
        self.ntok = (segs_ctx[-1][1] if segs_ctx else segs_lat[-1][1])
        self.voff = voff
        self.nv = nv
        self.din = {}
        self.vec = kb.sb("vec", [128, nv], F32)
        self.ones_bf = kb.sb("ones_bf", [128, 128], BF16)
        self.ones_f = kb.sb("ones_f", [128, 128], F32)
        self.P = [kb.ps("P%d" % i, [128, 512], F32) for i in range(8)]

    def inp(self, name, shape, dt=F32):
        b = self.kb.dram(name, shape, dt, kind="ExternalInput")
        self.din[name] = b
        return b

    def v(self, key, lo=0, n=None):
        o, w = self.voff[key]
        if n is None:
            n = w - lo
        return self.vec.a[:, o + lo:o + lo + n]

    def prologue(self):
        kb = self.kb
        vecd = self.inp("vecs", [128, self.nv])
        kb.dma("sp", self.vec.a, vecd.a, r=[vecd], w=[self.vec])
        kb.op("dve", lambda e: e.memset(self.ones_bf.a, 1.0), w=[self.ones_bf])
        kb.op("dve", lambda e: e.memset(self.ones_f.a, 1.0), w=[self.ones_f])
        self.mod = kb.sb("mod", [128, DEPTH, 3, 48], F32)
        self.gs = kb.sb("gs", [128, DEPTH, 3, 2, 8], F32)

    def stage_mod(self, layers):
        kb = self.kb
        with kb.scope():
            sc = kb.sb("sc", [128, 8, 4], F32)
            kb.op("act", lambda e: e.activation(sc.a[:, :, 0:3], self.v("cT").rearrange("p (k r) -> p k r", r=3),
                                                AF.Silu), r=[self.vec], w=[sc])
            wts = [kb.sb("wm%d" % i, [128, 8, 512], F32) for i in range(2)]
            it = 0
            for l in layers:
                wd = self.inp("w_mod%d" % l, [128, 8, 6144])
                pm = self.P[l % 2]
                for cb in range(12):
                    wt = wts[it % 2]
                    it += 1
                    kb.dma("sp", wt.a, wd.a[:, :, cb * 512:(cb + 1) * 512], r=[wd], w=[wt])
                    for jj in range(4):
                        j = cb * 4 + jj
                        kb.mm(pm.a[:, j * 4:j * 4 + 3],
                              [(wt.a[:, k, jj * 128:(jj + 1) * 128], sc.a[:, k, 0:3]) for k in range(8)],
                              r=[wt, sc], w=[pm])
                for r in range(3):
                    kb.op("dve", lambda e, r=r, l=l, pm=pm: e.tensor_tensor(
                        self.mod.a[:, l, r, :], pm.a[:, 0:192].rearrange("p (j r) -> p j r", r=4)[:, :, r],
                        self.v("bmod%d" % l), op=ALU.add), r=[pm, self.vec], w=[self.mod])
                    for h, (c0, gk) in enumerate(((8, "n1g%d" % l), (32, "n2g%d" % l))):
                        kb.op("dve", lambda e, r=r, l=l, h=h, c0=c0, gk=gk: e.scalar_tensor_tensor(
                            self.gs.a[:, l, r, h, :], self.mod.a[:, l, r, c0:c0 + 8], 1.0, self.v(gk),
                            op0=ALU.add, op1=ALU.mult), r=[self.mod, self.vec], w=[self.gs])

    def rstd_of(self, x3, n, sq, pss, rstd, rbufs, eps=RMS_EPS, nfeat=D, ones=None, nk=8):
        kb = self.kb
        ones = ones if ones is not None else self.ones_bf
        kb.op("act", lambda e: e.activation(sq.a[:, 0:nk, 0:n], x3, AF.Square), r=rbufs, w=[sq])
        kb.mm(pss.a[:, 0:n], [(ones.a, sq.a[:, k, 0:n]) for k in range(nk)], r=[ones, sq], w=[pss])
        kb.op("act", lambda e: e.activation(rstd.a[:, 0:n], pss.a[:, 0:n], AF.Sqrt, bias=eps, scale=1.0 / nfeat),
              r=[pss], w=[rstd])
        kb.op("dve", lambda e: e.reciprocal(rstd.a[:, 0:n], rstd.a[:, 0:n]), r=[rstd], w=[rstd])

    def modulate(self, x3, n, rstd, tmp, out, gsc, shift, rbufs):
        kb = self.kb
        kb.op("dve", lambda e: e.tensor_tensor(tmp.a[:, :, 0:n], x3,
                                               rstd.a[:, 0:n].unsqueeze(1).to_broadcast([128, 8, n]), op=ALU.mult),
              r=rbufs + [rstd], w=[tmp])
        for k in range(8):
            if k % 2 == 0:
                kb.op("act", lambda e, k=k: e.activation(out.a[:, k, 0:n], tmp.a[:, k, 0:n], AF.Identity,
                                                         bias=shift[:, k:k + 1], scale=gsc[:, k:k + 1]),
                      r=[tmp, self.mod, self.gs], w=[out])
            else:
                kb.op("pool", lambda e, k=k: e.tensor_scalar(out.a[:, k, 0:n], tmp.a[:, k, 0:n], gsc[:, k:k + 1],
                                                             shift[:, k:k + 1], op0=ALU.mult, op1=ALU.add),
                      r=[tmp, self.mod, self.gs], w=[out])

    def load_w_bf(self, dst, src, nk, ncols, step=1024):
        kb = self.kb
        for k in range(nk):
            for c0 in range(0, ncols, step):
                c1 = min(ncols, c0 + step)
                kb.dma("pool", dst.a[:, k, c0:c1], src.a[:, k, c0:c1], r=[src], w=[dst])

    def tiles(self, segs, T):
        for (s0, s1, row) in segs:
            for t0 in range(s0, s1, T):
                yield s0, s1, row, t0

    def post_mixer(self, l, row, hn, T, dst, fdst, t0, bufs):
        kb = self.kb
        sq, tmp, rstd, fbf = bufs
        self.rstd_of(hn.a[:, :, 0:T], T, sq, self.P[1], rstd, [hn])
        self.modulate(hn.a[:, :, 0:T], T, rstd, tmp, fbf, self.gs.a[:, l, row, 1, :], self.mod.a[:, l, row, 24:32], [hn])
        kb.dma("sp", dst.a[:, :, t0:t0 + T], hn.a[:, :, 0:T], r=[hn], w=[dst])
        kb.dma("sp", fdst.a[:, :, t0:t0 + T], fbf.a[:, :, 0:T], r=[fbf], w=[fdst])

    def stage_sconv(self, l, src, dst, fdst, segs):
        kb = self.kb
        T, H = 256, 1
        W = T + 2 * H
        j = l // 3
        with kb.scope():
            win = kb.sb("win", [128, 8, 3072], BF16)
            wout = kb.sb("wout", [128, 8, 1024], BF16)
            self.load_w_bf(win, self.inp("sc_w_in%d" % j, [128, 8, 3072]), 8, 3072)
            self.load_w_bf(wout, self.inp("sc_w_out%d" % j, [128, 8, 1024]), 8, 1024)
            xts = [kb.sb("xt%d" % i, [128, 8, W], F32) for i in range(2)]
            abfs = [kb.sb("abf%d" % i, [128, 8, W], BF16) for i in range(2)]
            gTs = [kb.sb("gT%d" % i, [128, 8, T], BF16) for i in range(2)]
            hns = [kb.sb("hn%d" % i, [128, 8, T], F32) for i in range(2)]
            fbfs = [kb.sb("fbf%d" % i, [128, 8, T], BF16) for i in range(2)]
            sq = kb.sb("sq", [128, 8, W], BF16)
            tmp = kb.sb("tmp", [128, 8, W], F32)
            rstd = kb.sb("rstd", [128, W], F32)
            rstd2 = kb.sb("rstd2", [128, W], F32)
            csbs = [kb.sb("csb%d" % i, [128, W], F32) for i in range(2)]
            cus = [kb.sb("cu%d" % i, [128, W], F32) for i in range(2)]
            accs = [kb.sb("acc%d" % i, [128, T], F32) for i in range(2)]
            scw = self.v("scw%d" % j)
            P = self.P
            for it, (s0, s1, row, t0) in enumerate(self.tiles(segs, T)):
                lo, hi = max(t0 - H, s0), min(t0 + T + H, s1)
                n = hi - lo
                off = lo - (t0 - H)
                xt, abf, gT, hn, fbf = xts[it % 2], abfs[it % 2], gTs[it % 2], hns[it % 2], fbfs[it % 2]
                kb.dma("sp", xt.a[:, :, off:off + n], src.a[:, :, lo:hi], r=[src], w=[xt])
                x3 = xt.a[:, :, off:off + n]
                self.rstd_of(x3, n, sq, P[0], rstd, [xt])
                self.modulate(x3, n, rstd, tmp, abf, self.gs.a[:, l, row, 0, :], self.mod.a[:, l, row, 0:8], [xt])
                for ch in range(8):
                    pc, pu, pb = P[2 + ch % 2], P[4 + ch % 2], P[6 + ch % 2]
                    csb, cu, acc = csbs[ch % 2], cus[ch % 2], accs[ch % 2]
                    for (pp, cc) in ((pc, 8 + ch), (pu, 16 + ch), (pb, ch)):
                        kb.mm(pp.a[:, 0:n], [(win.a[:, k, cc * 128:(cc + 1) * 128], abf.a[:, k, 0:n]) for k in range(8)],
                              r=[win, abf], w=[pp])
                    kb.op("act", lambda e, csb=csb, pc=pc: e.copy(csb.a[:, 0:n], pc.a[:, 0:n]), r=[pc], w=[csb])
                    if off > 0:
                        kb.op("pool", lambda e, cu=cu: e.memset(cu.a[:, 0:off], 0.0), w=[cu])
                    if off + n < W:
                        kb.op("pool", lambda e, cu=cu: e.memset(cu.a[:, off + n:W], 0.0), w=[cu])
                    kb.op("dve", lambda e, cu=cu, csb=csb, pu=pu: e.tensor_tensor(
                        cu.a[:, off:off + n], csb.a[:, 0:n], pu.a[:, 0:n], op=ALU.mult), r=[csb, pu], w=[cu])
                    kb.op("pool", lambda e, cu=cu, acc=acc: e.tensor_scalar(
                        acc.a, cu.a[:, 0:T], scw[:, ch:ch + 1], None, op0=ALU.mult), r=[cu, self.vec], w=[acc])
                    for tap in (1, 2):
                        kb.op("dve", lambda e, cu=cu, acc=acc, tap=tap: e.scalar_tensor_tensor(
                            acc.a, cu.a[:, tap:tap + T], scw[:, tap * 8 + ch:tap * 8 + ch + 1], acc.a,
                            op0=ALU.mult, op1=ALU.add), r=[cu, self.vec, acc], w=[acc])
                    c0 = H - off
                    kb.op("dve", lambda e, acc=acc, pb=pb, gT=gT, ch=ch, c0=c0: e.tensor_tensor(
                        gT.a[:, ch, :], acc.a, pb.a[:, c0:c0 + T], op=ALU.mult), r=[acc, pb], w=[gT])
                for dm in range(8):
                    py = P[2 + dm % 6]
                    kb.mm(py.a[:, 0:T], [(wout.a[:, k, dm * 128:(dm + 1) * 128], gT.a[:, k, :]) for k in range(8)],
                          r=[wout, gT], w=[py])
                    kb.op("dve", lambda e, py=py, dm=dm, hn=hn, xt=xt: e.scalar_tensor_tensor(
                        hn.a[:, dm, :], py.a[:, 0:T], self.mod.a[:, l, row, 16 + dm:17 + dm], xt.a[:, dm, H:H + T],
                        op0=ALU.mult, op1=ALU.add), r=[py, self.mod, xt], w=[hn])
                self.post_mixer(l, row, hn, T, dst, fdst, t0, (sq, tmp, rstd2, fbf))


    def stage_conformer(self, l, src, dst, fdst, segs):
        kb = self.kb
        T, H = 256, 15
        W = T + 2 * H
        P = self.P
        with kb.scope():
            w1 = kb.sb("w1", [128, 8, 2048], BF16)
            w2 = kb.sb("w2", [128, 8, 1024], BF16)
            self.load_w_bf(w1, self.inp("cf_w_pw1", [128, 8, 2048]), 8, 2048)
            self.load_w_bf(w2, self.inp("cf_w_pw2", [128, 8, 1024]), 8, 1024)
            xts = [kb.sb("xt%d" % i, [128, 8, W], F32) for i in range(2)]
            abf = kb.sb("abf", [128, 8, W], BF16)
            sq = kb.sb("sq", [128, 8, W], BF16)
            tmp = kb.sb("tmp", [128, 8, W], F32)
            rstd = kb.sb("rstd", [128, W], F32)
            rstd2 = kb.sb("rstd2", [128, W], F32)
            sgs = [kb.sb("sg%d" % i, [128, W], F32) for i in range(2)]
            us = [kb.sb("u%d" % i, [128, W], F32) for i in range(8)]
            uc = kb.sb("uc", [128, 8, T], F32)
            ucs = kb.slots(uc, 8)
            ucq = kb.sb("ucq", [128, 8, T], F32)
            mean = kb.sb("mean", [128, T], F32)
            var = kb.sb("var", [128, T], F32)
            sT = kb.sb("sT", [128, 8, T], BF16)
            hns = [kb.sb("hn%d" % i, [128, 8, T], F32) for i in range(2)]
            fbfs = [kb.sb("fbf%d" % i, [128, 8, T], BF16) for i in range(2)]
            yt = kb.sb("yt", [128, T], F32)
            b1, dw, dwb = self.v("cfb1"), self.v("cfdw"), self.v("cfdwb")
            lng, lnb, b2 = self.v("cflng"), self.v("cflnb"), self.v("cfb2")
            for it, (s0, s1, row, t0) in enumerate(self.tiles(segs, T)):
                lo, hi = max(t0 - H, s0), min(t0 + T + H, s1)
                n = hi - lo
                off = lo - (t0 - H)
                xt, hn, fbf = xts[it % 2], hns[it % 2], fbfs[it % 2]
                kb.dma("sp", xt.a[:, :, off:off + n], src.a[:, :, lo:hi], r=[src], w=[xt])
                x3 = xt.a[:, :, off:off + n]
                self.rstd_of(x3, n, sq, P[0], rstd, [xt])
                self.modulate(x3, n, rstd, tmp, abf, self.gs.a[:, l, row, 0, :], self.mod.a[:, l, row, 0:8], [xt])
                for ch in range(8):
                    pa, pg = P[2 + ch % 2], P[4 + ch % 2]
                    sg, u = sgs[ch % 2], us[ch]
                    for (pp, cc) in ((pa, ch), (pg, 8 + ch)):
                        kb.mm(pp.a[:, 0:n], [(w1.a[:, k, cc * 128:(cc + 1) * 128], abf.a[:, k, 0:n]) for k in range(8)],
                              r=[w1, abf], w=[pp])
                    kb.op("act", lambda e, sg=sg, pg=pg, ch=ch: e.activation(sg.a[:, 0:n], pg.a[:, 0:n], AF.Sigmoid,
                                                                             bias=b1[:, 8 + ch:9 + ch]), r=[pg, self.vec], w=[sg])
                    if off > 0:
                        kb.op("pool", lambda e, u=u: e.memset(u.a[:, 0:off], 0.0), w=[u])
                    if off + n < W:
                        kb.op("pool", lambda e, u=u: e.memset(u.a[:, off + n:W], 0.0), w=[u])
                    kb.op("dve", lambda e, u=u, pa=pa, sg=sg, ch=ch: e.scalar_tensor_tensor(
                        u.a[:, off:off + n], pa.a[:, 0:n], b1[:, ch:ch + 1], sg.a[:, 0:n], op0=ALU.add, op1=ALU.mult),
                        r=[pa, sg, self.vec], w=[u])
                kb.fork(uc, ucs)
                for ch in range(8):
                    u = us[ch]
                    kb.op("dve", lambda e: e.tensor_scalar(
                        uc.a[:, ch, :], u.a[:, 0:T], dw[:, ch:ch + 1], dwb[:, ch:ch + 1], op0=ALU.mult, op1=ALU.add),
                        r=[u, self.vec], w=[ucs[ch]])
                for tap in range(1, 31):
                    for ch in range(8):
                        u = us[ch]
                        kb.op("dve", lambda e: e.scalar_tensor_tensor(
                            uc.a[:, ch, :], u.a[:, tap:tap + T], dw[:, tap * 8 + ch:tap * 8 + ch + 1], uc.a[:, ch, :],
                            op0=ALU.mult, op1=ALU.add), r=[u, self.vec, ucs[ch]], w=[ucs[ch]])
                kb.join(uc, ucs)
                kb.op("act", lambda e: e.activation(ucq.a, uc.a, AF.Square), r=[uc], w=[ucq])
                kb.mm(P[6].a[:, 0:T], [(self.ones_f.a, uc.a[:, k, :]) for k in range(8)], r=[self.ones_f, uc], w=[P[6]])
                kb.mm(P[7].a[:, 0:T], [(self.ones_f.a, ucq.a[:, k, :]) for k in range(8)], r=[self.ones_f, ucq], w=[P[7]])
                kb.op("act", lambda e: e.activation(mean.a, P[6].a[:, 0:T], AF.Identity, scale=1.0 / D), r=[P[6]], w=[mean])
                kb.op("dve", lambda e: e.tensor_tensor(var.a, mean.a, mean.a, op=ALU.mult), r=[mean], w=[var])
                kb.op("dve", lambda e: e.scalar_tensor_tensor(var.a, P[7].a[:, 0:T], 1.0 / D, var.a,
                                                              op0=ALU.mult, op1=ALU.subtract), r=[P[7], var], w=[var])
                kb.op("act", lambda e: e.activation(var.a, var.a, AF.Sqrt, bias=LN_EPS, scale=1.0), r=[var], w=[var])
                kb.op("dve", lambda e: e.reciprocal(var.a, var.a), r=[var], w=[var])
                kb.op("dve", lambda e: e.tensor_tensor(uc.a, uc.a, mean.a.unsqueeze(1).to_broadcast([128, 8, T]),
                                                       op=ALU.subtract), r=[uc, mean], w=[uc])
                kb.op("dve", lambda e: e.tensor_tensor(uc.a, uc.a, var.a.unsqueeze(1).to_broadcast([128, 8, T]),
                                                       op=ALU.mult), r=[uc, var], w=[uc])
                for k in range(8):
                    kb.op("act", lambda e, k=k: e.activation(sT.a[:, k, :], uc.a[:, k, :], AF.Silu,
                                                             bias=lnb[:, k:k + 1], scale=lng[:, k:k + 1]),
                          r=[uc, self.vec], w=[sT])
                for dm in range(8):
                    py = P[2 + dm % 4]
                    kb.mm(py.a[:, 0:T], [(w2.a[:, k, dm * 128:(dm + 1) * 128], sT.a[:, k, :]) for k in range(8)],
                          r=[w2, sT], w=[py])
                    kb.op("dve", lambda e, py=py, dm=dm: e.tensor_scalar(
                        yt.a, py.a[:, 0:T], b2[:, dm:dm + 1], self.mod.a[:, l, row, 16 + dm:17 + dm],
                        op0=ALU.add, op1=ALU.mult), r=[py, self.vec, self.mod], w=[yt])
                    kb.op("dve", lambda e, dm=dm, hn=hn, xt=xt: e.tensor_tensor(
                        hn.a[:, dm, :], yt.a, xt.a[:, dm, H:H + T], op=ALU.add), r=[yt, xt], w=[hn])
                self.post_mixer(l, row, hn, T, dst, fdst, t0, (sq, tmp, rstd2, fbf))

    def stage_attn(self, l, src, dst, fdst, segs_lat, segs_ctx, lambda_init):
        kb = self.kb
        P = self.P
        T = 512
        nlat = segs_lat[-1][1]
        ntok = segs_ctx[-1][1]
        qTd = kb.dram("a_qT", [128, 8, nlat], BF16)
        kTd = kb.dram("a_kT", [128, 8, ntok], BF16)
        vTd = kb.dram("a_v", [ntok // 128, 128, 1024], BF16)
        oTd = kb.dram("a_oT", [128, 8, nlat], BF16)
        with kb.scope():
            wqkv = kb.sb("wqkv", [128, 8, 3072], BF16)
            self.load_w_bf(wqkv, self.inp("da_w_qkv", [128, 8, 3072]), 8, 3072)
            cosT = kb.sb("cosT", [128, NLAT], F32)
            sinT = kb.sb("sinT", [128, NLAT], F32)
            blk = kb.sb("blk", [128, 128], BF16)
            prot = kb.sb("prot", [128, 128], F32)
            kb.dma("sp", cosT.a, self.inp("ropecos", [128, NLAT]).a, r=[self.din["ropecos"]], w=[cosT])
            kb.dma("sp", sinT.a, self.inp("ropesin", [128, NLAT]).a, r=[self.din["ropesin"]], w=[sinT])
            kb.dma("pool", blk.a, self.inp("blkones", [128, 128]).a, r=[self.din["blkones"]], w=[blk])
            kb.dma("sp", prot.a, self.inp("protm", [128, 128]).a, r=[self.din["protm"]], w=[prot])
            xts = [kb.sb("xt%d" % i, [128, 8, T], F32) for i in range(2)]
            abf = kb.sb("abf", [128, 8, T], BF16)
            sq = kb.sb("sq", [128, 8, T], BF16)
            tmp = kb.sb("tmp", [128, 8, T], F32)
            rstd = kb.sb("rstd", [128, T], F32)
            sqh = kb.sb("sqh", [128, T], BF16)
            rs = kb.sb("rs", [128, T], F32)
            qn = kb.sb("qn", [128, T], F32)
            t1 = kb.sb("t1", [128, T], F32)
            t2 = kb.sb("t2", [128, T], F32)
            qks = [kb.sb("qk%d" % i, [128, 8, T], BF16) for i in range(2)]
            vsb = kb.sb("vsb", [128, 1024], BF16)
            for it, (s0, s1, row, t0) in enumerate(self.tiles(list(segs_lat) + list(segs_ctx), T)):
                n = min(T, s1 - t0)
                is_lat = t0 < nlat
                pos0 = t0 - s0
                xt = xts[it % 2]
                kb.dma("sp", xt.a[:, :, 0:n], src.a[:, :, t0:t0 + n], r=[src], w=[xt])
                x3 = xt.a[:, :, 0:n]
                self.rstd_of(x3, n, sq, P[0], rstd, [xt])
                self.modulate(x3, n, rstd, tmp, abf, self.gs.a[:, l, row, 0, :], self.mod.a[:, l, row, 0:8], [xt])
                for qi, (base, gk, dd, stage) in enumerate(((0, "qg", qTd, qks[0]), (8, "kg", kTd, qks[1]))):
                    if qi == 0 and not is_lat:
                        continue
                    for h in range(8):
                        pq, pss, pr = P[2 + h % 2], P[4 + h % 2], P[6 + h % 2]
                        cc = base + h
                        kb.mm(pq.a[:, 0:n], [(wqkv.a[:, k, cc * 128:(cc + 1) * 128], abf.a[:, k, 0:n]) for k in range(8)],
                              r=[wqkv, abf], w=[pq])
                        kb.op("act", lambda e, pq=pq: e.activation(sqh.a[:, 0:n], pq.a[:, 0:n], AF.Square), r=[pq], w=[sqh])
                        kb.mm(pss.a[:, 0:n], [(blk.a, sqh.a[:, 0:n])], r=[blk, sqh], w=[pss])
                        kb.op("act", lambda e, pss=pss: e.activation(rs.a[:, 0:n], pss.a[:, 0:n], AF.Sqrt, bias=RMS_EPS,
                                                                     scale=1.0 / 64), r=[pss], w=[rs])
                        kb.op("dve", lambda e: e.reciprocal(rs.a[:, 0:n], rs.a[:, 0:n]), r=[rs], w=[rs])
                        kb.op("dve", lambda e, pq=pq, gk=gk: e.scalar_tensor_tensor(
                            qn.a[:, 0:n], pq.a[:, 0:n], self.v(gk), rs.a[:, 0:n], op0=ALU.mult, op1=ALU.mult),
                            r=[pq, rs, self.vec], w=[qn])
                        if is_lat:
                            kb.mm(pr.a[:, 0:n], [(prot.a, qn.a[:, 0:n])], r=[prot, qn], w=[pr])
                            kb.op("pool", lambda e: e.tensor_tensor(t1.a[:, 0:n], qn.a[:, 0:n], cosT.a[:, pos0:pos0 + n],
                                                                    op=ALU.mult), r=[qn, cosT], w=[t1])
                            kb.op("dve", lambda e, pr=pr: e.tensor_tensor(t2.a[:, 0:n], pr.a[:, 0:n], sinT.a[:, pos0:pos0 + n],
                                                                          op=ALU.mult), r=[pr, sinT], w=[t2])
                            kb.op("dve", lambda e, stage=stage, h=h: e.tensor_tensor(stage.a[:, h, 0:n], t1.a[:, 0:n], t2.a[:, 0:n],
                                                                                     op=ALU.add), r=[t1, t2], w=[stage])
                        else:
                            kb.op("act", lambda e, stage=stage, h=h: e.copy(stage.a[:, h, 0:n], qn.a[:, 0:n]), r=[qn], w=[stage])
                    kb.dma("sp", dd.a[:, :, t0:t0 + n], stage.a[:, :, 0:n], r=[stage], w=[dd])
                for sub in range(n // 128):
                    for half in range(2):
                        pv = P[2 + half]
                        kb.mm(pv.a, [(abf.a[:, k, sub * 128:(sub + 1) * 128],
                                      wqkv.a[:, k, 2048 + half * 512:2048 + (half + 1) * 512]) for k in range(8)],
                              r=[abf, wqkv], w=[pv])
                        if half == 0:
                            kb.op("act", lambda e, pv=pv: e.copy(vsb.a[:, 0:512], pv.a), r=[pv], w=[vsb])
                        else:
                            kb.op("dve", lambda e, pv=pv: e.tensor_copy(vsb.a[:, 512:1024], pv.a), r=[pv], w=[vsb])
                    kb.dma("sp", vTd.a[(t0 + sub * 128) // 128], vsb.a, r=[vsb], w=[vTd])
        with kb.scope():
            lam = kb.sb("lam", [128, 4], F32)
            lt = kb.sb("lt", [128, 64], F32)
            for ii, (ka, kb_) in enumerate((("lq1", "lk1"), ("lq2", "lk2"))):
                kb.op("dve", lambda e, ka=ka, kb_=kb_: e.tensor_tensor(lt.a, self.v(ka), self.v(kb_), op=ALU.mult),
                      r=[self.vec], w=[lt])
                kb.op("dve", lambda e, ii=ii: e.tensor_reduce(out=lam.a[:, ii:ii + 1], in_=lt.a, axis=AX.X, op=ALU.add),
                      r=[lt], w=[lam])
            kb.op("act", lambda e: e.activation(lam.a[:, 0:2], lam.a[:, 0:2], AF.Exp), r=[lam], w=[lam])
            kb.op("dve", lambda e: e.tensor_tensor(lam.a[:, 2:3], lam.a[:, 1:2], lam.a[:, 0:1], op=ALU.subtract), r=[lam], w=[lam])
            kb.op("dve", lambda e: e.tensor_scalar(lam.a[:, 2:3], lam.a[:, 2:3], -float(lambda_init), None, op0=ALU.add),
                  r=[lam], w=[lam])
            kb.op("dve", lambda e: e.tensor_scalar(lam.a[:, 3:4], self.v("subg"), 1.0 - float(lambda_init), None, op0=ALU.mult),
                  r=[self.vec], w=[lam])
            kts = [kb.sb("kt%d" % i, [128, NLAT + NCTX], BF16) for i in range(2)]
            vts = [kb.sb("vt%d" % i, [128, 18, 128], BF16) for i in range(2)]
            qts = [kb.sb("qt%d" % i, [128, T], BF16) for i in range(2)]
            pTs = [kb.sb("pT%d" % i, [128, T], BF16) for i in range(4)]
            rz = kb.sb("rz", [128, 2, T], F32)
            o1 = kb.sb("o1", [128, T], F32)
            o2 = kb.sb("o2", [128, T], F32)
            osq = kb.sb("osq", [128, T], BF16)
            ors = kb.sb("ors", [128, T], F32)
            ofs = [kb.sb("of%d" % i, [128, T], BF16) for i in range(2)]
            scale = 64 ** -0.5
            nit = 0
            npt = 0
            for bi, ((l0, l1, _), (c0, c1, _)) in enumerate(zip(segs_lat, segs_ctx)):
                nkl = (l1 - l0) // 128
                nkc = (c1 - c0) // 128
                nk = nkl + nkc
                for h in range(8):
                    kt, vt = kts[(bi * 8 + h) % 2], vts[(bi * 8 + h) % 2]
                    kb.dma("sp", kt.a[:, 0:l1 - l0], kTd.a[:, h, l0:l1], r=[kTd], w=[kt])
                    kb.dma("sp", kt.a[:, l1 - l0:l1 - l0 + c1 - c0], kTd.a[:, h, c0:c1], r=[kTd], w=[kt])
                    kb.dma("sp", vt.a[:, 0:nkl, :], vTd.a[l0 // 128:l1 // 128, :, h * 128:(h + 1) * 128].rearrange("c p d -> p c d"),
                           r=[vTd], w=[vt])
                    kb.dma("sp", vt.a[:, nkl:nk, :], vTd.a[c0 // 128:c1 // 128, :, h * 128:(h + 1) * 128].rearrange("c p d -> p c d"),
                           r=[vTd], w=[vt])
                    for q0 in range(l0, l1, T):
                        qt = qts[nit % 2]
                        of = ofs[nit % 2]
                        nit += 1
                        kb.dma("sp", qt.a, qTd.a[:, h, q0:q0 + T], r=[qTd], w=[qt])
                        pend = []

                        def emit_pv(item):
                            kc, comp, pT = item
                            kb.mm(P[comp].a, [(vt.a[:, kc, :], pT.a)], r=[vt, pT], w=[P[comp]],
                                  start=(kc == 0), stop=(kc == nk - 1))
                            kb.mm(P[2 + comp].a, [(self.ones_bf.a, pT.a)], r=[self.ones_bf, pT], w=[P[2 + comp]],
                                  start=(kc == 0), stop=(kc == nk - 1))

                        for kc in range(nk):
                            for comp in range(2):
                                pS = P[4 + npt % 4]
                                pT = pTs[npt % 4]
                                npt += 1
                                kb.mm(pS.a, [(kt.a[comp * 64:(comp + 1) * 64, kc * 128:(kc + 1) * 128],
                                              qt.a[comp * 64:(comp + 1) * 64, :])], r=[kt, qt], w=[pS])
                                kb.op("act", lambda e, pS=pS, pT=pT: e.activation(pT.a, pS.a, AF.Exp, scale=scale), r=[pS], w=[pT])
                                pend.append((kc, comp, pT))
                                if len(pend) > 2:
                                    emit_pv(pend.pop(0))
                        while pend:
                            emit_pv(pend.pop(0))
                        for comp in range(2):
                            kb.op("dve", lambda e, comp=comp: e.reciprocal(rz.a[:, comp, :], P[2 + comp].a), r=[P[2 + comp]], w=[rz])
                        kb.op("dve", lambda e: e.tensor_tensor(o1.a, P[0].a, rz.a[:, 0, :], op=ALU.mult), r=[P[0], rz], w=[o1])
                        kb.op("dve", lambda e: e.tensor_tensor(o2.a, P[1].a, rz.a[:, 1, :], op=ALU.mult), r=[P[1], rz], w=[o2])
                        kb.op("dve", lambda e: e.scalar_tensor_tensor(o1.a, o2.a, lam.a[:, 2:3], o1.a, op0=ALU.mult, op1=ALU.add),
                              r=[o1, o2, lam], w=[o1])
                        kb.op("act", lambda e: e.activation(osq.a, o1.a, AF.Square), r=[o1], w=[osq])
                        pss = P[4 + npt % 4]
                        npt += 1
                        kb.mm(pss.a, [(self.ones_bf.a, osq.a)], r=[self.ones_bf, osq], w=[pss])
                        kb.op("act", lambda e, pss=pss: e.activation(ors.a, pss.a, AF.Sqrt, bias=RMS_EPS, scale=1.0 / 128),
                              r=[pss], w=[ors])
                        kb.op("dve", lambda e: e.reciprocal(ors.a, ors.a), r=[ors], w=[ors])
                        kb.op("dve", lambda e, of=of: e.scalar_tensor_tensor(of.a, o1.a, lam.a[:, 3:4], ors.a, op0=ALU.mult, op1=ALU.mult),
                              r=[o1, lam, ors], w=[of])
                        kb.dma("sp", oTd.a[:, h, q0:q0 + T], of.a, r=[of], w=[oTd])
        with kb.scope():
            wo = kb.sb("wo", [128, 8, 1024], BF16)
            self.load_w_bf(wo, self.inp("da_w_o", [128, 8, 1024]), 8, 1024)
            T3 = 256
            xts = [kb.sb("xt%d" % i, [128, 8, T3], F32) for i in range(2)]
            ots = [kb.sb("ot%d" % i, [128, 8, T3], BF16) for i in range(2)]
            hns = [kb.sb("hn%d" % i, [128, 8, T3], F32) for i in range(2)]
            fbfs = [kb.sb("fbf%d" % i, [128, 8, T3], BF16) for i in range(2)]
            sq = kb.sb("sq", [128, 8, T3], BF16)
            tmp = kb.sb("tmp", [128, 8, T3], F32)
            rstd2 = kb.sb("rstd2", [128, T3], F32)
            for it, (s0, s1, row, t0) in enumerate(self.tiles(segs_lat, T3)):
                xt, ot, hn, fbf = xts[it % 2], ots[it % 2], hns[it % 2], fbfs[it % 2]
                kb.dma("sp", xt.a, src.a[:, :, t0:t0 + T3], r=[src], w=[xt])
                kb.dma("sp", ot.a, oTd.a[:, :, t0:t0 + T3], r=[oTd], w=[ot])
                for dm in range(8):
                    py = P[2 + dm % 6]
                    kb.mm(py.a[:, 0:T3], [(wo.a[:, k, dm * 128:(dm + 1) * 128], ot.a[:, k, :]) for k in range(8)],
                          r=[wo, ot], w=[py])
                    kb.op("dve", lambda e, py=py, dm=dm, hn=hn, xt=xt: e.scalar_tensor_tensor(
                        hn.a[:, dm, :], py.a[:, 0:T3], self.mod.a[:, l, row, 16 + dm:17 + dm], xt.a[:, dm, :],
                        op0=ALU.mult, op1=ALU.add), r=[py, self.mod, xt], w=[hn])
                self.post_mixer(l, row, hn, T3, dst, fdst, t0, (sq, tmp, rstd2, fbf))

    def peer_convert(self, l):
        kb = self.kb
        uT = self.inp("peer_uT%d" % l, [128, 8, PEER_E])
        vv = self.inp("peer_v%d" % l, [128, 128, 1024])
        ubf = kb.dram("ubf%d" % l, [128, 8, PEER_E], BF16)
        vbf = kb.dram("vbf%d" % l, [128, 128, 1024], BF16)
        for k in range(8):
            for c in range(0, PEER_E, 4096):
                kb.dma("pool", ubf.a[:, k, c:c + 4096], uT.a[:, k, c:c + 4096], r=[uT], w=[ubf])
        for i0 in range(0, 128, 4):
            kb.dma("pool", vbf.a[:, i0:i0 + 4, :], vv.a[:, i0:i0 + 4, :], r=[vv], w=[vbf])
        return ubf, vbf

    def stage_peer_q(self, l, fsrc, qTd, t_lo, t_hi):
        kb = self.kb
        T = 512
        with kb.scope():
            wq = kb.sb("wq", [128, 8, 2048], BF16)
            self.load_w_bf(wq, self.inp("peer_wq%d" % l, [128, 8, 2048]), 8, 2048)
            fts = [kb.sb("ft%d" % i, [128, 8, T], BF16) for i in range(2)]
            qts = [kb.sb("qt%d" % i, [128, 16, T], BF16) for i in range(2)]
            for it, t0 in enumerate(range(t_lo, t_hi, T)):
                n = min(T, t_hi - t0)
                ft, qt = fts[it % 2], qts[it % 2]
                kb.dma("sp", ft.a[:, :, 0:n], fsrc.a[:, :, t0:t0 + n], r=[fsrc], w=[ft])
                for c in range(16):
                    pq = self.P[c % 8]
                    kb.mm(pq.a[:, 0:n], [(wq.a[:, k, c * 128:(c + 1) * 128], ft.a[:, k, 0:n]) for k in range(8)],
                          r=[wq, ft], w=[pq])
                    en = "act" if c % 2 == 0 else "dve"
                    if en == "act":
                        kb.op("act", lambda e, c=c, pq=pq, qt=qt: e.copy(qt.a[:, c, 0:n], pq.a[:, 0:n]), r=[pq], w=[qt])
                    else:
                        kb.op("dve", lambda e, c=c, pq=pq, qt=qt: e.tensor_copy(qt.a[:, c, 0:n], pq.a[:, 0:n]), r=[pq], w=[qt])
                kb.dma("sp", qTd.a[:, :, t0:t0 + n], qt.a[:, :, 0:n], r=[qt], w=[qTd])

    def stage_peer_route(self, l, qTd, Wd, t_lo, t_hi):
        kb = self.kb
        P = self.P
        NEG = -1.0e30
        with kb.scope():
            KT = kb.sb("KT", [128, 16, 128], BF16)
            kb.dma("pool", KT.a, self.inp("peer_kT%d" % l, [128, 16, 128]).a, r=[self.din["peer_kT%d" % l]], w=[KT])
            ident = kb.sb("ident", [128, 128], BF16)
            if "ident" not in self.din:
                self.inp("ident", [128, 128])
            kb.dma("pool", ident.a, self.din["ident"].a, r=[self.din["ident"]], w=[ident])
            qts = [kb.sb("rq%d" % i, [128, 16, 128], BF16) for i in range(2)]
            S = kb.sb("S", [128, 16, 128], F32)
            Ss = kb.slots(S, 16)
            Sx = kb.sb("Sx", [128, 16, 128], F32)
            Sxs = kb.slots(Sx, 16)
            M = kb.sb("M", [128, 16, 16], F32)
            Ms = kb.slots(M, 32)
            cand = kb.sb("cand", [128, 8, 256], F32)
            cands = kb.slots(cand, 8)
            Cx = kb.sb("Cx", [128, 8, 256], F32)
            Cxs = kb.slots(Cx, 8)
            C16 = kb.sb("C16", [128, 8, 16], F32)
            C16s = kb.slots(C16, 16)
            sm = kb.sb("sm", [128, 8, 16], F32)
            Zs = kb.sb("Zs", [128, 8], F32)
            thr = kb.sb("thr", [128, 8], F32)
            E1 = kb.sb("E1", [128, 8, 16], F32)
            cc = kb.sb("cc", [128, 8, 16], F32)
            E2 = kb.sb("E2", [128, 8, 128], F32)
            tms = [kb.sb("tm%d" % i, [128, 128], F32) for i in range(8)]
            R = kb.sb("R", [128, 128, 128], BF16)
            Rs = kb.slots(R, 128)
            Ws = kb.slots(R, 32)
            OH = kb.sb("OH", [128, 128, 128], BF16)
            OHs = kb.slots(OH, 8)
            RT = kb.sb("RT", [128, 128, 128], BF16)
            RTs = kb.slots(RT, 32)
            OHT = kb.sb("OHT", [128, 128, 128], BF16)
            OHTs = kb.slots(OHT, 32)
            S4 = S.a.rearrange("p (h two) n -> p h two n", two=2)
            M4 = M.a.rearrange("p (h two) n -> p h two n", two=2)
            nb = 0
            for it, t0 in enumerate(range(t_lo, t_hi, 128)):
                g = t0 // 128
                qt = qts[it % 2]
                kb.dma("sp", qt.a, qTd.a[:, :, t0:t0 + 128], r=[qTd], w=[qt])
                kb.fork(S, Ss)
                for b in range(4):
                    pb = P[nb % 8]
                    nb += 1
                    for cI in range(4):
                        c = b * 4 + cI
                        kb.mm(pb.a[:, cI * 128:(cI + 1) * 128], [(qt.a[:, c, :], KT.a[:, c, :])], r=[qt, KT], w=[pb])
                    kb.op("act", lambda e: e.copy(S.a[:, b * 4:(b + 1) * 4, :].rearrange("p c n -> p (c n)"), pb.a),
                          r=[pb], w=Ss[b * 4:(b + 1) * 4])
                kb.fork(M, Ms)
                for c in range(16):
                    kb.op("dve", lambda e: e.max(out=M.a[:, c, 0:8], in_=S.a[:, c, :]), r=[Ss[c]], w=[Ms[2 * c]])
                for c in range(16):
                    kb.op("dve", lambda e: e.match_replace(out=Sx.a[:, c, :], in_to_replace=M.a[:, c, 0:8],
                                                           in_values=S.a[:, c, :], imm_value=NEG),
                          r=[Ss[c], Ms[2 * c]], w=[Sxs[c]])
                for c in range(16):
                    kb.op("dve", lambda e: e.max(out=M.a[:, c, 8:16], in_=Sx.a[:, c, :]), r=[Sxs[c]], w=[Ms[2 * c + 1]])
                kb.join(S, Ss)
                kb.join(M, Ms)
                for h in range(8):
                    kb.op("pool", lambda e: e.tensor_tensor(
                        cand.a[:, h, :].rearrange("p (a b) -> p a b", a=16),
                        M.a[:, 2 * h, :].unsqueeze(2).to_broadcast([128, 16, 16]),
                        M.a[:, 2 * h + 1, :].unsqueeze(1).to_broadcast([128, 16, 16]), op=ALU.add),
                        r=Ms[4 * h:4 * h + 4], w=[cands[h]])
                kb.fork(C16, C16s)
                for h in range(8):
                    kb.op("dve", lambda e: e.max(out=C16.a[:, h, 0:8], in_=cand.a[:, h, :]), r=[cands[h]], w=[C16s[2 * h]])
                for h in range(8):
                    kb.op("dve", lambda e: e.match_replace(out=Cx.a[:, h, :], in_to_replace=C16.a[:, h, 0:8],
                                                           in_values=cand.a[:, h, :], imm_value=NEG),
                          r=[cands[h], C16s[2 * h]], w=[Cxs[h]])
                for h in range(8):
                    kb.op("dve", lambda e: e.max(out=C16.a[:, h, 8:16], in_=Cx.a[:, h, :]), r=[Cxs[h]], w=[C16s[2 * h + 1]])
                kb.join(C16, C16s)
                kb.op("dve", lambda e: e.tensor_tensor(sm.a, C16.a, C16.a[:, :, 0:1].to_broadcast([128, 8, 16]),
                                                       op=ALU.subtract), r=[C16], w=[sm])
                kb.op("act", lambda e: e.activation(sm.a, sm.a, AF.Exp), r=[sm], w=[sm])
                kb.op("dve", lambda e: e.tensor_reduce(out=Zs.a, in_=sm.a, axis=AX.X, op=ALU.add), r=[sm], w=[Zs])
                kb.op("dve", lambda e: e.reciprocal(Zs.a, Zs.a), r=[Zs], w=[Zs])
                kb.op("dve", lambda e: e.scalar_tensor_tensor(thr.a, C16.a[:, :, 15], -1.0, C16.a[:, :, 0],
                                                              op0=ALU.mult, op1=ALU.max), r=[C16], w=[thr])
                kb.op("dve", lambda e: e.scalar_tensor_tensor(thr.a, thr.a, -2.0e-5, C16.a[:, :, 15],
                                                              op0=ALU.mult, op1=ALU.add), r=[thr, C16], w=[thr])
                kb.op("dve", lambda e: e.tensor_tensor(E1.a, M4[:, :, 0, :], M4[:, :, 0, 0:1].to_broadcast([128, 8, 16]),
                                                       op=ALU.subtract), r=[M], w=[E1])
                kb.op("act", lambda e: e.activation(E1.a, E1.a, AF.Exp), r=[E1], w=[E1])
                kb.op("dve", lambda e: e.tensor_tensor(E1.a, E1.a, Zs.a.unsqueeze(2).to_broadcast([128, 8, 16]),
                                                       op=ALU.mult), r=[E1, Zs], w=[E1])
                kb.op("dve", lambda e: e.scalar_tensor_tensor(cc.a, M4[:, :, 0, :], -1.0,
                                                              thr.a.unsqueeze(2).to_broadcast([128, 8, 16]),
                                                              op0=ALU.mult, op1=ALU.add), r=[M, thr], w=[cc])
                kb.op("dve", lambda e: e.tensor_tensor(E2.a, S4[:, :, 1, :], M4[:, :, 1, 0:1].to_broadcast([128, 8, 128]),
                                                       op=ALU.subtract), r=[S, M], w=[E2])
                kb.op("act", lambda e: e.activation(E2.a, E2.a, AF.Exp), r=[E2], w=[E2])
                kb.fork(R, Rs)
                kb.fork(OH, OHs)
                for h in range(8):
                    for r in range(16):
                        c = h * 16 + r
                        tm = tms[c % 8]
                        kb.op("dve", lambda e: e.scalar_tensor_tensor(
                            tm.a, S.a[:, 2 * h + 1, :], cc.a[:, h, r:r + 1], E2.a[:, h, :], op0=ALU.is_ge, op1=ALU.mult),
                            r=[S, cc, E2], w=[tm])
                        kb.op("act", lambda e: e.activation(
                            R.a[:, c, :], tm.a, AF.Identity, scale=E1.a[:, h, r:r + 1]), r=[tm, E1], w=[Rs[c]])
                kb.join(R, Rs)
                for h in range(8):
                    kb.op("dve", lambda e: e.tensor_tensor(
                        OH.a[:, h * 16:(h + 1) * 16, :],
                        S.a[:, 2 * h, :].unsqueeze(1).to_broadcast([128, 16, 128]),
                        M.a[:, 2 * h, :].unsqueeze(2).to_broadcast([128, 16, 128]), op=ALU.is_equal),
                        r=[S, M], w=[OHs[h]])
                kb.join(OH, OHs)
                kb.fork(RT, RTs)
                kb.fork(OHT, OHTs)
                for (srcb, dstb, dsl) in ((R, RT, RTs), (OH, OHT, OHTs)):
                    for j0 in range(0, 128, 4):
                        pb = P[nb % 8]
                        nb += 1
                        for jj in range(4):
                            kb.mm(pb.a[:, jj * 128:(jj + 1) * 128], [(srcb.a[:, :, j0 + jj], ident.a)],
                                  r=[srcb, ident], w=[pb])
                        dv = dstb.a[:, j0:j0 + 4, :].rearrange("p j t -> p (j t)")
                        if nb % 2 == 0 or srcb is R:
                            kb.op("act", lambda e: e.copy(dv, pb.a), r=[pb], w=[dsl[j0 // 4]])
                        else:
                            kb.op("dve", lambda e: e.tensor_copy(dv, pb.a), r=[pb], w=[dsl[j0 // 4]])
                kb.join(RT, RTs)
                kb.join(OHT, OHTs)
                kb.fork(R, Ws)
                for tq in range(0, 128, 4):
                    pb = P[nb % 8]
                    nb += 1
                    for tt in range(4):
                        kb.mm(pb.a[:, tt * 128:(tt + 1) * 128], [(RT.a[:, :, tq + tt], OHT.a[:, :, tq + tt])],
                              r=[RT, OHT], w=[pb])
                    dv = R.a[:, :, tq:tq + 4]
                    sv = pb.a.rearrange("p (t i) -> p i t", t=4)
                    if nb % 2 == 0:
                        kb.op("act", lambda e: e.copy(dv, sv), r=[pb], w=[Ws[tq // 4]])
                    else:
                        kb.op("dve", lambda e: e.tensor_copy(dv, sv), r=[pb], w=[Ws[tq // 4]])
                kb.join(R, Ws)
                kb.dma("sp", Wd.a[g], R.a, r=[R], w=[Wd])

    def stage_peer_experts(self, l, ubf, vbf, fsrc, hsrc, hdst, Wd, segs, dst_off=0):
        kb = self.kb
        P = self.P
        T = 256
        NBG = 4
        with kb.scope():
            uts = [kb.sb("ut%d" % i, [128, 8, NBG * 128], BF16) for i in range(3)]
            vts = [kb.sb("vt%d" % i, [128, NBG, 1024], BF16) for i in range(3)]
            wts = [kb.sb("wt%d" % i, [128, 2, NBG, 128], BF16) for i in range(3)]
            fts = [kb.sb("eft%d" % i, [128, 8, T], BF16) for i in range(2)]
            hts = [kb.sb("eht%d" % i, [128, 8, T], F32) for i in range(2)]
            gzs = [kb.sb("gz%d" % i, [128, T], F32) for i in range(3)]
            As = [kb.sb("A%d" % i, [128, T], BF16) for i in range(4)]
            nslot = 0
            nz = 0
            LOOK = 2
            for ig, (s0, s1, row, t0) in enumerate(self.tiles(segs, T)):
                ft, ht = fts[ig % 2], hts[ig % 2]
                kb.dma("sp", ft.a, fsrc.a[:, :, t0:t0 + T], r=[fsrc], w=[ft])
                kb.dma("sp", ht.a, hsrc.a[:, :, t0:t0 + T], r=[hsrc], w=[ht])
                g0 = t0 // 128
                pend = []

                def emit_out(item):
                    i, vt, b, A = item
                    for dm in range(8):
                        po = P[dm // 2]
                        kb.mm(po.a[:, (dm % 2) * T:(dm % 2 + 1) * T], [(vt.a[:, b, dm * 128:(dm + 1) * 128], A.a)],
                              r=[vt, A], w=[po], start=(i == 0), stop=(i == 127))

                for bg in range(128 // NBG):
                    ut, vt, wt = uts[nslot % 3], vts[nslot % 3], wts[nslot % 3]
                    nslot += 1
                    if not (getattr(self, "dbg_noload", False) and nslot > 3):
                        kb.dma("sp", ut.a, ubf.a[:, :, bg * NBG * 128:(bg + 1) * NBG * 128], r=[ubf], w=[ut])
                        kb.dma(getattr(self, "vq", "pool"), vt.a, vbf.a[:, bg * NBG:(bg + 1) * NBG, :], r=[vbf], w=[vt])
                        for tl in range(2):
                            kb.dma("sp", wt.a[:, tl, :, :], Wd.a[g0 + tl][:, bg * NBG:(bg + 1) * NBG, :], r=[Wd], w=[wt])
                    for b in range(NBG):
                        i = bg * NBG + b
                        pz = P[4 + nz % 4]
                        gz, A = gzs[nz % 3], As[nz % 4]
                        nz += 1
                        kb.mm(pz.a[:, 0:T], [(ut.a[:, k, b * 128:(b + 1) * 128], ft.a[:, k, :]) for k in range(8)],
                              r=[ut, ft], w=[pz])
                        kb.op("act", lambda e: e.activation(gz.a, pz.a[:, 0:T], AF.Gelu), r=[pz], w=[gz])
                        kb.op("dve", lambda e: e.tensor_tensor(
                            A.a.rearrange("p (a t) -> p a t", a=2), gz.a.rearrange("p (a t) -> p a t", a=2),
                            wt.a[:, :, b, :], op=ALU.mult), r=[gz, wt], w=[A])
                        pend.append((i, vt, b, A))
                        if len(pend) > LOOK:
                            emit_out(pend.pop(0))
                while pend:
                    emit_out(pend.pop(0))
                for dm in range(8):
                    po = P[dm // 2]
                    kb.op("dve", lambda e, dm=dm, po=po, ht=ht: e.scalar_tensor_tensor(
                        ht.a[:, dm, :], po.a[:, (dm % 2) * T:(dm % 2 + 1) * T], self.mod.a[:, l, row, 40 + dm:41 + dm],
                        ht.a[:, dm, :], op0=ALU.mult, op1=ALU.add), r=[po, self.mod, ht], w=[ht])
                kb.dma("sp", hdst.a[:, :, t0 - dst_off:t0 - dst_off + T], ht.a, r=[ht], w=[hdst])

def kmaj(w):
    w = np.asarray(w, np.float32)
    nk = w.shape[0] // 128
    return np.ascontiguousarray(w.reshape(nk, 128, w.shape[1]).transpose(1, 0, 2))


def tok_to_T(X):
    X = np.asarray(X)
    return np.ascontiguousarray(X.T.reshape(8, 128, X.shape[0]).transpose(1, 0, 2))


def T_to_tok(hT):
    return np.ascontiguousarray(hT.transpose(1, 0, 2).reshape(1024, hT.shape[2]).T)


def build_vecs(inp, crows):
    vp = VecPack()
    cT = np.stack([packv(crows[r]) for r in range(3)], axis=2)
    vp.add("cT", cT.reshape(128, 24))
    for l in range(DEPTH):
        vp.add("bmod%d" % l, packv(inp["b_mod"][l]))
        vp.add("n1g%d" % l, packv(inp["norm1_g"][l]))
        vp.add("n2g%d" % l, packv(inp["norm2_g"][l]))
    for j in range(inp["sc_conv_w"].shape[0]):
        cw = inp["sc_conv_w"][j]
        vp.add("scw%d" % j, np.concatenate([packv(cw[k]) for k in range(3)], axis=1))
    if "cf_b_pw1" in inp:
        vp.add("cfb1", packv(inp["cf_b_pw1"][0]))
        dw = inp["cf_dw_w"][0]
        vp.add("cfdw", np.concatenate([packv(dw[k]) for k in range(dw.shape[0])], axis=1))
        for key, nm in (("cf_dw_b", "cfdwb"), ("cf_ln_g", "cflng"), ("cf_ln_b", "cflnb"), ("cf_b_pw2", "cfb2")):
            vp.add(nm, packv(inp[key][0]))
    if "da_q_norm_g" in inp:
        rep = lambda v: np.ascontiguousarray(np.broadcast_to(np.asarray(v, np.float32)[None, :], (128, len(v))))
        vp.add("qg", np.tile(np.asarray(inp["da_q_norm_g"][0], np.float32), 2).reshape(128, 1))
        vp.add("kg", np.tile(np.asarray(inp["da_k_norm_g"][0], np.float32), 2).reshape(128, 1))
        vp.add("subg", np.asarray(inp["da_subln_g"][0], np.float32).reshape(128, 1))
        for key, nm in (("da_lam_q1", "lq1"), ("da_lam_k1", "lk1"), ("da_lam_q2", "lq2"), ("da_lam_k2", "lk2")):
            vp.add(nm, rep(inp[key][0]))
    return vp


def host_consts():
    c = {}
    c["ident"] = np.eye(128, dtype=np.float32)
    blk = np.zeros((128, 128), np.float32)
    blk[:64, :64] = 1.0
    blk[64:, 64:] = 1.0
    c["blkones"] = blk
    prot = np.zeros((128, 128), np.float32)
    cosT = np.zeros((128, NLAT), np.float32)
    sinT = np.zeros((128, NLAT), np.float32)
    t = np.arange(NLAT)
    pos = (np.floor_divide(t, 64).astype(np.float32), np.mod(t, 64).astype(np.float32))
    nf = 16
    inv_freq = (10000.0 ** (-np.arange(nf, dtype=np.float32) / nf)).astype(np.float32)
    for p in range(128):
        dh = p % 64
        axis, half, f = dh // 32, (dh % 32) // 16, dh % 16
        ang = pos[axis] * inv_freq[f]
        cosT[p] = np.cos(ang)
        sinT[p] = np.sin(ang)
        if half == 0:
            prot[p + 16, p] = -1.0
        else:
            prot[p - 16, p] = 1.0
    c["protm"] = prot
    c["ropecos"] = cosT
    c["ropesin"] = sinT
    return c


SEGS_LAT = [(0, NLAT, 0), (NLAT, 2 * NLAT, 1)]
SEGS_CTX = [(2 * NLAT, 2 * NLAT + NCTX, 2), (2 * NLAT + NCTX, 2 * NLAT + 2 * NCTX, 2)]
NTOK = 2 * NLAT + 2 * NCTX


def build_program(voff, nv):
    pg = Prog(SEGS_LAT, SEGS_CTX, voff, nv)
    kb = pg.kb
    pg.prologue()
    pg.stage_mod(range(DEPTH))
    xT = pg.inp("xT", [128, 8, NTOK])
    hX = kb.dram("hX", [128, 8, NTOK], F32)
    hM = kb.dram("hM", [128, 8, NTOK], F32)
    fT = kb.dram("fT", [128, 8, NTOK], BF16)
    qTd = kb.dram("p_qT", [128, 16, NTOK], BF16)
    Wd = kb.dram("p_W", [NTOK // 128, 128, 128, 128], BF16)
    yT = kb.dram("yT", [128, 8, 2 * NLAT], F32, kind="ExternalOutput")
    tabs = {0: pg.peer_convert(0)}
    for l in range(DEPTH):
        src = xT if l == 0 else hX
        kind = l % 3
        if kind == 0:
            segs = SEGS_LAT + (SEGS_CTX if l == 0 else [])
            pg.stage_sconv(l, src, hM, fT, segs)
        elif kind == 1:
            pg.stage_attn(l, src, hM, fT, SEGS_LAT, SEGS_CTX, 0.8 - 0.6 * math.exp(-0.3 * l))
        else:
            pg.stage_conformer(l, src, hM, fT, SEGS_LAT)
        if l + 1 < DEPTH:
            tabs[l + 1] = pg.peer_convert(l + 1)
        psegs = SEGS_LAT + (SEGS_CTX if l == 0 else [])
        t_hi = psegs[-1][1]
        pg.stage_peer_q(l, fT, qTd, 0, t_hi)
        pg.stage_peer_route(l, qTd, Wd, 0, t_hi)
        ubf, vbf = tabs[l]
        pg.stage_peer_experts(l, ubf, vbf, fT, hM, yT if l == DEPTH - 1 else hX, Wd, psegs)
    kb.finish([yT])
    return pg


def kernel(**inputs):
    inp = {k: np.asarray(v) for k, v in inputs.items()}
    ncore = 8
    consts = host_consts()
    shared = dict(consts)
    for l in range(DEPTH):
        shared["w_mod%d" % l] = kmaj(inp["w_mod"][l])
        shared["peer_uT%d" % l] = kmaj(inp["peer_u"][l].T)
        shared["peer_v%d" % l] = np.ascontiguousarray(inp["peer_v"][l].reshape(128, 128, D).transpose(1, 0, 2))
        shared["peer_wq%d" % l] = kmaj(inp["peer_w_query"][l])
        shared["peer_kT%d" % l] = np.ascontiguousarray(inp["peer_sub_keys"][l].reshape(16, 128, 128).transpose(2, 0, 1))
    for j in range(inp["sc_w_in"].shape[0]):
        shared["sc_w_in%d" % j] = kmaj(inp["sc_w_in"][j])
        shared["sc_w_out%d" % j] = kmaj(inp["sc_w_out"][j])
    shared["da_w_qkv"] = kmaj(inp["da_w_qkv"][0])
    shared["da_w_o"] = kmaj(inp["da_w_o"][0])
    shared["cf_w_pw1"] = kmaj(inp["cf_w_pw1"][0])
    shared["cf_w_pw2"] = kmaj(inp["cf_w_pw2"][0])
    in_maps = []
    pg = None
    for c in range(ncore):
        b0, b1 = 2 * c, 2 * c + 1
        crows = np.stack([inp["c"][b0], inp["c"][b1], inp["c_ctx"]], axis=0)
        vp = build_vecs(inp, crows)
        if pg is None:
            pg = build_program(vp.off, vp.n)
        X = np.concatenate([inp["x"][b0], inp["x"][b1], inp["ctx"][b0], inp["ctx"][b1]], axis=0)
        m = dict(shared)
        m["vecs"] = vp.array()
        m["xT"] = tok_to_T(X)
        in_maps.append({k: m[k] for k in pg.din})
    res = run_bass_kernel_spmd(pg.kb.nc, in_maps, core_ids=list(range(ncore)))
    out = np.empty((2 * ncore, NLAT, D), np.float32)
    for c in range(ncore):
        Y = T_to_tok(np.asarray(res.results[c]["yT"]))
        out[2 * c] = Y[:NLAT]
        out[2 * c + 1] = Y[NLAT:]
    return out
```
